# Optimizing a Trainium2 kernel written in Bass

```python
import math
import jax, jax.numpy as jnp
from jax import lax
import numpy as np

D_MODEL = 1024
BATCH = 8
SEQ = 2048
DEPTH = 1
DEC_BATCH = 128
DEC_SEQ = 1
PAST_LEN = 16384
PAGE_SIZE = 128

MIX = D_MODEL
POOL_WIDTH = MIX // 2
POOL_WINDOWS = (2, 4, 8, 16)
POOL_GROUPS = len(POOL_WINDOWS)
POOL_GROUP_W = POOL_WIDTH // POOL_GROUPS
POOL_HIST = max(POOL_WINDOWS) - 1
CONV_CH = MIX - POOL_WIDTH
CONV_K = 31
CONV_HIST = CONV_K - 1
IN_COLS = POOL_WIDTH + 2 * CONV_CH
N_MEM = 256
MEM_HEADS = 4
MEM_HEAD_DIM = D_MODEL // MEM_HEADS
D_FF = -(-8 * D_MODEL // (3 * 256)) * 256
EPS = 1e-6

kernel_name = "hymba_pool_conformer_memxattn_step"


def rmsnorm(x, g):
    xf = x.astype(jnp.float32)
    y = xf * lax.rsqrt(jnp.mean(xf * xf, axis=-1, keepdims=True) + EPS)
    return (y * g.astype(jnp.float32)).astype(x.dtype)


def pool_mixer(a_ext, start_pos, w_map, b_map, scale):
    B, T, C = a_ext.shape
    L = T - POOL_HIST
    af = a_ext.astype(jnp.float32)
    csum = jnp.concatenate([jnp.zeros((B, 1, C), jnp.float32), jnp.cumsum(af, axis=1)], axis=1)
    pos = start_pos + jnp.arange(L)
    outs = []
    for g, w in enumerate(POOL_WINDOWS):
        sl = slice(g * POOL_GROUP_W, (g + 1) * POOL_GROUP_W)
        cg = csum[..., sl]
        win_sum = cg[:, POOL_HIST + 1:] - cg[:, POOL_HIST + 1 - w: POOL_HIST + 1 - w + L]
        cnt = jnp.minimum(w, pos + 1).astype(jnp.float32)[None, :, None]
        outs.append(win_sum / cnt - af[:, POOL_HIST:, sl])
    d = jnp.concatenate(outs, axis=-1).reshape(B, L, POOL_GROUPS, POOL_GROUP_W)
    y = jnp.einsum('blgc,gcd->blgd', d, w_map.astype(jnp.float32)).reshape(B, L, C)
    y = (y + b_map.astype(jnp.float32)) * scale.astype(jnp.float32)
    return y.astype(a_ext.dtype)


def conv_mixer(g_ext, w_dw, b_dw, ln_g, ln_b):
    y = lax.conv_general_dilated(g_ext, w_dw[:, None, :], window_strides=(1,), padding='VALID',
                                 dimension_numbers=('NWC', 'WIO', 'NWC'), feature_group_count=CONV_CH)
    yf = (y + b_dw).astype(jnp.float32)
    mu = jnp.mean(yf, axis=-1, keepdims=True)
    var = jnp.mean(jnp.square(yf - mu), axis=-1, keepdims=True)
    yn = (yf - mu) * lax.rsqrt(var + EPS) * ln_g.astype(jnp.float32) + ln_b.astype(jnp.float32)
    return jax.nn.silu(yn).astype(g_ext.dtype)


def mem_kv(mem, g_mem, w_k, w_v):
    B = mem.shape[0]
    m = rmsnorm(mem, g_mem)
    k = (m @ w_k).reshape(B, N_MEM, MEM_HEADS, MEM_HEAD_DIM)
    v = (m @ w_v).reshape(B, N_MEM, MEM_HEADS, MEM_HEAD_DIM)
    return k, v


def cross_attn(h, k, v, w_q, w_o):
    B, L, _ = h.shape
    q = (h @ w_q).reshape(B, L, MEM_HEADS, MEM_HEAD_DIM)
    s = jnp.einsum('blhd,bmhd->bhlm', q, k).astype(jnp.float32) / math.sqrt(MEM_HEAD_DIM)
    p = jax.nn.softmax(s, axis=-1).astype(v.dtype)
    o = jnp.einsum('bhlm,bmhd->blhd', p, v).reshape(B, L, D_MODEL)
    return o @ w_o


def layer(x, pool_hist, conv_hist, start_pos, mk, mv, g_mix, w_in, pool_map_w, pool_map_b, pool_scale,
          conv_dw_w, conv_dw_b, conv_ln_g, conv_ln_b, w_out, g_attn, w_q, w_o, g_ffn, w_gate, w_up, w_down):
    h = rmsnorm(x, g_mix)
    u = h @ w_in
    a = u[..., :POOL_WIDTH]
    val = u[..., POOL_WIDTH:POOL_WIDTH + CONV_CH]
    gate = u[..., POOL_WIDTH + CONV_CH:]
    glu = val * jax.nn.sigmoid(gate)
    a_ext = jnp.concatenate([pool_hist, a], axis=1)
    g_ext = jnp.concatenate([conv_hist, glu], axis=1)
    ya = pool_mixer(a_ext, start_pos, pool_map_w, pool_map_b, pool_scale)
    yb = conv_mixer(g_ext, conv_dw_w, conv_dw_b, conv_ln_g, conv_ln_b)
    x = x + jnp.concatenate([ya, yb], axis=-1) @ w_out
    x = x + cross_attn(rmsnorm(x, g_attn), mk, mv, w_q, w_o)
    h = rmsnorm(x, g_ffn)
    x = x + (jax.nn.silu(h @ w_gate) * (h @ w_up)) @ w_down
    return x, a_ext[:, -POOL_HIST:], g_ext[:, -CONV_HIST:]


def setup_inputs(seed: int = 0) -> dict:
    key = jax.random.key(seed)
    ks = jax.random.split(key, 32)
    f32 = jnp.float32

    def nrm(k, shape, scale=1.0):
        return jax.random.normal(k, shape, f32) * scale

    def gain(k, shape):
        return 1.0 + 0.05 * jax.random.normal(k, shape, f32)

    return {
        "x_prompt": nrm(ks[0], (BATCH, SEQ, D_MODEL)),
        "x_sample": nrm(ks[1], (DEC_BATCH, DEC_SEQ, D_MODEL)),
        "mem_prompt": nrm(ks[2], (BATCH, N_MEM, D_MODEL)),
        "state_pool": nrm(ks[3], (DEPTH, DEC_BATCH, POOL_HIST, POOL_WIDTH)),
        "state_conv": nrm(ks[4], (DEPTH, DEC_BATCH, CONV_HIST, CONV_CH), 0.5),
        "cache_mem_k": nrm(ks[5], (DEPTH, DEC_BATCH, N_MEM, MEM_HEADS, MEM_HEAD_DIM)),
        "cache_mem_v": nrm(ks[6], (DEPTH, DEC_BATCH, N_MEM, MEM_HEADS, MEM_HEAD_DIM)),
        "g_mix": gain(ks[7], (DEPTH, D_MODEL)),
        "w_in": nrm(ks[8], (DEPTH, D_MODEL, IN_COLS), D_MODEL ** -0.5),
        "pool_map_w": nrm(ks[9], (DEPTH, POOL_GROUPS, POOL_GROUP_W, POOL_GROUP_W), POOL_GROUP_W ** -0.5),
        "pool_map_b": nrm(ks[10], (DEPTH, POOL_WIDTH), 0.02),
        "pool_scale": gain(ks[11], (DEPTH, POOL_WIDTH)),
        "conv_dw_w": nrm(ks[12], (DEPTH, CONV_K, CONV_CH), CONV_K ** -0.5),
        "conv_dw_b": nrm(ks[13], (DEPTH, CONV_CH), 0.02),
        "conv_ln_g": gain(ks[14], (DEPTH, CONV_CH)),
        "conv_ln_b": nrm(ks[15], (DEPTH, CONV_CH), 0.02),
        "w_out": nrm(ks[16], (DEPTH, MIX, D_MODEL), MIX ** -0.5),
        "g_attn": gain(ks[17], (DEPTH, D_MODEL)),
        "g_mem": gain(ks[18], (DEPTH, D_MODEL)),
        "w_q": nrm(ks[19], (DEPTH, D_MODEL, D_MODEL), D_MODEL ** -0.5),
        "w_k": nrm(ks[20], (DEPTH, D_MODEL, D_MODEL), D_MODEL ** -0.5),
        "w_v": nrm(ks[21], (DEPTH, D_MODEL, D_MODEL), D_MODEL ** -0.5),
        "w_o": nrm(ks[22], (DEPTH, D_MODEL, D_MODEL), D_MODEL ** -0.5),
        "g_ffn": gain(ks[23], (DEPTH, D_MODEL)),
        "w_gate": nrm(ks[24], (DEPTH, D_MODEL, D_FF), D_MODEL ** -0.5),
        "w_up": nrm(ks[25], (DEPTH, D_MODEL, D_FF), D_MODEL ** -0.5),
        "w_down": nrm(ks[26], (DEPTH, D_FF, D_MODEL), D_FF ** -0.5),
        "g_final": gain(ks[27], (D_MODEL,)),
    }


def reference(x_prompt, x_sample, mem_prompt, state_pool, state_conv, cache_mem_k, cache_mem_v,
              g_mix, w_in, pool_map_w, pool_map_b, pool_scale, conv_dw_w, conv_dw_b, conv_ln_g, conv_ln_b,
              w_out, g_attn, g_mem, w_q, w_k, w_v, w_o, g_ffn, w_gate, w_up, w_down, g_final):
    xp, xs = x_prompt, x_sample
    pool_p, pool_s, conv_p, conv_s, mk_p, mv_p = [], [], [], [], [], []
    for l in range(DEPTH):
        shared = (g_mix[l], w_in[l], pool_map_w[l], pool_map_b[l], pool_scale[l], conv_dw_w[l], conv_dw_b[l],
                  conv_ln_g[l], conv_ln_b[l], w_out[l], g_attn[l], w_q[l], w_o[l], g_ffn[l], w_gate[l], w_up[l],
                  w_down[l])
        mk, mv = mem_kv(mem_prompt, g_mem[l], w_k[l], w_v[l])
        zp = jnp.zeros((xp.shape[0], POOL_HIST, POOL_WIDTH), xp.dtype)
        zc = jnp.zeros((xp.shape[0], CONV_HIST, CONV_CH), xp.dtype)
        xp, hp, cp = layer(xp, zp, zc, 0, mk, mv, *shared)
        xs, hs, cs = layer(xs, state_pool[l], state_conv[l], PAST_LEN, cache_mem_k[l], cache_mem_v[l], *shared)
        pool_p.append(hp); pool_s.append(hs); conv_p.append(cp); conv_s.append(cs)
        mk_p.append(mk); mv_p.append(mv)
    y_prompt = rmsnorm(xp, g_final)
    y_sample = rmsnorm(xs, g_final)
    return (y_prompt, y_sample, jnp.stack(pool_p), jnp.stack(pool_s), jnp.stack(conv_p), jnp.stack(conv_s),
            jnp.stack(mk_p), jnp.stack(mv_p))
```

```python
from contextlib import ExitStack
import numpy as np
import concourse.bass as bass
import concourse.mybir as mybir
from concourse.bass_utils import run_bass_kernel_spmd

F32 = mybir.dt.float32
BF16 = mybir.dt.bfloat16
AF = mybir.ActivationFunctionType
ALU = mybir.AluOpType
AX = mybir.AxisListType

D = 1024
SEQ = 2048
NS = 16
NMEM = 256
DFF = 2816
EPS = 1e-6
POOL_W = (2, 4, 8, 16)
CONV_K = 31
NCON = 384
NCORES = 8
TAG = False
REWAIT_DMA = False
STRICT = True

C_GMIX, C_GATT, C_GFFN, C_GMEM = 0, 8, 16, 24
C_PB, C_PS, C_CB, C_LG, C_LB = 32, 36, 40, 44, 48
C_CW = 52
C_IC = 176
C_EPS = 240
C_ID = 256


class Slot:
    __slots__ = ("w", "r", "excl")

    def __init__(self, inherit=None, excl=False):
        self.w = {}
        self.r = dict(inherit) if inherit else {}
        self.excl = excl


def _merge(d, toks):
    for k, v in toks.items():
        if d.get(k, 0) < v:
            d[k] = v


class Eng:
    def __init__(self, name):
        self.name = name
        self.ops = []
        self.cnt = 0
        self.seen = {}


class Prog:
    NDMA = {"sp": 24, "pool": 32}

    def __init__(self):
        self.eng = {n: Eng(n) for n in ("pe", "act", "dve", "pool", "sp")}
        self.dma_tot = {}
        self.dma_rr = {"sp": 0, "pool": 0}
        self.stage = "init"

    def _deps(self, E, reads, writes, extra):
        deps = {}
        for s in reads:
            _merge(deps, s.w)
            if s.excl:
                for k, v in s.r.items():
                    if k != E.name and deps.get(k, 0) < v:
                        deps[k] = v
        strict = STRICT and E.name != "pe"
        for s in writes:
            for k, v in s.w.items():
                if (strict or k != E.name) and deps.get(k, 0) < v:
                    deps[k] = v
            for k, v in s.r.items():
                if (strict or k != E.name) and deps.get(k, 0) < v:
                    deps[k] = v
        for t in extra:
            _merge(deps, t)
        waits = []
        for k, v in deps.items():
            if E.seen.get(k, 0) < v or (REWAIT_DMA and k.startswith("d_")):
                E.seen[k] = max(v, E.seen.get(k, 0))
                waits.append((k, v))
        return waits

    def op(self, eng, fn, reads=(), writes=(), extra=()):
        E = self.eng[eng]
        waits = self._deps(E, reads, writes, extra)
        E.cnt += 1
        tok = {E.name: E.cnt}
        E.ops.append((waits, fn, (E.name, 1), self.stage))
        for s in reads:
            _merge(s.r, tok)
        for s in writes:
            s.w = dict(tok)
            s.r = {}
        return tok

    def dma(self, eng, fn, reads=(), writes=(), extra=()):
        E = self.eng[eng]
        j = self.dma_rr[eng]
        self.dma_rr[eng] = (j + 1) % self.NDMA[eng]
        key = "d_%s_%d" % (eng, j)
        prev = self.dma_tot.get(key, 0)
        ex = list(extra)
        if prev:
            ex.append({key: prev})
        waits = self._deps(E, reads, writes, ex)
        tot = prev + 16
        self.dma_tot[key] = tot
        tok = {key: tot}
        E.ops.append((waits, fn, (key, 16), self.stage))
        for s in reads:
            _merge(s.r, tok)
        for s in writes:
            s.w = dict(tok)
            s.r = {}
        return tok

    def sem_keys(self):
        keys = list(self.eng.keys())
        for eng in ("sp", "pool"):
            keys += ["d_%s_%d" % (eng, j) for j in range(self.NDMA[eng])]
        return keys


class Arena:
    def __init__(self, ap, nbytes):
        self.ap = ap
        self.nbytes = nbytes
        self.live = {}
        self.dead = []
        self.peak = 0

    def alloc(self, name, nbytes):
        assert name not in self.live, name
        req = nbytes
        nbytes = (nbytes + 63) // 64 * 64
        spans = sorted((r["off"], r["size"]) for r in self.live.values())
        pos = 0
        off = None
        for o, sz in spans:
            if o - pos >= nbytes:
                off = pos
                break
            pos = max(pos, o + sz)
        if off is None:
            if self.nbytes - pos >= nbytes:
                off = pos
            else:
                raise RuntimeError("arena full allocating %s (%d B); live=%s" % (
                    name, nbytes, {k: (v["off"], v["size"]) for k, v in self.live.items()}))
        inherit = {}
        for (o, sz, toks) in self.dead:
            if o < off + nbytes and off < o + sz:
                _merge(inherit, toks)
        self.dead = [(o, sz, t) for (o, sz, t) in self.dead if not (o >= off and o + sz <= off + nbytes)]
        self.live[name] = dict(off=off, size=nbytes, req=req, slots=[], inherit=inherit)
        self.peak = max(self.peak, off + nbytes)
        return name

    def view(self, name, dtype, pattern=None, **kw):
        r = self.live[name]
        off, nb = r["off"], r["req"]
        a = self.ap[:, off // 2:(off + nb) // 2]
        if dtype != BF16:
            a = a.bitcast(dtype)
        if pattern:
            a = a.rearrange(pattern, **kw)
        return a

    def slot(self, name):
        r = self.live[name]
        sl = Slot(r["inherit"])
        r["slots"].append(sl)
        return sl

    def free(self, name):
        r = self.live.pop(name)
        toks = dict(r["inherit"])
        for sl in r["slots"]:
            _merge(toks, sl.w)
            _merge(toks, sl.r)
        self.dead.append((r["off"], r["size"], toks))


class _Stop(Exception):
    pass


def build_program(limit=None):
    nc = bass.Bass("TRN2", target_bir_lowering=False)
    P = Prog()

    def ckpt(n, name=None):
        if name is not None:
            P.stage = name
        if limit is not None and n >= limit:
            raise _Stop()

    def din(name, shape):
        return nc.dram_tensor(name, list(shape), F32, kind="ExternalInput").ap()

    def dout(name, shape):
        return nc.dram_tensor(name, list(shape), F32, kind="ExternalOutput").ap()

    x_d = din("x", [SEQ, D])
    xs_d = din("xs", [NS, D])
    mem_d = din("mem", [NMEM, D])
    sp_d = din("sp", [NS, 15, 512])
    sc_d = din("sc", [NS, 30, 512])
    ck_d = din("ck", [NS, NMEM, D])
    cv_d = din("cv", [NS, NMEM, D])
    w_in_d = din("w_in", [D, 1536])
    wmap_d = din("wmap", [4, 128, 128])
    w_out_d = din("w_out", [D, D])
    w_q_d = din("w_q", [D, D])
    w_k_d = din("w_k", [D, D])
    w_v_d = din("w_v", [D, D])
    w_o_d = din("w_o", [D, D])
    w_g_d = din("w_gate", [D, DFF])
    w_u_d = din("w_up", [D, DFF])
    w_d_d = din("w_down", [DFF, D])
    con_d = din("consts", [128, NCON])
    gf_d = din("gfin", [128, D])
    sel_d = din("sel", [NS, NS * 128])

    y_d = dout("y", [SEQ, D])
    ys_d = dout("ys", [NS, D])
    pp_d = dout("pool_p", [15, 512])
    ps_d = dout("pool_s", [NS, 15, 512])
    cp_d = dout("conv_p", [30, 512])
    cs_d = dout("conv_s", [NS, 30, 512])
    ko_d = dout("k_out", [NMEM, D])
    vo_d = dout("v_out", [NMEM, D])

    ARENA_BYTES = 212736
    es = ExitStack()
    with es:
        arena_t = es.enter_context(nc.sbuf_tensor("arena", [128, ARENA_BYTES // 2], BF16))
        ps_t = es.enter_context(nc.psum_tensor("ps", [128, 8, 512], F32))
        sems = {k: es.enter_context(nc.semaphore(k)) for k in P.sem_keys()}
        AR = Arena(arena_t[:], ARENA_BYTES)

        bank_slots = [Slot(excl=True) for _ in range(8)]
        bank_rr = [0]
        bank_reserved = set()

        def bank():
            while True:
                i = bank_rr[0]
                bank_rr[0] = (i + 1) % 8
                if i not in bank_reserved:
                    return i, bank_slots[i]

        def psf(i):
            return ps_t[:, i, :]

        def psb(i):
            return ps_t[:, i, :].bitcast(BF16)

        def mm_group(out_ap, pairs, reads, bslot):
            def fn(e):
                n = len(pairs)
                ins = None
                for i, (l, r) in enumerate(pairs):
                    ins = e.matmul(out=out_ap, lhsT=l, rhs=r, start=(i == 0), stop=(i == n - 1))
                return ins
            return P.op("pe", fn, reads=reads, writes=[bslot])

        def act(fn, reads, writes, extra=()):
            return P.op("act", fn, reads=reads, writes=writes, extra=extra)

        def dve(fn, reads, writes, extra=()):
            return P.op("dve", fn, reads=reads, writes=writes, extra=extra)

        AR.alloc("con", NCON * 4)
        AR.alloc("gf", D * 4)
        AR.alloc("idb", 128 * 2)
        AR.alloc("ones", 128 * 2)
        AR.alloc("wmap", 4 * 128 * 2)
        AR.alloc("X", 8 * D * 4)
        AR.alloc("XS", D * 4)
        AR.alloc("A", 8 * 1040 * 2)
        AR.alloc("B", 8 * 1040 * 2)
        AR.alloc("KT", 8 * 256 * 2)
        AR.alloc("V", 2 * D * 2)
        AR.alloc("gth", 4 * 32 * 2)
        AR.alloc("ath", 4 * 16 * 4)
        AR.alloc("st", 64 * 4)

        con = AR.view("con", F32)
        gfv = AR.view("gf", F32)
        idb = AR.view("idb", BF16)
        idf = con[:, C_ID:C_ID + 128]
        ones = AR.view("ones", BF16)
        wmapv = AR.view("wmap", BF16, "p (g d) -> p g d", g=4)
        Xv = AR.view("X", F32, "p (c d) -> p c d", c=8)
        XSv = AR.view("XS", F32)
        Av = AR.view("A", BF16, "p (f t) -> p f t", f=8)
        Bv = AR.view("B", BF16, "p (f t) -> p f t", f=8)
        KTv = AR.view("KT", BF16, "p (f m) -> p f m", f=8)
        Vv = AR.view("V", BF16, "p (c d) -> p c d", c=2)
        gthv = AR.view("gth", BF16, "p (c t) -> p c t", c=4)
        athv = AR.view("ath", F32, "p (c t) -> p c t", c=4)
        stv = AR.view("st", F32)

        s_con = AR.slot("con")
        s_gf = AR.slot("gf")
        s_idb = AR.slot("idb")
        s_ones = AR.slot("ones")
        s_wmap = AR.slot("wmap")
        s_X = [AR.slot("X") for _ in range(8)] + [AR.slot("XS")]
        s_A = [[AR.slot("A") for _ in range(9)] for _ in range(8)]
        s_B = [[AR.slot("B") for _ in range(9)] for _ in range(8)]
        s_KT = [AR.slot("KT") for _ in range(8)]
        s_V = [AR.slot("V") for _ in range(2)]
        s_gth = AR.slot("gth")
        s_ath = AR.slot("ath")

        def urows(u):
            return 128 if u < 8 else NS

        def ucols(u):
            return (128 * u, urows(u))

        def xu(u):
            return Xv[:, u, :] if u < 8 else XSv[0:NS, :]

        prefetch = {}
        P.dma("sp", lambda e: e.dma_start(out=con, in_=con_d), writes=[s_con])
        P.dma("sp", lambda e: e.dma_start(out=gfv, in_=gf_d), writes=[s_gf])
        P.dma("pool", lambda e: e.dma_start(out=wmapv, in_=wmap_d.rearrange("g c d -> c g d")), writes=[s_wmap])
        dve(lambda e: e.tensor_copy(out=idb, in_=idf), [s_con], [s_idb])
        P.op("pool", lambda e: e.memset(ones, 1.0), writes=[s_ones])
        P.op("pool", lambda e: e.memset(gthv, 0.0), writes=[s_gth])
        P.op("pool", lambda e: e.memset(athv, 0.0), writes=[s_ath])

        epsc = con[:, C_EPS:C_EPS + 1]

        def load_w_in():
            AR.alloc("w_in", 8 * 1536 * 2)
            v = AR.view("w_in", BF16, "p (k n) -> p k n", k=8)
            sl = [AR.slot("w_in") for _ in range(3)]
            src = w_in_d.rearrange("(k p) n -> p k n", p=128)
            for q in range(3):
                P.dma("pool", lambda e, q=q: e.dma_start(out=v[:, :, q * 512:(q + 1) * 512],
                                                         in_=src[:, :, q * 512:(q + 1) * 512]), writes=[sl[q]])
            return v, sl

        def load_w(name, dram_ap, kchunks, ncols):
            AR.alloc(name, kchunks * ncols * 2)
            v = AR.view(name, BF16, "p (k n) -> p k n", k=kchunks)
            s = AR.slot(name)
            P.dma("pool", lambda e: e.dma_start(out=v, in_=dram_ap.rearrange("(k p) n -> p k n", p=128)),
                  writes=[s])
            return v, s

        hs_ring = {"i": 0, "h": 0}
        s_stat = [AR.slot("st") for _ in range(8)]
        AR.alloc("sqj", D * 2)
        s_sqj = AR.slot("sqj")
        sqjv = AR.view("sqj", BF16)
        AR.alloc("ots", 8 * NS * 2)
        otsv = AR.view("ots", BF16, "p (f b) -> p f b", f=8)
        s_ots = AR.slot("ots")

        def stats_rstd(xin, s_in, rows):
            j_ = hs_ring["i"] % 8
            hs_ring["i"] += 1
            i = j_
            st = stv[0:rows, 4 * j_:4 * j_ + 4]
            s_st = s_stat[j_]
            act(lambda e: e.activation(out=sqjv[0:rows, :], in_=xin, func=AF.Square, accum_out=st[:, 0:1]),
                [s_in], [s_sqj, s_st])
            act(lambda e: e.activation(out=st[:, 1:2], in_=st[:, 0:1], func=AF.Ln, scale=1.0 / D,
                                       bias=epsc[0:rows]), [s_st, s_con], [s_st])
            act(lambda e: e.activation(out=st[:, 2:3], in_=st[:, 1:2], func=AF.Exp, scale=-0.5), [s_st], [s_st])
            return i, st, s_st

        def norm_A1(xin, s_in, rows):
            return (xin, s_in, rows) + stats_rstd(xin, s_in, rows)

        def norm_A2(a1, hsv, s_hs):
            xin, s_in, rows, _, st, s_st = a1
            i = hs_ring["h"] % 4
            hs_ring["h"] += 1
            hs = hsv[0:rows, i, :]
            act(lambda e: e.activation(out=hs, in_=xin, func=AF.Copy, scale=st[:, 2:3]),
                [s_in, s_st], [s_hs[i]])
            return hs, s_hs[i]

        def norm_A(xin, s_in, rows, hsv, s_hs):
            return norm_A2(norm_A1(xin, s_in, rows), hsv, s_hs)

        def norm_B(hs, s_h, rows, gcol, dst3, dst_slots):
            bi, bs = bank()
            pb = psb(bi).rearrange("p (f t) -> p f t", f=8)

            def tfn(e):
                ins = None
                for f in range(8):
                    ins = e.transpose(out=pb[:, f, 0:rows], in_=hs[:, f * 128:(f + 1) * 128],
                                      identity=idb[0:rows, 0:rows])
                return ins
            P.op("pe", tfn, reads=[s_h, s_idb], writes=[bs])
            g = con[:, gcol:gcol + 8]
            dve(lambda e: e.tensor_tensor(out=dst3, in0=pb[:, :, 0:rows],
                                          in1=g.unsqueeze(2).to_broadcast([128, 8, rows]), op=ALU.mult),
                [bs, s_con], dst_slots)

        def norm_unit(u, gcol, dstv, s_dst, hsv, s_hs):
            rows = urows(u)
            c0, n = ucols(u)
            hs, s_h = norm_A(xu(u), s_X[u], rows, hsv, s_hs)
            norm_B(hs, s_h, rows, gcol, dstv[:, :, c0:c0 + n], [s_dst[f][u] for f in range(8)])

        def proj_back(u, lhs, rd, wv, s_w):
            rows = urows(u)
            for h in range(2):
                bi, bs = bank()
                pairs = [(l, wv[:, k, h * 512:(h + 1) * 512]) for k, l in enumerate(lhs)]
                mm_group(psf(bi)[0:rows, :], pairs, rd + [s_w], bs)
                xs_ = xu(u)[:, h * 512:(h + 1) * 512]
                pv = psf(bi)[0:rows, :]
                dve(lambda e, xs_=xs_, pv=pv: e.tensor_tensor(out=xs_, in0=xs_, in1=pv, op=ALU.add),
                    [bs, s_X[u]], [s_X[u]])

        def lhs_cols(srcv, s_src, nf, u):
            c0, n = ucols(u)
            return [srcv[:, f, c0:c0 + n] for f in range(nf)], [s_src[f][u] for f in range(nf)]

        class ProjNorm:
            def __init__(self, gcol, dstv, s_dst, hsv, s_hs, lag=2):
                self.pa = []
                self.pb = []
                self.a = (gcol, dstv, s_dst, hsv, s_hs)
                self.lag = lag

            def unit(self, u, lhs, rd, wv, s_w):
                gcol, dstv, s_dst, hsv, s_hs = self.a
                proj_back(u, lhs, rd, wv, s_w)
                self.pa.append((u, norm_A1(xu(u), s_X[u], urows(u))))
                if len(self.pa) > 1:
                    self._a2()
                if len(self.pb) > self.lag:
                    self._b()

            def _a2(self):
                gcol, dstv, s_dst, hsv, s_hs = self.a
                u, a1 = self.pa.pop(0)
                hs, s_h = norm_A2(a1, hsv, s_hs)
                self.pb.append((u, hs, s_h))

            def _b(self):
                gcol, dstv, s_dst, hsv, s_hs = self.a
                u, hs, s_h = self.pb.pop(0)
                c0, n = ucols(u)
                norm_B(hs, s_h, urows(u), gcol, dstv[:, :, c0:c0 + n], [s_dst[f][u] for f in range(8)])

            def flush(self):
                while self.pa:
                    self._a2()
                while self.pb:
                    self._b()

        def tcols(units):
            c0 = 128 * units[0]
            n = sum(urows(u) for u in units)
            return c0, n

        out_toks = []

        def load_x_unit(ps_i, u):
            r0 = 1024 * ps_i + 128 * u
            P.dma("sp", lambda e: e.dma_start(out=Xv[:, u, :], in_=x_d[r0:r0 + 128, :]), writes=[s_X[u]])

        def final_F1(u):
            return stats_rstd(xu(u), s_X[u], urows(u))

        def final_F2(u, f1, tok0, yov, s_yo):
            rows = urows(u)
            xin = xu(u)
            i, st, s_st = f1
            yo = yov[0:rows, :]
            dve(lambda e: e.scalar_tensor_tensor(out=yo, in0=xin, scalar=st[:, 2:3], in1=gfv[0:rows, :],
                                                 op0=ALU.mult, op1=ALU.mult), [s_X[u], s_st, s_gf], [s_yo])
            if u < 8:
                r0 = tok0 + 128 * u
                out_toks.append(P.dma("sp", lambda e: e.dma_start(out=y_d[r0:r0 + 128, :], in_=yo), reads=[s_yo]))
            else:
                out_toks.append(P.dma("sp", lambda e: e.dma_start(out=ys_d, in_=yo), reads=[s_yo]))

        T0, T1, TS = (0, [0, 1, 2, 3]), (1, [4, 5, 6, 7]), (2, [8])

        def body():
          P.stage = "load"
          prefetch["w_in"] = load_w_in()
          for u in range(8):
              load_x_unit(0, u)
          P.dma("sp", lambda e: e.dma_start(out=XSv[0:NS, :], in_=xs_d), writes=[s_X[8]])

          def one_pass(ps_i):
              tok0 = 1024 * ps_i
              has_s = ps_i == 0
              last_pass = ps_i == 1
              tiles = [T0, T1] + ([TS] if has_s else [])
              units = [u for _, us in tiles for u in us]

              ckpt(20 * ps_i + 1, "p%d_Ia" % ps_i)
              if has_s:
                  AR.alloc("hst", 4 * 512 * 4)
                  hstv = AR.view("hst", F32, "p (r c) -> p r c", r=4)
                  s_hst = [AR.slot("hst") for _ in range(4)]
                  spf = sp_d.rearrange("b t c -> (b t) c")
                  for r in range(2):
                      P.dma("sp", lambda e, r=r: e.dma_start(out=hstv[0:120, r, :], in_=spf[120 * r:120 * r + 120, :]),
                            writes=[s_hst[r]])
                  AR.alloc("hsc", 4 * 512 * 4)
                  hscv = AR.view("hsc", F32, "p (r c) -> p r c", r=4)
                  s_hsc = [AR.slot("hsc") for _ in range(4)]
                  scf = sc_d.rearrange("b t c -> (b t) c")
                  for r in range(4):
                      P.dma("sp", lambda e, r=r: e.dma_start(out=hscv[0:120, r, :], in_=scf[120 * r:120 * r + 120, :]),
                            writes=[s_hsc[r]])
              if "w_in" in prefetch:
                  w_in_v, s_win3 = prefetch.pop("w_in")
              else:
                  w_in_v, s_win3 = load_w_in()
              AR.alloc("diag", 4 * CONV_K * 128 * 2)
              diagv = AR.view("diag", BF16, "p (c k j) -> p c k j", c=4, k=CONV_K)
              s_diagc = [AR.slot("diag") for _ in range(4)]

              def build_diag(c):
                  dve(lambda e: e.tensor_tensor(
                      out=diagv[:, c, :, :],
                      in0=idf.unsqueeze(1).to_broadcast([128, CONV_K, 128]),
                      in1=con[:, C_CW + c * CONV_K:C_CW + (c + 1) * CONV_K].unsqueeze(2).to_broadcast([128, CONV_K, 128]),
                      op=ALU.mult), [s_con], [s_diagc[c]])
              if "hs" in prefetch:
                  hsv, s_hs = prefetch.pop("hs")
              else:
                  AR.alloc("hs", 4 * D * 2)
                  hsv = AR.view("hs", BF16, "p (i d) -> p i d", i=4)
                  s_hs = [AR.slot("hs") for _ in range(4)]
              AR.alloc("gt", 4 * 1056 * 2)
              gtv = AR.view("gt", BF16, "p (c t) -> p c t", c=4)
              s_gt = [[AR.slot("gt") for _ in range(3)] for _ in range(4)]

              pa, pb = [], []
              pre_done = prefetch.pop("norm1", None)
              if pre_done is not None:
                  pre_done()
                  for c_ in range(4):
                      build_diag(c_)
              for n_, u in enumerate([] if pre_done is not None else units):
                  pa.append((u, norm_A1(xu(u), s_X[u], urows(u))))
                  if len(pa) > 1:
                      uu, a1 = pa.pop(0)
                      pb.append((uu,) + norm_A2(a1, hsv, s_hs))
                  if len(pb) > 1:
                      uu, hs_, sh_ = pb.pop(0)
                      c0_, nn_ = ucols(uu)
                      norm_B(hs_, sh_, urows(uu), C_GMIX, Av[:, :, c0_:c0_ + nn_], [s_A[f][uu] for f in range(8)])
                  if 2 <= n_ < 6:
                      build_diag(n_ - 2)
              while pa:
                  uu, a1 = pa.pop(0)
                  pb.append((uu,) + norm_A2(a1, hsv, s_hs))
              while pb:
                  uu, hs_, sh_ = pb.pop(0)
                  c0_, nn_ = ucols(uu)
                  norm_B(hs_, sh_, urows(uu), C_GMIX, Av[:, :, c0_:c0_ + nn_], [s_A[f][uu] for f in range(8)])

              dve(lambda e: e.tensor_copy(out=gtv[:, :, 0:30], in_=gthv[:, :, 0:30]), [s_gth],
                  [s_gt[c][0] for c in range(4)])

              AR.alloc("at", 2 * 4 * 528 * 4)
              atv = AR.view("at", F32, "p (i c t) -> p i c t", i=2, c=4)
              s_at = [[AR.slot("at") for _ in range(4)] for _ in range(2)]
              AR.alloc("pt", 2 * 528 * 4)
              ptv = AR.view("pt", F32, "p (i t) -> p i t", i=2)
              s_pt = [AR.slot("pt") for _ in range(2)]
              AR.alloc("sg", 2 * 512 * 4)
              sgv = AR.view("sg", F32, "p (i t) -> p i t", i=2)
              s_sg = [AR.slot("sg") for _ in range(2)]
              if has_s:
                  AR.alloc("as", 4 * NS * 4)
                  asv = AR.view("as", F32, "p (c b) -> p c b", c=4)
                  s_as = [AR.slot("as") for _ in range(4)]
                  AR.alloc("gs", 4 * NS * 4)
                  gsv = AR.view("gs", F32, "p (c b) -> p c b", c=4)
                  s_gs = [AR.slot("gs") for _ in range(4)]
              if last_pass:
                  AR.alloc("gl", 4 * 32 * 4)
                  glv = AR.view("gl", F32, "p (c t) -> p c t", c=4)
                  s_gl = [AR.slot("gl") for _ in range(4)]

              sgi = [0]
              deferred_pool = []
              for t_i, us in tiles:
                  c0, n = tcols(us)
                  is_s = us[0] == 8
                  rd_A = lambda: [s_A[f][u] for f in range(8) for u in us]
                  ai = t_i % 2
                  for j in range(4):
                      bi, bs = bank()
                      pairs = [(w_in_v[:, k, j * 128:(j + 1) * 128], Av[:, k, c0:c0 + n]) for k in range(8)]
                      mm_group(psf(bi)[:, 0:n], pairs, rd_A() + [s_win3[0]], bs)
                      if is_s:
                          act(lambda e, bi=bi, j=j: e.activation(out=asv[:, j, :], in_=psf(bi)[:, 0:NS], func=AF.Copy),
                              [bs], [s_as[j]])
                      else:
                          act(lambda e, bi=bi, j=j, ai=ai: e.activation(out=atv[:, ai, j, 15:15 + 512],
                                                                         in_=psf(bi)[:, 0:512], func=AF.Copy),
                              [bs], [s_at[ai][j]])
                  for j in range(4):
                      bv, bsv = bank()
                      pairs = [(w_in_v[:, k, 512 + j * 128:512 + (j + 1) * 128], Av[:, k, c0:c0 + n]) for k in range(8)]
                      mm_group(psf(bv)[:, 0:n], pairs, rd_A() + [s_win3[1]], bsv)
                      bg, bsg = bank()
                      pairs = [(w_in_v[:, k, 1024 + j * 128:1024 + (j + 1) * 128], Av[:, k, c0:c0 + n]) for k in range(8)]
                      mm_group(psf(bg)[:, 0:n], pairs, rd_A() + [s_win3[2]], bsg)
                      si = sgi[0] % 2
                      sgi[0] += 1
                      act(lambda e, bg=bg, si=si, n=n: e.activation(out=sgv[:, si, 0:n], in_=psf(bg)[:, 0:n],
                                                                   func=AF.Sigmoid), [bsg], [s_sg[si]])
                      if is_s:
                          dve(lambda e, bv=bv, si=si, j=j: e.tensor_tensor(out=gsv[:, j, :], in0=psf(bv)[:, 0:NS],
                                                                           in1=sgv[:, si, 0:NS], op=ALU.mult),
                              [bsv, s_sg[si]], [s_gs[j]])
                      else:
                          g0 = 30 + c0
                          if last_pass and t_i == 1:
                              dve(lambda e, bv=bv, si=si, j=j: e.tensor_tensor(
                                  out=glv[:, j, :], in0=psf(bv)[:, 480:512], in1=sgv[:, si, 480:512], op=ALU.mult),
                                  [bsv, s_sg[si]], [s_gl[j]])
                          dve(lambda e, bv=bv, si=si, j=j, g0=g0: e.tensor_tensor(
                              out=gtv[:, j, g0:g0 + 512], in0=psf(bv)[:, 0:512], in1=sgv[:, si, 0:512], op=ALU.mult),
                              [bsv, s_sg[si]], [s_gt[j][1 + t_i]])
                  if is_s:
                      for fn_ in deferred_pool:
                          fn_()
                      del deferred_pool[:]

                  def pooling(t_i=t_i, us=us, c0=c0, ai=ai):
                      first_global = (ps_i == 0 and t_i == 0)
                      if t_i == 0:
                          dve(lambda e, ai=ai: e.tensor_copy(out=atv[:, ai, :, 0:15], in_=athv[:, :, 0:15]),
                              [s_ath], [s_at[ai][c] for c in range(4)])
                      else:
                          dve(lambda e, ai=ai: e.tensor_copy(out=atv[:, ai, :, 0:15], in_=atv[:, 1 - ai, :, 512:527]),
                              [s_at[1 - ai][c] for c in range(4)], [s_at[ai][c] for c in range(4)])
                      if t_i == 1:
                          dve(lambda e, ai=ai: e.tensor_copy(out=athv[:, :, 0:15], in_=atv[:, ai, :, 512:527]),
                              [s_at[ai][c] for c in range(4)], [s_ath])
                          dve(lambda e: e.tensor_copy(out=gthv[:, :, 0:30], in_=gtv[:, :, 1024:1054]),
                              [s_gt[c][2] for c in range(4)], [s_gth])
                      for g in range(4):
                          w = POOL_W[g]
                          a = atv[:, ai, g, :]
                          src, ssrc = a, s_at[ai][g]
                          sh = 1
                          pi = 0
                          while sh < w:
                              dst = ptv[:, pi, :]
                              lo = 2 * sh - 1
                              dve(lambda e, dst=dst, src=src, lo=lo, sh=sh: e.tensor_tensor(
                                  out=dst[:, lo:527], in0=src[:, lo:527], in1=src[:, lo - sh:527 - sh], op=ALU.add),
                                  [ssrc], [s_pt[pi]])
                              src, ssrc = dst, s_pt[pi]
                              pi = 1 - pi
                              sh *= 2
                          dve(lambda e, src=src, a=a, w=w, g=g, c0=c0: e.scalar_tensor_tensor(
                              out=Bv[:, 4 + g, c0:c0 + 512], in0=src[:, 15:527], scalar=1.0 / w, in1=a[:, 15:527],
                              op0=ALU.mult, op1=ALU.subtract), [ssrc, s_at[ai][g]], [s_B[4 + g][u] for u in us])
                          if first_global:
                              ic = con[:, C_IC + 16 * g:C_IC + 16 * g + 16]
                              tmp = stv[:, 32:48]
                              dve(lambda e, src=src, ic=ic, tmp=tmp: e.tensor_tensor(out=tmp, in0=src[:, 15:31], in1=ic,
                                                                                    op=ALU.mult),
                                  [ssrc, s_con], [s_stat[3]])
                              dve(lambda e, tmp=tmp, a=a, g=g: e.tensor_tensor(out=Bv[:, 4 + g, 0:16], in0=tmp,
                                                                              in1=a[:, 15:31], op=ALU.subtract),
                                  [s_stat[3], s_at[ai][g]], [s_B[4 + g][0]])
                      if last_pass and t_i == 1:
                          bi, bs = bank()
                          for g in range(4):
                              P.op("pe", lambda e, bi=bi, g=g, ai=ai: e.transpose(
                                  out=psf(bi)[0:15, g * 128:(g + 1) * 128], in_=atv[:, ai, g, 512:527], identity=idf),
                                  reads=[s_at[ai][g], s_con], writes=[bs] if g == 0 else [])
                          AR.alloc("ppo", 512 * 4)
                          ppov = AR.view("ppo", F32)
                          s_ppo = AR.slot("ppo")
                          bs.w = {"pe": P.eng["pe"].cnt}
                          act(lambda e, bi=bi: e.activation(out=ppov[0:15, :], in_=psf(bi)[0:15, :], func=AF.Copy),
                              [bs], [s_ppo])
                          out_toks.append(P.dma("sp", lambda e: e.dma_start(out=pp_d, in_=ppov[0:15, :]), reads=[s_ppo]))
                          AR.free("ppo")

                  if not is_s:
                      if has_s and t_i == 1:
                          deferred_pool.append(pooling)
                      else:
                          pooling()
              for fn_ in deferred_pool:
                  fn_()

              AR.free("sg")
              AR.free("pt")
              AR.free("hs")
              AR.free("w_in")

              ckpt(20 * ps_i + 2, "p%d_samp_pool" % ps_i)
              if has_s:
                  AR.alloc("es", 4 * NS * 16 * 4)
                  esv = AR.view("es", F32, "p (c b t) -> p c b t", c=4, b=NS)
                  s_es = [AR.slot("es") for _ in range(4)]
                  for c in range(4):
                      for r in range(2):
                          bi, bs = bank()
                          P.op("pe", lambda e, bi=bi, c=c, r=r: e.transpose(
                              out=psf(bi)[:, 0:120], in_=hstv[0:120, r, c * 128:(c + 1) * 128], identity=idf[0:120, 0:120]),
                              reads=[s_hst[r], s_con], writes=[bs])
                          act(lambda e, bi=bi, c=c, r=r: e.activation(
                              out=esv[:, c, 8 * r:8 * r + 8, 0:15],
                              in_=psf(bi)[:, 0:120].rearrange("p (b t) -> p b t", b=8), func=AF.Copy), [bs], [s_es[c]])
                      dve(lambda e, c=c: e.tensor_copy(out=esv[:, c, :, 15:16], in_=asv[:, c, :].unsqueeze(2)),
                          [s_as[c]], [s_es[c]])
                      w = POOL_W[c]
                      ws = stv[:, 48:64]
                      dve(lambda e, c=c, w=w, ws=ws: e.tensor_reduce(out=ws, in_=esv[:, c, :, 16 - w:16], axis=AX.X,
                                                                    op=ALU.add), [s_es[c]], [s_stat[3]])
                      dve(lambda e, c=c, w=w, ws=ws: e.scalar_tensor_tensor(
                          out=Bv[:, 4 + c, 1024:1040], in0=ws, scalar=1.0 / w, in1=asv[:, c, :], op0=ALU.mult,
                          op1=ALU.subtract), [s_stat[3], s_as[c]], [s_B[4 + c][8]])
                  out_toks.append(P.dma("sp", lambda e: e.dma_start(out=ps_d[:, 0:14, :], in_=sp_d[:, 1:15, :])))
                  bi, bs = bank()
                  for c in range(4):
                      P.op("pe", lambda e, bi=bi, c=c: e.transpose(out=psf(bi)[0:NS, c * 128:(c + 1) * 128],
                                                                  in_=asv[:, c, :], identity=idf),
                           reads=[s_as[c], s_con], writes=[bs] if c == 0 else [])
                  bs.w = {"pe": P.eng["pe"].cnt}
                  AR.alloc("aso", 512 * 4)
                  asov = AR.view("aso", F32)
                  s_aso = AR.slot("aso")
                  act(lambda e, bi=bi: e.activation(out=asov[0:NS, :], in_=psf(bi)[0:NS, :], func=AF.Copy), [bs], [s_aso])
                  out_toks.append(P.dma("sp", lambda e: e.dma_start(
                      out=ps_d[:, 14:15, :].rearrange("b o c -> b (o c)"), in_=asov[0:NS, :]), reads=[s_aso]))
                  AR.free("aso")
                  AR.free("es")
                  AR.free("hst")
              AR.free("at")

              ckpt(20 * ps_i + 3, "p%d_Ib_poolmap" % ps_i)
              w_out_v, s_wout = load_w("w_out", w_out_d, 8, D)

              def poolmap_all():
                  stg = P.stage
                  P.stage = "p%d_Ib_poolmap" % ps_i
                  for t_i, us in tiles:
                      c0, n = tcols(us)
                      for g in range(4):
                          bi, bs = bank()
                          mm_group(psf(bi)[:, 0:n], [(wmapv[:, g, :], Bv[:, 4 + g, c0:c0 + n])],
                                   [s_B[4 + g][u] for u in us] + [s_wmap], bs)
                          dve(lambda e, bi=bi, g=g, c0=c0, n=n: e.tensor_scalar(
                              out=Bv[:, g, c0:c0 + n], in0=psf(bi)[:, 0:n], scalar1=con[:, C_PB + g:C_PB + g + 1],
                              scalar2=con[:, C_PS + g:C_PS + g + 1], op0=ALU.add, op1=ALU.mult),
                              [bs, s_con], [s_B[g][u] for u in us])
                  P.stage = stg

              ckpt(20 * ps_i + 4, "p%d_conv" % ps_i)
              AR.alloc("cy", 4 * 512 * 4)
              cyv = AR.view("cy", F32, "p (c t) -> p c t", c=4)
              s_cy = [[AR.slot("cy") for _ in range(4)] for _ in range(2)]
              AR.alloc("cb", 8 * 512 * 2)
              cbv = AR.view("cb", BF16, "p (c t) -> p c t", c=8)
              s_cb = [[AR.slot("cb") for _ in range(8)] for _ in range(2)]
              AR.alloc("cm", 4 * 512 * 4)
              cmv = AR.view("cm", F32, "p (c t) -> p c t", c=4)
              s_cm = [[AR.slot("cm") for _ in range(4)] for _ in range(2)]
              CB = dict(cy=cyv, cb=cbv, cm=cmv, s_cy=s_cy, s_cb=s_cb, s_cm=s_cm)
              if has_s:
                  AR.alloc("cyS", 4 * NS * 4)
                  AR.alloc("cbS", 8 * NS * 2)
                  AR.alloc("cmS", 4 * NS * 4)
                  CBS = dict(cy=AR.view("cyS", F32, "p (c t) -> p c t", c=4), cb=AR.view("cbS", BF16, "p (c t) -> p c t", c=8),
                             cm=AR.view("cmS", F32, "p (c t) -> p c t", c=4),
                             s_cy=[[AR.slot("cyS") for _ in range(4)]], s_cb=[[AR.slot("cbS") for _ in range(8)]],
                             s_cm=[[AR.slot("cmS") for _ in range(4)]])
              AR.alloc("hs", 4 * D * 2)
              hsv = AR.view("hs", BF16, "p (i d) -> p i d", i=4)
              s_hs = [AR.slot("hs") for _ in range(4)]
              if has_s:
                  AR.alloc("ec", 4 * NS * 32 * 4)
                  ecv = AR.view("ec", F32, "p (c b k) -> p c b k", c=4, b=NS)
                  s_ec = [AR.slot("ec") for _ in range(4)]
                  AR.alloc("ect", NS * 32 * 4)
                  ectv = AR.view("ect", F32, "p (b k) -> p b k", b=NS)
                  s_ect = AR.slot("ect")

              def conv_aux(n, par, B=None):
                  B = B or CB
                  o = par * 256
                  cyv, cbv, cmv, s_cy, s_cb, s_cm = B["cy"], B["cb"], B["cm"], B["s_cy"], B["s_cb"], B["s_cm"]
                  for c in range(4):
                      dve(lambda e, c=c: e.tensor_copy(out=cbv[:, c, o:o + n], in_=cyv[:, c, o:o + n]),
                          [s_cy[par][c]], [s_cb[par][c]])
                      act(lambda e, c=c: e.activation(out=cbv[:, 4 + c, o:o + n], in_=cyv[:, c, o:o + n], func=AF.Square),
                          [s_cy[par][c]], [s_cb[par][4 + c]])

              def ln_silu(us, n, c0, par, B=None):
                  B = B or CB
                  o = par * 256
                  cyv, cbv, cmv, s_cy, s_cb, s_cm = B["cy"], B["cb"], B["cm"], B["s_cy"], B["s_cb"], B["s_cm"]
                  b1, bs1 = bank()
                  mm_group(psf(b1)[:, 0:n], [(ones, cbv[:, c, o:o + n]) for c in range(4)],
                           [s_cb[par][c] for c in range(4)] + [s_ones], bs1)
                  b2, bs2 = bank()
                  mm_group(psf(b2)[:, 0:n], [(ones, cbv[:, 4 + c, o:o + n]) for c in range(4)],
                           [s_cb[par][4 + c] for c in range(4)] + [s_ones], bs2)
                  mean, msq, var, rstd = (cmv[:, q, o:o + n] for q in range(4))
                  scm = s_cm[par]
                  dve(lambda e: e.tensor_scalar(out=mean, in0=psf(b1)[:, 0:n], scalar1=1.0 / 512, scalar2=None,
                                                op0=ALU.mult), [bs1], [scm[0]])
                  dve(lambda e: e.tensor_tensor(out=msq, in0=mean, in1=mean, op=ALU.mult), [scm[0]], [scm[1]])
                  dve(lambda e: e.scalar_tensor_tensor(out=var, in0=psf(b2)[:, 0:n], scalar=1.0 / 512, in1=msq,
                                                       op0=ALU.mult, op1=ALU.subtract), [bs2, scm[1]], [scm[2]])
                  act(lambda e: e.activation(out=var, in_=var, func=AF.Ln, bias=epsc, scale=1.0),
                      [scm[2], s_con], [scm[2]])
                  act(lambda e: e.activation(out=rstd, in_=var, func=AF.Exp, scale=-0.5), [scm[2]], [scm[3]])
                  for c in range(4):
                      y = cyv[:, c, o:o + n]
                      dve(lambda e, y=y: e.tensor_tensor(out=y, in0=y, in1=mean, op=ALU.subtract),
                          [s_cy[par][c], scm[0]], [s_cy[par][c]])
                      dve(lambda e, y=y: e.tensor_tensor(out=y, in0=y, in1=rstd, op=ALU.mult),
                          [s_cy[par][c], scm[3]], [s_cy[par][c]])
                      act(lambda e, y=y, c=c: e.activation(out=Bv[:, 4 + c, c0:c0 + n], in_=y, func=AF.Silu,
                                                           scale=con[:, C_LG + c:C_LG + c + 1],
                                                           bias=con[:, C_LB + c:C_LB + c + 1]),
                          [s_cy[par][c], s_con], [s_B[4 + c][u] for u in us])

              def conv_mm(p):
                  c0 = 256 * p
                  banks = []
                  for c in range(4):
                      bi, bs = bank()
                      pairs = [(diagv[:, c, k, :], gtv[:, c, c0 + k:c0 + k + 256]) for k in range(CONV_K)]
                      rd = [s_diagc[c]] + {0: [s_gt[c][0], s_gt[c][1]], 1: [s_gt[c][1]], 2: [s_gt[c][1], s_gt[c][2]],
                                       3: [s_gt[c][2]]}[p]
                      mm_group(psf(bi)[:, 0:256], pairs, rd, bs)
                      banks.append((bi, bs))
                  return banks

              def conv_evac(banks, par):
                  o = par * 256
                  for c, (bi, bs) in enumerate(banks):
                      cbias = con[:, C_CB + c:C_CB + c + 1]
                      act(lambda e, bi=bi, c=c, cbias=cbias: e.activation(out=cyv[:, c, o:o + 256], in_=psf(bi)[:, 0:256],
                                                                          func=AF.Identity, bias=cbias),
                          [bs, s_con], [s_cy[par][c]])
                  conv_aux(256, par)

              def conv_sample():
                  for c in range(4):
                      for r in range(4):
                          bi, bs = bank()
                          P.op("pe", lambda e, bi=bi, c=c, r=r: e.transpose(
                              out=psf(bi)[:, 0:120], in_=hscv[0:120, r, c * 128:(c + 1) * 128],
                              identity=idf[0:120, 0:120]), reads=[s_hsc[r], s_con], writes=[bs])
                          act(lambda e, bi=bi, c=c, r=r: e.activation(
                              out=ecv[:, c, 4 * r:4 * r + 4, 0:30],
                              in_=psf(bi)[:, 0:120].rearrange("p (b t) -> p b t", b=4), func=AF.Copy), [bs], [s_ec[c]])
                      dve(lambda e, c=c: e.tensor_copy(out=ecv[:, c, :, 30:31], in_=gsv[:, c, :].unsqueeze(2)),
                          [s_gs[c]], [s_ec[c]])

              def conv_sample2(par):
                  o = par * 256
                  for c in range(4):
                      wt = con[:, C_CW + c * CONV_K:C_CW + (c + 1) * CONV_K]
                      dve(lambda e, c=c, wt=wt: e.tensor_tensor(
                          out=ectv[:, :, 0:31], in0=ecv[:, c, :, 0:31],
                          in1=wt.unsqueeze(1).to_broadcast([128, NS, CONV_K]), op=ALU.mult),
                          [s_ec[c], s_con], [s_ect])
                      dve(lambda e, c=c: e.tensor_reduce(out=stv[:, 48:64], in_=ectv[:, :, 0:31], axis=AX.X, op=ALU.add),
                          [s_ect], [s_stat[3]])
                      cbias = con[:, C_CB + c:C_CB + c + 1]
                      act(lambda e, c=c, cbias=cbias: e.activation(out=CBS["cy"][:, c, 0:NS], in_=stv[:, 48:64],
                                                                   func=AF.Identity, bias=cbias),
                          [s_stat[3], s_con], [CBS["s_cy"][0][c]])
                  conv_aux(NS, 0, CBS)

              def conv_sample2b(par):
                  ln_silu(TS[1], NS, 1024, 0, CBS)
                  out_toks.append(P.dma("sp", lambda e: e.dma_start(out=cs_d[:, 0:29, :], in_=sc_d[:, 1:30, :])))
                  bi, bs = bank()
                  for c in range(4):
                      P.op("pe", lambda e, bi=bi, c=c: e.transpose(out=psf(bi)[0:NS, c * 128:(c + 1) * 128],
                                                                  in_=gsv[:, c, :], identity=idf),
                           reads=[s_gs[c], s_con], writes=[bs] if c == 0 else [])
                  bs.w = {"pe": P.eng["pe"].cnt}
                  AR.alloc("gso", 512 * 4)
                  gsov = AR.view("gso", F32)
                  s_gso = AR.slot("gso")
                  act(lambda e, bi=bi: e.activation(out=gsov[0:NS, :], in_=psf(bi)[0:NS, :], func=AF.Copy), [bs], [s_gso])
                  out_toks.append(P.dma("sp", lambda e: e.dma_start(
                      out=cs_d[:, 29:30, :].rearrange("b o c -> b (o c)"), in_=gsov[0:NS, :]), reads=[s_gso]))
                  AR.free("gso")

              pn = ProjNorm(C_GATT, Av, s_A, hsv, s_hs)

              def wout_units(us_):
                  stg = P.stage
                  P.stage = "p%d_wout_norm2" % ps_i
                  for u in us_:
                      lhs, rd = lhs_cols(Bv, s_B, 8, u)
                      pn.unit(u, lhs, rd, w_out_v, s_wout)
                  P.stage = stg

              for p in range(4):
                  bk_ = conv_mm(p)
                  conv_evac(bk_, p % 2)
                  if p == 0:
                      poolmap_all()
                  if p == 2 and has_s:
                      conv_sample()
                  if p >= 1:
                      ln_silu([2 * (p - 1), 2 * (p - 1) + 1], 256, 256 * (p - 1), (p - 1) % 2)
                  if p >= 2:
                      wout_units([2 * (p - 2), 2 * (p - 2) + 1])
              ln_silu([6, 7], 256, 768, 1)
              AR.free("diag")
              AR.free("gt")
              w_q_v, s_wq = load_w("w_q", w_q_d, 8, D)
              wout_units([4, 5])
              wout_units([6, 7])
              AR.free("cm")
              AR.free("cb")
              AR.free("cy")
              if has_s:
                  conv_sample2(0)

              if last_pass:
                  bi, bs = bank()
                  for c in range(4):
                      P.op("pe", lambda e, bi=bi, c=c: e.transpose(out=psf(bi)[0:30, c * 128:(c + 1) * 128],
                                                                  in_=glv[:, c, 2:32], identity=idf),
                           reads=[s_gl[c], s_con], writes=[bs] if c == 0 else [])
                  bs.w = {"pe": P.eng["pe"].cnt}
                  AR.alloc("cpo", 512 * 4)
                  cpov = AR.view("cpo", F32)
                  s_cpo = AR.slot("cpo")
                  act(lambda e, bi=bi: e.activation(out=cpov[0:30, :], in_=psf(bi)[0:30, :], func=AF.Copy), [bs], [s_cpo])
                  out_toks.append(P.dma("sp", lambda e: e.dma_start(out=cp_d, in_=cpov[0:30, :]), reads=[s_cpo]))
                  AR.free("cpo")

              ckpt(20 * ps_i + 5, "p%d_wout_norm2" % ps_i)
              if ps_i == 0:
                  w_k_v, s_wk = load_w("w_k", w_k_d, 8, D)
                  w_v_v, s_wv = load_w("w_v", w_v_d, 8, D)

              kvb = {}

              def kv_prep():
                  stg = P.stage
                  P.stage = "p0_kv"
                  AR.alloc("mem", 2 * D * 4)
                  memv = AR.view("mem", F32, "p (c d) -> p c d", c=2)
                  s_mem = [AR.slot("mem") for _ in range(2)]
                  AR.alloc("mnt", 8 * 256 * 2)
                  mntv = AR.view("mnt", BF16, "p (f m) -> p f m", f=8)
                  s_mnt = [[AR.slot("mnt") for _ in range(2)] for _ in range(8)]
                  P.dma("sp", lambda e: e.dma_start(out=memv, in_=mem_d.rearrange("(c p) d -> p c d", p=128)),
                        writes=s_mem)
                  kvb["hs"] = [norm_A(memv[:, mc, :], s_mem[mc], 128, hsv, s_hs) for mc in range(2)]
                  kvb["mnt"] = (mntv, s_mnt)
                  P.stage = stg

              def kv_prep_b():
                  stg = P.stage
                  P.stage = "p0_kv"
                  mntv, s_mnt = kvb["mnt"]
                  for mc in range(2):
                      hs, s_h = kvb["hs"][mc]
                      norm_B(hs, s_h, 128, C_GMEM, mntv[:, :, mc * 128:(mc + 1) * 128], [s_mnt[f][mc] for f in range(8)])
                  P.stage = stg

              def kv_stage():
                  stg = P.stage
                  P.stage = "p0_kv"
                  mntv, s_mnt = kvb["mnt"]
                  rd_m = [s_mnt[f][mc] for f in range(8) for mc in range(2)]
                  for j in range(8):
                      bi, bs = bank()
                      pairs = [(w_k_v[:, k, j * 128:(j + 1) * 128], mntv[:, k, :]) for k in range(8)]
                      mm_group(psf(bi)[:, 0:256], pairs, rd_m + [s_wk], bs)
                      act(lambda e, bi=bi, j=j: e.activation(out=KTv[:, j, :], in_=psf(bi)[:, 0:256], func=AF.Copy),
                          [bs], [s_KT[j]])
                  AR.alloc("kvo", 4 * 512 * 4)
                  kvov = AR.view("kvo", F32, "p (i n) -> p i n", i=4)
                  s_kvo = [AR.slot("kvo") for _ in range(4)]
                  oi = [0]
                  for (wv_, sw_, od_, isv) in ((w_k_v, s_wk, ko_d, False), (w_v_v, s_wv, vo_d, True)):
                      for mc in range(2):
                          for h in range(2):
                              bi, bs = bank()
                              pairs = [(mntv[:, k, mc * 128:(mc + 1) * 128], wv_[:, k, h * 512:(h + 1) * 512]) for k in range(8)]
                              mm_group(psf(bi), pairs, [s_mnt[f][mc] for f in range(8)] + [sw_], bs)
                              i = oi[0] % 4
                              oi[0] += 1
                              act(lambda e, bi=bi, i=i: e.activation(out=kvov[:, i, :], in_=psf(bi), func=AF.Copy),
                                  [bs], [s_kvo[i]])
                              if isv:
                                  dve(lambda e, i=i, mc=mc, h=h: e.tensor_copy(out=Vv[:, mc, h * 512:(h + 1) * 512],
                                                                                in_=kvov[:, i, :]),
                                      [s_kvo[i]], [s_V[mc]])
                              out_toks.append(P.dma("sp", lambda e, od_=od_, mc=mc, h=h, i=i: e.dma_start(
                                  out=od_[mc * 128:(mc + 1) * 128, h * 512:(h + 1) * 512], in_=kvov[:, i, :]),
                                  reads=[s_kvo[i]]))
                  AR.free("kvo")
                  AR.free("mnt")
                  AR.free("mem")
                  AR.free("w_k")
                  AR.free("w_v")
                  P.stage = stg

              def sample_tail():
                  stg = P.stage
                  P.stage = "p%d_conv" % ps_i
                  conv_sample2b(0)
                  AR.free("ect")
                  AR.free("ec")
                  AR.free("hsc")
                  kv_stage()
                  wout_units(TS[1])
                  pn.flush()
                  AR.free("gs")
                  AR.free("as")
                  AR.free("cyS")
                  AR.free("cbS")
                  AR.free("cmS")
                  P.stage = stg

              def free_conv():
                  if last_pass:
                      AR.free("gl")
                  AR.free("w_out")

              ckpt(20 * ps_i + 6, "p%d_q" % ps_i)
              if has_s:
                  AR.alloc("qs", D * 2)
                  qsv = AR.view("qs", BF16)
                  s_qs = AR.slot("qs")
              ev = [0]
              for t_i, us in tiles:
                  c0, n = tcols(us)
                  if t_i == 1:
                      pn.flush()
                      if has_s:
                          kv_prep()
                  if us[0] == 8:
                      kv_prep_b()
                      sample_tail()
                  for j in range(8):
                      bi, bs = bank()
                      pairs = [(w_q_v[:, k, j * 128:(j + 1) * 128], Av[:, k, c0:c0 + n]) for k in range(8)]
                      mm_group(psf(bi)[:, 0:n], pairs, [s_A[f][u] for f in range(8) for u in us] + [s_wq], bs)
                      if True:
                          act(lambda e, bi=bi, j=j, c0=c0, n=n: e.activation(out=Bv[:, j, c0:c0 + n], in_=psf(bi)[:, 0:n],
                                                                             func=AF.Copy), [bs], [s_B[j][u] for u in us])
                      else:
                          dve(lambda e, bi=bi, j=j, c0=c0, n=n: e.tensor_copy(out=Bv[:, j, c0:c0 + n], in_=psf(bi)[:, 0:n]),
                              [bs], [s_B[j][u] for u in us])
                      ev[0] += 1
                  if us[0] == 8:
                      for h in range(2):
                          bi, bs = bank()
                          pairs = [(Av[:, k, c0:c0 + NS], w_q_v[:, k, h * 512:(h + 1) * 512]) for k in range(8)]
                          mm_group(psf(bi)[0:NS, :], pairs, [s_A[f][8] for f in range(8)] + [s_wq], bs)
                          act(lambda e, bi=bi, h=h: e.activation(out=qsv[0:NS, h * 512:(h + 1) * 512], in_=psf(bi)[0:NS, :],
                                                                 func=AF.Copy), [bs], [s_qs])
              AR.free("w_q")
              free_conv()
              w_o_v, s_wo = load_w("w_o", w_o_d, 8, D)


              samp = {"next": 0}
              if has_s:
                  NR = 4
                  AR.alloc("sel", NS * 128 * 2)
                  selv = AR.view("sel", BF16, "p (b j) -> p b j", b=NS)
                  s_sel = AR.slot("sel")
                  P.dma("pool", lambda e: e.dma_start(out=selv[0:NS], in_=sel_d.rearrange("p (b j) -> p b j", b=NS)),
                        writes=[s_sel])
                  AR.alloc("kvr", NR * 2 * 2 * D * 2)
                  kvrv = AR.view("kvr", BF16, "p (i w c d) -> p i w c d", i=NR, w=2, c=2)
                  s_kvr = [[AR.slot("kvr") for _ in range(2)] for _ in range(NR)]
                  AR.alloc("pz", NS * 2 * 4 * NS * 2)
                  pzv = AR.view("pz", BF16, "p (b c h j) -> p b c h j", b=NS, c=2, h=4)
                  s_pz = [AR.slot("pz") for _ in range(NS)]
                  AR.alloc("sj", 8 * 256 * 2)
                  sjv = AR.view("sj", BF16, "p (i d) -> p i d", i=8)
                  s_sj = [AR.slot("sj") for _ in range(8)]
                  AR.alloc("sc8", 3 * 3 * 8 * 4)
                  sc8 = AR.view("sc8", F32, "p (i a k) -> p i a k", i=3, a=3)
                  s_sc8 = [[AR.slot("sc8") for _ in range(9)] for _ in range(3)]
                  AR.alloc("ez", 4 * 8 * 2)
                  ezv = AR.view("ez", BF16, "p (i k) -> p i k", i=4)
                  s_ez = [AR.slot("ez") for _ in range(4)]
                  P.op("pool", lambda e: e.memset(pzv, 0.0), writes=s_pz)
                  obk = [bank(), bank()]
                  bank_reserved.update([obk[0][0], obk[1][0]])

                  def samp_load(b):
                      i = b % NR
                      P.dma("pool", lambda e: e.dma_start(
                          out=kvrv[:, i, 0, :, :], in_=ck_d[b].rearrange("(c p) d -> p c d", p=128)), writes=[s_kvr[i][0]])
                      P.dma("pool", lambda e: e.dma_start(
                          out=kvrv[:, i, 1, :, :], in_=cv_d[b].rearrange("(c p) d -> p c d", p=128)), writes=[s_kvr[i][1]])

                  for b_ in range(NR):
                      samp_load(b_)
                  pipe = []

                  def s1(b):
                      i = b % NR
                      qb = []
                      for h2 in range(2):
                          bi, bs = bank()
                          mm_group(psf(bi), [(selv[0:NS, b, :], qsv[0:NS, h2 * 512:(h2 + 1) * 512])], [s_sel, s_qs], bs)
                          qb.append((bi, bs))
                      si = b % 3
                      for mc in range(2):
                          for h in range(4):
                              bi, bs = qb[h // 2]
                              dve(lambda e, mc=mc, h=h, bi=bi: e.scalar_tensor_tensor(
                                  out=sjv[:, mc * 4 + h, :], in0=kvrv[:, i, 0, mc, h * 256:(h + 1) * 256], scalar=0.0625,
                                  in1=psf(bi)[:, (h % 2) * 256:(h % 2) * 256 + 256],
                                  op0=ALU.mult, op1=ALU.mult, accum_out=sc8[:, si, 0, mc * 4 + h:mc * 4 + h + 1]),
                                  [s_kvr[i][0], bs], [s_sj[mc * 4 + h], s_sc8[si][mc * 4 + h]])
                      act(lambda e: e.activation(out=ezv[:, si, :], in_=sc8[:, si, 0, :], func=AF.Exp),
                          s_sc8[si][0:8], [s_ez[si]])

                  def s2(b):
                      si = b % 3
                      bi, bs = bank()
                      mm_group(psf(bi)[:, 0:4], [(ones, ezv[:, si, mc * 4:mc * 4 + 4]) for mc in range(2)],
                               [s_ez[si], s_ones], bs)
                      dve(lambda e: e.reciprocal(out=sc8[:, si, 1, 0:4], in_=psf(bi)[:, 0:4]), [bs], [s_sc8[si][8]])
                      dve(lambda e: e.tensor_tensor(
                          out=pzv[:, b, :, :, b], in0=ezv[:, si, :].rearrange("p (c h) -> p c h", c=2),
                          in1=sc8[:, si, 1, 0:4].unsqueeze(1).to_broadcast([128, 2, 4]), op=ALU.mult),
                          [s_ez[si], s_sc8[si][8]], [s_pz[b]])

                  def s3(b):
                      i = b % NR
                      for h in range(4):
                          ob, obs = obk[h // 2]

                          def pvfn(e, h=h, ob=ob):
                              ins = None
                              for mc in range(2):
                                  ins = e.matmul(out=psf(ob)[0:NS, (h % 2) * 256:(h % 2) * 256 + 256],
                                                 lhsT=pzv[:, b, mc, h, :], rhs=kvrv[:, i, 1, mc, h * 256:(h + 1) * 256],
                                                 start=(b == 0 and mc == 0 and h % 2 == 0), stop=(b == NS - 1 and mc == 1),
                                                 skip_group_check=True)
                              return ins
                          P.op("pe", pvfn, reads=[s_pz[b], s_kvr[i][1]], writes=[obs] if (b == 0 and h % 2 == 0) else [])
                          if b > 0 or h % 2 == 1:
                              _merge(obs.w, {"pe": P.eng["pe"].cnt})
                      if b + NR < NS:
                          samp_load(b + NR)

                  def samp_token():
                      if samp["next"] >= NS and not pipe:
                          if not samp.get("fin"):
                              samp["fin"] = True
                              samp_finish()
                          return
                      stg = P.stage
                      P.stage = "p0_samp_attn"
                      for t in list(pipe):
                          if t[1] == 2:
                              s3(t[0])
                              pipe.remove(t)
                      for t in pipe:
                          if t[1] == 1:
                              s2(t[0])
                              t[1] = 2
                      if samp["next"] < NS:
                          b = samp["next"]
                          samp["next"] = b + 1
                          s1(b)
                          pipe.append([b, 1])
                      P.stage = stg

                  def samp_finish():
                      stg0 = P.stage
                      P.stage = "p0_samp_fin"
                      AR.alloc("osb", D * 2)
                      osbv = AR.view("osb", BF16)
                      s_osb = AR.slot("osb")
                      for h2 in range(2):
                          ob, obs = obk[h2]
                          act(lambda e, ob=ob, h2=h2: e.activation(out=osbv[0:NS, h2 * 512:(h2 + 1) * 512],
                                                                   in_=psf(ob)[0:NS, :], func=AF.Copy), [obs], [s_osb])
                      bank_reserved.clear()
                      bi, bs = bank()
                      pb = psb(bi).rearrange("p (f t) -> p f t", f=8)

                      def tfn2(e, pb=pb):
                          ins = None
                          for f in range(8):
                              ins = e.transpose(out=pb[:, f, 0:NS], in_=osbv[0:NS, f * 128:(f + 1) * 128],
                                                identity=idb[0:NS, 0:NS])
                          return ins
                      P.op("pe", tfn2, reads=[s_osb, s_idb], writes=[bs])
                      dve(lambda e, pb=pb: e.tensor_copy(out=otsv, in_=pb[:, :, 0:NS]), [bs], [s_ots])
                      for nm_ in ("osb", "ez", "sc8", "sj", "pz", "kvr", "sel", "qs"):
                          AR.free(nm_)
                      P.stage = stg0
              else:
                  def samp_token():
                      return

              blocks = ([(0, 4), (4, 4), (8, 4), (12, 4), (16, 3), (19, 3)] if has_s
                        else [(0, 6), (6, 6), (12, 5), (17, 5)])
              NBK = len(blocks)
              NJM = max(nj_ for _, nj_ in blocks)

              def load_block(bk):
                  j0, nj = blocks[bk]
                  nm = "gu%d" % (bk % 2)
                  AR.alloc(nm, 2 * 8 * nj * 128 * 2)
                  v = AR.view(nm, BF16, "p (w k n) -> p w k n", w=2, k=8)
                  sg_, su_ = AR.slot(nm), AR.slot(nm)
                  P.dma("pool", lambda e: e.dma_start(out=v[:, 0], in_=w_g_d[:, j0 * 128:(j0 + nj) * 128].rearrange(
                      "(k p) n -> p k n", p=128)), writes=[sg_])
                  P.dma("pool", lambda e: e.dma_start(out=v[:, 1], in_=w_u_d[:, j0 * 128:(j0 + nj) * 128].rearrange(
                      "(k p) n -> p k n", p=128)), writes=[su_])
                  nmd = "wd%d" % (bk % 2)
                  AR.alloc(nmd, nj * D * 2)
                  vd = AR.view(nmd, BF16, "p (j n) -> p j n", j=nj)
                  sd_ = AR.slot(nmd)
                  P.dma("pool", lambda e: e.dma_start(out=vd, in_=w_d_d[j0 * 128:(j0 + nj) * 128, :].rearrange(
                      "(j p) n -> p j n", p=128)), writes=[sd_])
                  return (v, sg_, su_, vd, sd_, nm, nmd)


              pre_blk0 = load_block(0)
              ckpt(20 * ps_i + 7, "p%d_attn" % ps_i)
              AR.alloc("et", 2 * 2 * 512 * 2)
              etv = AR.view("et", BF16, "p (i c t) -> p i c t", i=2, c=2)
              s_et = [[AR.slot("et") for _ in range(2)] for _ in range(2)]
              AR.alloc("rs", 2 * 512 * 4)
              rsv = AR.view("rs", F32, "p (i t) -> p i t", i=2)
              s_rs = [AR.slot("rs") for _ in range(2)]
              seq = [(us, h) for _, us in (T0, T1) for h in range(4)]

              def att_scores(idx):
                  us, h = seq[idx]
                  c0, n = tcols(us)
                  i = idx % 2
                  for mc in range(2):
                      bi, bs = bank()
                      pairs = [(KTv[:, 2 * h + dc, mc * 128:(mc + 1) * 128], Bv[:, 2 * h + dc, c0:c0 + n]) for dc in range(2)]
                      mm_group(psf(bi)[:, 0:n], pairs,
                               [s_KT[2 * h], s_KT[2 * h + 1]] + [s_B[2 * h + dc][u] for dc in range(2) for u in us], bs)
                      act(lambda e, bi=bi, i=i, mc=mc, n=n: e.activation(out=etv[:, i, mc, 0:n], in_=psf(bi)[:, 0:n],
                                                                         func=AF.Exp, scale=0.0625), [bs], [s_et[i][mc]])

              def att_pv(idx):
                  us, h = seq[idx]
                  c0, n = tcols(us)
                  i = idx % 2
                  bi, bs = bank()
                  mm_group(psf(bi)[:, 0:n], [(ones, etv[:, i, mc, 0:n]) for mc in range(2)],
                           [s_et[i][0], s_et[i][1], s_ones], bs)
                  act(lambda e, bi=bi: e.activation(out=rsv[:, i, 0:n], in_=psf(bi)[:, 0:n], func=AF.Ln), [bs], [s_rs[i]])
                  act(lambda e: e.activation(out=rsv[:, i, 0:n], in_=rsv[:, i, 0:n], func=AF.Exp, scale=-1.0),
                      [s_rs[i]], [s_rs[i]])
                  for dc in range(2):
                      bi, bs = bank()
                      pairs = [(Vv[:, mc, h * 256 + dc * 128:h * 256 + (dc + 1) * 128], etv[:, i, mc, 0:n]) for mc in range(2)]
                      mm_group(psf(bi)[:, 0:n], pairs, [s_V[0], s_V[1], s_et[i][0], s_et[i][1]], bs)
                      dve(lambda e, bi=bi, dc=dc: e.tensor_tensor(
                          out=Av[:, 2 * h + dc, c0:c0 + n], in0=psf(bi)[:, 0:n], in1=rsv[:, i, 0:n], op=ALU.mult),
                          [bs, s_rs[i]], [s_A[2 * h + dc][u] for u in us])

              att_scores(0)
              for idx in range(len(seq)):
                  if idx + 1 < len(seq):
                      att_scores(idx + 1)
                  att_pv(idx)
              AR.free("rs")
              AR.free("et")

              ckpt(20 * ps_i + 9, "p%d_wo_norm3" % ps_i)
              pn3 = ProjNorm(C_GFFN, Bv, s_B, hsv, s_hs)
              for u in range(8):
                  lhs, rd = lhs_cols(Av, s_A, 8, u)
                  pn3.unit(u, lhs, rd, w_o_v, s_wo)
              if not has_s:
                  pn3.unit(8, [otsv[:, f, :] for f in range(8)], [s_ots], w_o_v, s_wo)
              AR.free("w_o")
              pn3_pending = [not has_s]
              if has_s:
                  pn3.flush()
                  AR.free("hs")

              ckpt(20 * ps_i + 10, "p%d_ffn" % ps_i)
              ftiles = [T0, T1] + ([] if has_s else [TS])
              funits = [u for _, us in ftiles for u in us]
              AR.alloc("inter", NJM * 1040 * 2)
              intv = AR.view("inter", BF16, "p (j t) -> p j t", j=NJM)
              s_int = [[AR.slot("inter") for _ in range(9)] for _ in range(NJM)]
              AR.alloc("fsg", 2 * 512 * 4)
              fsgv = AR.view("fsg", F32, "p (i t) -> p i t", i=2)
              s_fsg = [AR.slot("fsg") for _ in range(2)]
              NYO = 2 if ps_i == 1 else 1
              AR.alloc("yo", NYO * D * 4)
              yov = AR.view("yo", F32, "p (i d) -> p i d", i=NYO)
              s_yo = [AR.slot("yo") for _ in range(NYO)]
              fcnt = [0]
              fi = [0]
              it_cnt = [0]

              class PreNorm:
                  def __init__(self):
                      AR.alloc("hs", 4 * D * 2)
                      self.hsv = AR.view("hs", BF16, "p (i d) -> p i d", i=4)
                      self.s_hs = [AR.slot("hs") for _ in range(4)]
                      prefetch["hs"] = (self.hsv, self.s_hs)
                      prefetch["norm1"] = self.flush
                      self.pa, self.pb = [], []

                  def unit(self, u):
                      self.pa.append((u, norm_A1(xu(u), s_X[u], urows(u))))
                      if len(self.pa) > 1:
                          self._a2()
                      if len(self.pb) > 2:
                          self._b()

                  def _a2(self):
                      uu, a1 = self.pa.pop(0)
                      self.pb.append((uu,) + norm_A2(a1, self.hsv, self.s_hs))

                  def _b(self):
                      uu, hs_, sh_ = self.pb.pop(0)
                      c0_, nn_ = ucols(uu)
                      norm_B(hs_, sh_, urows(uu), C_GMIX, Av[:, :, c0_:c0_ + nn_], [s_A[f][uu] for f in range(8)])

                  def flush(self):
                      self.unit(6)
                      self.unit(7)
                      while self.pa:
                          self._a2()
                      while self.pb:
                          self._b()

              pre = None
              cur = pre_blk0
              for bk in range(NBK):
                  j0, nj = blocks[bk]
                  nxt = load_block(bk + 1) if bk + 1 < NBK else None
                  v, sg_, su_, vd, sd_, nm, nmd = cur
                  for t_i, us in ftiles:
                      c0, n = tcols(us)
                      rdB = [s_B[f][u] for f in range(8) for u in us]
                      for j in range(nj):
                          bg, bsg = bank()
                          mm_group(psf(bg)[:, 0:n], [(v[:, 0, k, j * 128:(j + 1) * 128], Bv[:, k, c0:c0 + n]) for k in range(8)],
                                   rdB + [sg_], bsg)
                          bu, bsu = bank()
                          mm_group(psf(bu)[:, 0:n], [(v[:, 1, k, j * 128:(j + 1) * 128], Bv[:, k, c0:c0 + n]) for k in range(8)],
                                   rdB + [su_], bsu)
                          i = fi[0] % 2
                          fi[0] += 1
                          act(lambda e, bg=bg, i=i, n=n: e.activation(out=fsgv[:, i, 0:n], in_=psf(bg)[:, 0:n], func=AF.Silu),
                              [bsg], [s_fsg[i]])
                          dve(lambda e, bu=bu, i=i, j=j, c0=c0, n=n: e.tensor_tensor(
                              out=intv[:, j, c0:c0 + n], in0=psf(bu)[:, 0:n], in1=fsgv[:, i, 0:n], op=ALU.mult),
                              [bsu, s_fsg[i]], [s_int[j][u] for u in us])
                          it_cnt[0] += 1
                          if it_cnt[0] % 2 == 0:
                              samp_token()
                      if pn3_pending[0]:
                          stg_ = P.stage
                          P.stage = "p%d_wo_norm3" % ps_i
                          pn3.flush()
                          AR.free("hs")
                          P.stage = stg_
                          pn3_pending[0] = False
                  if ps_i == 0 and "w_in" not in prefetch and bk >= 2 and samp.get("fin", not has_s):
                      prefetch["w_in"] = load_w_in()
                  if ps_i == 0 and bk == NBK - 1:
                      pre = PreNorm()
                  pend_f = []

                  def finish_unit(uu, f1):
                      P.stage = "p%d_final" % ps_i
                      final_F2(uu, f1, tok0, yov[:, fcnt[0] % NYO, :], s_yo[fcnt[0] % NYO])
                      fcnt[0] += 1
                      if ps_i == 0 and uu < 8:
                          load_x_unit(1, uu)
                          P.stage = "p1_Ia"
                          if uu >= 2:
                              pre.unit(uu - 2)
                      P.stage = "p%d_ffn" % ps_i

                  for u in funits:
                      lhs, rd = lhs_cols(intv, s_int, nj, u)
                      proj_back(u, lhs, rd, vd, sd_)
                      if bk == NBK - 1:
                          P.stage = "p%d_final" % ps_i
                          pend_f.append((u, final_F1(u)))
                          P.stage = "p%d_ffn" % ps_i
                          if len(pend_f) > 1:
                              finish_unit(*pend_f.pop(0))
                  while pend_f:
                      finish_unit(*pend_f.pop(0))
                  AR.free(nm)
                  AR.free(nmd)
                  cur = nxt
              if has_s:
                  while not samp.get("fin"):
                      samp_token()
              AR.free("yo")
              AR.free("fsg")
              AR.free("inter")

          one_pass(0)
          one_pass(1)

        try:
            body()
        except _Stop:
            pass

        fin = {}
        for t in out_toks:
            _merge(fin, t)
        _merge(fin, P.dma_tot)
        E = P.eng["sp"]
        E.ops.append(([(k, v) for k, v in fin.items()], None, None, 'fin'))

        def run(e, name):
            for waits, fn, inc, stg in P.eng[name].ops:
                for k, v in waits:
                    e.wait_ge(sems[k], v)
                if fn is None:
                    continue
                if TAG:
                    with nc.named_scope(stg):
                        ins = fn(e)
                else:
                    ins = fn(e)
                ins.then_inc(sems[inc[0]], inc[1])

        with nc.Block() as block:
            @block.tensor
            def _(e):
                run(e, "pe")

            @block.scalar
            def _(e):
                run(e, "act")

            @block.vector
            def _(e):
                run(e, "dve")

            @block.gpsimd
            def _(e):
                run(e, "pool")

            @block.sync
            def _(e):
                run(e, "sp")
    return nc


_NC_CACHE = {}


def _consts():
    c = np.zeros((128, NCON), np.float32)
    for g, w in enumerate(POOL_W):
        for t in range(16):
            c[:, C_IC + 16 * g + t] = 1.0 / min(w, t + 1)
    c[:, C_EPS] = EPS
    c[:, C_ID:C_ID + 128] = np.eye(128, dtype=np.float32)
    return c


def _prep(x_prompt, x_sample, mem_prompt, state_pool, state_conv, cache_mem_k, cache_mem_v,
          g_mix, w_in, pool_map_w, pool_map_b, pool_scale, conv_dw_w, conv_dw_b, conv_ln_g, conv_ln_b,
          w_out, g_attn, g_mem, w_q, w_k, w_v, w_o, g_ffn, w_gate, w_up, w_down, g_final, cores=None):
    f = lambda a: np.ascontiguousarray(np.asarray(a, dtype=np.float32))
    con = _consts()

    def fm(v, n):
        return f(v).reshape(n, 128).T
    con[:, C_GMIX:C_GMIX + 8] = fm(g_mix[0], 8)
    con[:, C_GATT:C_GATT + 8] = fm(g_attn[0], 8)
    con[:, C_GFFN:C_GFFN + 8] = fm(g_ffn[0], 8)
    con[:, C_GMEM:C_GMEM + 8] = fm(g_mem[0], 8)
    con[:, C_PB:C_PB + 4] = fm(pool_map_b[0], 4)
    con[:, C_PS:C_PS + 4] = fm(pool_scale[0], 4)
    con[:, C_CB:C_CB + 4] = fm(conv_dw_b[0], 4)
    con[:, C_LG:C_LG + 4] = fm(conv_ln_g[0], 4)
    con[:, C_LB:C_LB + 4] = fm(conv_ln_b[0], 4)
    cw = f(conv_dw_w[0])
    for c in range(4):
        con[:, C_CW + c * CONV_K:C_CW + (c + 1) * CONV_K] = cw[:, c * 128:(c + 1) * 128].T
    gfin = np.ascontiguousarray(np.broadcast_to(f(g_final)[None, :], (128, D)))
    sel = np.zeros((NS, NS, 128), np.float32)
    for b in range(NS):
        sel[b, b, :] = 1.0
    sel = sel.reshape(NS, NS * 128)

    shared = {
        "w_in": f(w_in[0]), "wmap": f(pool_map_w[0]), "w_out": f(w_out[0]), "w_q": f(w_q[0]), "w_k": f(w_k[0]),
        "w_v": f(w_v[0]), "w_o": f(w_o[0]), "w_gate": f(w_gate[0]), "w_up": f(w_up[0]), "w_down": f(w_down[0]),
        "consts": con, "gfin": gfin, "sel": sel,
    }
    in_maps = []
    for c in (range(NCORES) if cores is None else cores):
        sl = slice(NS * c, NS * (c + 1))
        m = dict(shared)
        m["x"] = f(x_prompt[c])
        m["xs"] = f(x_sample[sl, 0])
        m["mem"] = f(mem_prompt[c])
        m["sp"] = f(state_pool[0, sl])
        m["sc"] = f(state_conv[0, sl])
        m["ck"] = f(cache_mem_k[0, sl]).reshape(NS, NMEM, D)
        m["cv"] = f(cache_mem_v[0, sl]).reshape(NS, NMEM, D)
        in_maps.append(m)
    return in_maps


def kernel(**inputs):
    if "nc" not in _NC_CACHE:
        _NC_CACHE["nc"] = build_program()
    nc = _NC_CACHE["nc"]
    in_maps = _prep(**inputs)
    res = run_bass_kernel_spmd(nc, in_maps, core_ids=list(range(NCORES)))
    r = res.results
    y_prompt = np.stack([r[c]["y"] for c in range(NCORES)], 0)
    y_sample = np.concatenate([r[c]["ys"] for c in range(NCORES)], 0)[:, None, :]
    pool_p = np.stack([r[c]["pool_p"] for c in range(NCORES)], 0)[None]
    pool_s = np.concatenate([r[c]["pool_s"] for c in range(NCORES)], 0)[None]
    conv_p = np.stack([r[c]["conv_p"] for c in range(NCORES)], 0)[None]
    conv_s = np.concatenate([r[c]["conv_s"] for c in range(NCORES)], 0)[None]
    k_p = np.stack([r[c]["k_out"] for c in range(NCORES)], 0).reshape(1, NCORES, NMEM, 4, 256)
    v_p = np.stack([r[c]["v_out"] for c in range(NCORES)], 0).reshape(1, NCORES, NMEM, 4, 256)
    return (y_prompt.astype(np.float32), y_sample.astype(np.float32), pool_p.astype(np.float32),
            pool_s.astype(np.float32), conv_p.astype(np.float32), conv_s.astype(np.float32),
            k_p.astype(np.float32), v_p.astype(np.float32))
```

```python
from contextlib import ExitStack
import numpy as np
import concourse.bass as bass
import concourse.mybir as mybir
from concourse.bass_utils import run_bass_kernel_spmd

F32 = mybir.dt.float32
BF16 = mybir.dt.bfloat16
AF = mybir.ActivationFunctionType
ALU = mybir.AluOpType
AX = mybir.AxisListType

D = 1024
SEQ = 2048
NS = 16
NMEM = 256
DFF = 2816
EPS = 1e-6
POOL_W = (2, 4, 8, 16)
CONV_K = 31
NCON = 384
NCORES = 8
TAG = False
FUSE_WAIT = True
REWAIT_DMA = False
STRICT = True

C_GMIX, C_GATT, C_GFFN, C_GMEM = 0, 8, 16, 24
C_PB, C_PS, C_CB, C_LG, C_LB = 32, 36, 40, 44, 48
C_CW = 52
C_IC = 176
C_EPS = 240
C_ID = 256


class Slot:
    __slots__ = ("w", "r", "excl")

    def __init__(self, inherit=None, excl=False):
        self.w = {}
        self.r = dict(inherit) if inherit else {}
        self.excl = excl


def _merge(d, toks):
    for k, v in toks.items():
        if d.get(k, 0) < v:
            d[k] = v


class Eng:
    def __init__(self, name):
        self.name = name
        self.ops = []
        self.cnt = 0
        self.seen = {}


class Prog:
    NDMA = {"sp": 24, "pool": 32}

    def __init__(self):
        self.eng = {n: Eng(n) for n in ("pe", "act", "dve", "pool", "sp")}
        self.dma_tot = {}
        self.dma_rr = {"sp": 0, "pool": 0}
        self.stage = "init"

    def _deps(self, E, reads, writes, extra):
        deps = {}
        for s in reads:
            _merge(deps, s.w)
            if s.excl:
                for k, v in s.r.items():
                    if k != E.name and deps.get(k, 0) < v:
                        deps[k] = v
        strict = STRICT and E.name != "pe"
        for s in writes:
            for k, v in s.w.items():
                if (strict or k != E.name) and deps.get(k, 0) < v:
                    deps[k] = v
            for k, v in s.r.items():
                if (strict or k != E.name) and deps.get(k, 0) < v:
                    deps[k] = v
        for t in extra:
            _merge(deps, t)
        waits = []
        for k, v in deps.items():
            if E.seen.get(k, 0) < v or (REWAIT_DMA and k.startswith("d_")):
                E.seen[k] = max(v, E.seen.get(k, 0))
                waits.append((k, v))
        return waits

    def op(self, eng, fn, reads=(), writes=(), extra=()):
        E = self.eng[eng]
        waits = self._deps(E, reads, writes, extra)
        E.cnt += 1
        tok = {E.name: E.cnt}
        E.ops.append((waits, fn, (E.name, 1), self.stage))
        for s in reads:
            _merge(s.r, tok)
        for s in writes:
            s.w = dict(tok)
            s.r = {}
        return tok

    def dma(self, eng, fn, reads=(), writes=(), extra=()):
        E = self.eng[eng]
        j = self.dma_rr[eng]
        self.dma_rr[eng] = (j + 1) % self.NDMA[eng]
        key = "d_%s_%d" % (eng, j)
        prev = self.dma_tot.get(key, 0)
        ex = list(extra)
        if prev:
            ex.append({key: prev})
        waits = self._deps(E, reads, writes, ex)
        tot = prev + 16
        self.dma_tot[key] = tot
        tok = {key: tot}
        E.ops.append((waits, fn, (key, 16), self.stage))
        for s in reads:
            _merge(s.r, tok)
        for s in writes:
            s.w = dict(tok)
            s.r = {}
        return tok

    def sem_keys(self):
        keys = list(self.eng.keys())
        for eng in ("sp", "pool"):
            keys += ["d_%s_%d" % (eng, j) for j in range(self.NDMA[eng])]
        return keys


class Arena:
    def __init__(self, ap, nbytes):
        self.ap = ap
        self.nbytes = nbytes
        self.live = {}
        self.dead = []
        self.peak = 0

    def alloc(self, name, nbytes):
        assert name not in self.live, name
        req = nbytes
        nbytes = (nbytes + 63) // 64 * 64
        spans = sorted((r["off"], r["size"]) for r in self.live.values())
        pos = 0
        off = None
        for o, sz in spans:
            if o - pos >= nbytes:
                off = pos
                break
            pos = max(pos, o + sz)
        if off is None:
            if self.nbytes - pos >= nbytes:
                off = pos
            else:
                raise RuntimeError("arena full allocating %s (%d B); live=%s" % (
                    name, nbytes, {k: (v["off"], v["size"]) for k, v in self.live.items()}))
        inherit = {}
        for (o, sz, toks) in self.dead:
            if o < off + nbytes and off < o + sz:
                _merge(inherit, toks)
        self.dead = [(o, sz, t) for (o, sz, t) in self.dead if not (o >= off and o + sz <= off + nbytes)]
        self.live[name] = dict(off=off, size=nbytes, req=req, slots=[], inherit=inherit)
        self.peak = max(self.peak, off + nbytes)
        return name

    def view(self, name, dtype, pattern=None, **kw):
        r = self.live[name]
        off, nb = r["off"], r["req"]
        a = self.ap[:, off // 2:(off + nb) // 2]
        if dtype != BF16:
            a = a.bitcast(dtype)
        if pattern:
            a = a.rearrange(pattern, **kw)
        return a

    def slot(self, name):
        r = self.live[name]
        sl = Slot(r["inherit"])
        r["slots"].append(sl)
        return sl

    def free(self, name):
        r = self.live.pop(name)
        toks = dict(r["inherit"])
        for sl in r["slots"]:
            _merge(toks, sl.w)
            _merge(toks, sl.r)
        self.dead.append((r["off"], r["size"], toks))


class _Stop(Exception):
    pass


def build_program(limit=None):
    nc = bass.Bass("TRN2", target_bir_lowering=False)
    P = Prog()

    def ckpt(n, name=None):
        if name is not None:
            P.stage = name
        if limit is not None and n >= limit:
            raise _Stop()

    def din(name, shape):
        return nc.dram_tensor(name, list(shape), F32, kind="ExternalInput").ap()

    def dout(name, shape):
        return nc.dram_tensor(name, list(shape), F32, kind="ExternalOutput").ap()

    x_d = din("x", [SEQ, D])
    xs_d = din("xs", [NS, D])
    mem_d = din("mem", [NMEM, D])
    sp_d = din("sp", [NS, 15, 512])
    sc_d = din("sc", [NS, 30, 512])
    ck_d = din("ck", [NS, NMEM, D])
    cv_d = din("cv", [NS, NMEM, D])
    w_in_d = din("w_in", [D, 1536])
    wmap_d = din("wmap", [4, 128, 128])
    w_out_d = din("w_out", [D, D])
    w_q_d = din("w_q", [D, D])
    w_k_d = din("w_k", [D, D])
    w_v_d = din("w_v", [D, D])
    w_o_d = din("w_o", [D, D])
    w_g_d = din("w_gate", [D, DFF])
    w_u_d = din("w_up", [D, DFF])
    w_d_d = din("w_down", [DFF, D])
    con_d = din("consts", [128, NCON])
    gf_d = din("gfin", [128, D])
    sel_d = din("sel", [NS, NS * 128])

    y_d = dout("y", [SEQ, D])
    ys_d = dout("ys", [NS, D])
    pp_d = dout("pool_p", [15, 512])
    ps_d = dout("pool_s", [NS, 15, 512])
    cp_d = dout("conv_p", [30, 512])
    cs_d = dout("conv_s", [NS, 30, 512])
    ko_d = dout("k_out", [NMEM, D])
    vo_d = dout("v_out", [NMEM, D])

    ARENA_BYTES = 212736
    es = ExitStack()
    with es:
        arena_t = es.enter_context(nc.sbuf_tensor("arena", [128, ARENA_BYTES // 2], BF16))
        ps_t = es.enter_context(nc.psum_tensor("ps", [128, 8, 512], F32))
        sems = {k: es.enter_context(nc.semaphore(k)) for k in P.sem_keys()}
        AR = Arena(arena_t[:], ARENA_BYTES)

        bank_slots = [Slot(excl=True) for _ in range(8)]
        bank_rr = [0]
        bank_reserved = set()

        def bank():
            while True:
                i = bank_rr[0]
                bank_rr[0] = (i + 1) % 8
                if i not in bank_reserved:
                    return i, bank_slots[i]

        def psf(i):
            return ps_t[:, i, :]

        def psb(i):
            return ps_t[:, i, :].bitcast(BF16)

        def mm_group(out_ap, pairs, reads, bslot):
            def fn(e):
                n = len(pairs)
                ins = first = None
                for i, (l, r) in enumerate(pairs):
                    ins = e.matmul(out=out_ap, lhsT=l, rhs=r, start=(i == 0), stop=(i == n - 1))
                    if first is None:
                        first = ins
                return (first, ins)
            fn._fuse = True
            return P.op("pe", fn, reads=reads, writes=[bslot])

        class _F:
            def __init__(self, f, fuse):
                self.f = f
                self._fuse = fuse

            def __call__(self, e):
                return self.f(e)

        def act(fn, reads, writes, extra=(), fuse=True):
            return P.op("act", _F(fn, fuse), reads=reads, writes=writes, extra=extra)

        def dve(fn, reads, writes, extra=(), fuse=True):
            return P.op("dve", _F(fn, fuse), reads=reads, writes=writes, extra=extra)

        AR.alloc("con", NCON * 4)
        AR.alloc("gf", D * 4)
        AR.alloc("idb", 128 * 2)
        AR.alloc("ones", 128 * 2)
        AR.alloc("wmap", 4 * 128 * 2)
        AR.alloc("X", 8 * D * 4)
        AR.alloc("XS", D * 4)
        AR.alloc("A", 8 * 1040 * 2)
        AR.alloc("B", 8 * 1040 * 2)
        AR.alloc("KT", 8 * 256 * 2)
        AR.alloc("V", 2 * D * 2)
        AR.alloc("gth", 4 * 32 * 2)
        AR.alloc("ath", 4 * 16 * 4)
        AR.alloc("st", 64 * 4)

        con = AR.view("con", F32)
        gfv = AR.view("gf", F32)
        idb = AR.view("idb", BF16)
        idf = con[:, C_ID:C_ID + 128]
        ones = AR.view("ones", BF16)
        wmapv = AR.view("wmap", BF16, "p (g d) -> p g d", g=4)
        Xv = AR.view("X", F32, "p (c d) -> p c d", c=8)
        XSv = AR.view("XS", F32)
        Av = AR.view("A", BF16, "p (f t) -> p f t", f=8)
        Bv = AR.view("B", BF16, "p (f t) -> p f t", f=8)
        KTv = AR.view("KT", BF16, "p (f m) -> p f m", f=8)
        Vv = AR.view("V", BF16, "p (c d) -> p c d", c=2)
        gthv = AR.view("gth", BF16, "p (c t) -> p c t", c=4)
        athv = AR.view("ath", F32, "p (c t) -> p c t", c=4)
        stv = AR.view("st", F32)

        s_con = AR.slot("con")
        s_gf = AR.slot("gf")
        s_idb = AR.slot("idb")
        s_ones = AR.slot("ones")
        s_wmap = AR.slot("wmap")
        s_X = [AR.slot("X") for _ in range(8)] + [AR.slot("XS")]
        s_A = [[AR.slot("A") for _ in range(9)] for _ in range(8)]
        s_B = [[AR.slot("B") for _ in range(9)] for _ in range(8)]
        s_KT = [AR.slot("KT") for _ in range(8)]
        s_V = [AR.slot("V") for _ in range(2)]
        s_gth = AR.slot("gth")
        s_ath = AR.slot("ath")

        def urows(u):
            return 128 if u < 8 else NS

        def ucols(u):
            return (128 * u, urows(u))

        def xu(u):
            return Xv[:, u, :] if u < 8 else XSv[0:NS, :]

        prefetch = {}
        P.dma("sp", lambda e: e.dma_start(out=con, in_=con_d), writes=[s_con])
        P.dma("sp", lambda e: e.dma_start(out=gfv, in_=gf_d), writes=[s_gf])
        P.dma("pool", lambda e: e.dma_start(out=wmapv, in_=wmap_d.rearrange("g c d -> c g d")), writes=[s_wmap])
        dve(lambda e: e.tensor_copy(out=idb, in_=idf), [s_con], [s_idb])
        P.op("pool", lambda e: e.memset(ones, 1.0), writes=[s_ones])
        P.op("pool", lambda e: e.memset(gthv, 0.0), writes=[s_gth])
        P.op("pool", lambda e: e.memset(athv, 0.0), writes=[s_ath])

        epsc = con[:, C_EPS:C_EPS + 1]

        def load_w_in():
            AR.alloc("w_in", 8 * 1536 * 2)
            v = AR.view("w_in", BF16, "p (k n) -> p k n", k=8)
            sl = [AR.slot("w_in") for _ in range(3)]
            src = w_in_d.rearrange("(k p) n -> p k n", p=128)
            for q in range(3):
                P.dma("pool", lambda e, q=q: e.dma_start(out=v[:, :, q * 512:(q + 1) * 512],
                                                         in_=src[:, :, q * 512:(q + 1) * 512]), writes=[sl[q]])
            return v, sl

        def load_w(name, dram_ap, kchunks, ncols):
            AR.alloc(name, kchunks * ncols * 2)
            v = AR.view(name, BF16, "p (k n) -> p k n", k=kchunks)
            s = AR.slot(name)
            P.dma("pool", lambda e: e.dma_start(out=v, in_=dram_ap.rearrange("(k p) n -> p k n", p=128)),
                  writes=[s])
            return v, s

        hs_ring = {"i": 0, "h": 0}
        s_stat = [AR.slot("st") for _ in range(8)]
        AR.alloc("sqj", D * 2)
        s_sqj = AR.slot("sqj")
        sqjv = AR.view("sqj", BF16)
        AR.alloc("ots", 8 * NS * 2)
        otsv = AR.view("ots", BF16, "p (f b) -> p f b", f=8)
        s_ots = AR.slot("ots")

        def stats_rstd(xin, s_in, rows):
            j_ = hs_ring["i"] % 8
            hs_ring["i"] += 1
            i = j_
            st = stv[0:rows, 4 * j_:4 * j_ + 4]
            s_st = s_stat[j_]
            act(lambda e: e.activation(out=sqjv[0:rows, :], in_=xin, func=AF.Square, accum_out=st[:, 0:1]),
                [s_in], [s_sqj, s_st], fuse=False)
            act(lambda e: e.activation(out=st[:, 1:2], in_=st[:, 0:1], func=AF.Ln, scale=1.0 / D,
                                       bias=epsc[0:rows]), [s_st, s_con], [s_st])
            act(lambda e: e.activation(out=st[:, 2:3], in_=st[:, 1:2], func=AF.Exp, scale=-0.5), [s_st], [s_st])
            return i, st, s_st

        def norm_A1(xin, s_in, rows):
            return (xin, s_in, rows) + stats_rstd(xin, s_in, rows)

        def norm_A2(a1, hsv, s_hs):
            xin, s_in, rows, _, st, s_st = a1
            i = hs_ring["h"] % 4
            hs_ring["h"] += 1
            hs = hsv[0:rows, i, :]
            act(lambda e: e.activation(out=hs, in_=xin, func=AF.Copy, scale=st[:, 2:3]),
                [s_in, s_st], [s_hs[i]])
            return hs, s_hs[i]

        def norm_A(xin, s_in, rows, hsv, s_hs):
            return norm_A2(norm_A1(xin, s_in, rows), hsv, s_hs)

        def norm_B(hs, s_h, rows, gcol, dst3, dst_slots):
            bi, bs = bank()
            pb = psb(bi).rearrange("p (f t) -> p f t", f=8)

            def tfn(e):
                ins = first = None
                for f in range(8):
                    ins = e.transpose(out=pb[:, f, 0:rows], in_=hs[:, f * 128:(f + 1) * 128],
                                      identity=idb[0:rows, 0:rows])
                    if first is None:
                        first = ins
                return (first, ins)
            tfn._fuse = True
            P.op("pe", tfn, reads=[s_h, s_idb], writes=[bs])
            g = con[:, gcol:gcol + 8]
            dve(lambda e: e.tensor_tensor(out=dst3, in0=pb[:, :, 0:rows],
                                          in1=g.unsqueeze(2).to_broadcast([128, 8, rows]), op=ALU.mult),
                [bs, s_con], dst_slots)

        def norm_unit(u, gcol, dstv, s_dst, hsv, s_hs):
            rows = urows(u)
            c0, n = ucols(u)
            hs, s_h = norm_A(xu(u), s_X[u], rows, hsv, s_hs)
            norm_B(hs, s_h, rows, gcol, dstv[:, :, c0:c0 + n], [s_dst[f][u] for f in range(8)])

        def proj_back(u, lhs, rd, wv, s_w):
            rows = urows(u)
            for h in range(2):
                bi, bs = bank()
                pairs = [(l, wv[:, k, h * 512:(h + 1) * 512]) for k, l in enumerate(lhs)]
                mm_group(psf(bi)[0:rows, :], pairs, rd + [s_w], bs)
                xs_ = xu(u)[:, h * 512:(h + 1) * 512]
                pv = psf(bi)[0:rows, :]
                dve(lambda e, xs_=xs_, pv=pv: e.tensor_tensor(out=xs_, in0=xs_, in1=pv, op=ALU.add),
                    [bs, s_X[u]], [s_X[u]])

        def lhs_cols(srcv, s_src, nf, u):
            c0, n = ucols(u)
            return [srcv[:, f, c0:c0 + n] for f in range(nf)], [s_src[f][u] for f in range(nf)]

        class ProjNorm:
            def __init__(self, gcol, dstv, s_dst, hsv, s_hs, lag=2):
                self.pa = []
                self.pb = []
                self.a = (gcol, dstv, s_dst, hsv, s_hs)
                self.lag = lag

            def unit(self, u, lhs, rd, wv, s_w):
                gcol, dstv, s_dst, hsv, s_hs = self.a
                proj_back(u, lhs, rd, wv, s_w)
                self.pa.append((u, norm_A1(xu(u), s_X[u], urows(u))))
                if len(self.pa) > 1:
                    self._a2()
                if len(self.pb) > self.lag:
                    self._b()

            def _a2(self):
                gcol, dstv, s_dst, hsv, s_hs = self.a
                u, a1 = self.pa.pop(0)
                hs, s_h = norm_A2(a1, hsv, s_hs)
                self.pb.append((u, hs, s_h))

            def _b(self):
                gcol, dstv, s_dst, hsv, s_hs = self.a
                u, hs, s_h = self.pb.pop(0)
                c0, n = ucols(u)
                norm_B(hs, s_h, urows(u), gcol, dstv[:, :, c0:c0 + n], [s_dst[f][u] for f in range(8)])

            def flush(self):
                while self.pa:
                    self._a2()
                while self.pb:
                    self._b()

        def tcols(units):
            c0 = 128 * units[0]
            n = sum(urows(u) for u in units)
            return c0, n

        out_toks = []

        def load_x_unit(ps_i, u):
            r0 = 1024 * ps_i + 128 * u
            P.dma("sp", lambda e: e.dma_start(out=Xv[:, u, :], in_=x_d[r0:r0 + 128, :]), writes=[s_X[u]])

        def final_F1(u):
            return stats_rstd(xu(u), s_X[u], urows(u))

        def final_F2(u, f1, tok0, yov, s_yo):
            rows = urows(u)
            xin = xu(u)
            i, st, s_st = f1
            yo = yov[0:rows, :]
            dve(lambda e: e.scalar_tensor_tensor(out=yo, in0=xin, scalar=st[:, 2:3], in1=gfv[0:rows, :],
                                                 op0=ALU.mult, op1=ALU.mult), [s_X[u], s_st, s_gf], [s_yo])
            if u < 8:
                r0 = tok0 + 128 * u
                out_toks.append(P.dma("sp", lambda e: e.dma_start(out=y_d[r0:r0 + 128, :], in_=yo), reads=[s_yo]))
            else:
                out_toks.append(P.dma("sp", lambda e: e.dma_start(out=ys_d, in_=yo), reads=[s_yo]))

        T0, T1, TS = (0, [0, 1, 2, 3]), (1, [4, 5, 6, 7]), (2, [8])

        def body():
          P.stage = "load"
          prefetch["w_in"] = load_w_in()
          for u in range(8):
              load_x_unit(0, u)
          P.dma("sp", lambda e: e.dma_start(out=XSv[0:NS, :], in_=xs_d), writes=[s_X[8]])

          def one_pass(ps_i):
              tok0 = 1024 * ps_i
              has_s = ps_i == 0
              last_pass = ps_i == 1
              tiles = [T0, T1] + ([TS] if has_s else [])
              units = [u for _, us in tiles for u in us]

              ckpt(20 * ps_i + 1, "p%d_Ia" % ps_i)
              if has_s:
                  AR.alloc("hst", 4 * 512 * 4)
                  hstv = AR.view("hst", F32, "p (r c) -> p r c", r=4)
                  s_hst = [AR.slot("hst") for _ in range(4)]
                  spf = sp_d.rearrange("b t c -> (b t) c")
                  for r in range(2):
                      P.dma("sp", lambda e, r=r: e.dma_start(out=hstv[0:120, r, :], in_=spf[120 * r:120 * r + 120, :]),
                            writes=[s_hst[r]])
                  AR.alloc("hsc", 4 * 512 * 4)
                  hscv = AR.view("hsc", F32, "p (r c) -> p r c", r=4)
                  s_hsc = [AR.slot("hsc") for _ in range(4)]
                  scf = sc_d.rearrange("b t c -> (b t) c")
                  for r in range(4):
                      P.dma("sp", lambda e, r=r: e.dma_start(out=hscv[0:120, r, :], in_=scf[120 * r:120 * r + 120, :]),
                            writes=[s_hsc[r]])
              if "w_in" in prefetch:
                  w_in_v, s_win3 = prefetch.pop("w_in")
              else:
                  w_in_v, s_win3 = load_w_in()
              AR.alloc("diag", 4 * CONV_K * 128 * 2)
              diagv = AR.view("diag", BF16, "p (c k j) -> p c k j", c=4, k=CONV_K)
              s_diagc = [AR.slot("diag") for _ in range(4)]

              def build_diag(c):
                  dve(lambda e: e.tensor_tensor(
                      out=diagv[:, c, :, :],
                      in0=idf.unsqueeze(1).to_broadcast([128, CONV_K, 128]),
                      in1=con[:, C_CW + c * CONV_K:C_CW + (c + 1) * CONV_K].unsqueeze(2).to_broadcast([128, CONV_K, 128]),
                      op=ALU.mult), [s_con], [s_diagc[c]])
              if "hs" in prefetch:
                  hsv, s_hs = prefetch.pop("hs")
              else:
                  AR.alloc("hs", 4 * D * 2)
                  hsv = AR.view("hs", BF16, "p (i d) -> p i d", i=4)
                  s_hs = [AR.slot("hs") for _ in range(4)]
              AR.alloc("gt", 4 * 1056 * 2)
              gtv = AR.view("gt", BF16, "p (c t) -> p c t", c=4)
              s_gt = [[AR.slot("gt") for _ in range(3)] for _ in range(4)]

              pa, pb = [], []
              pre_done = prefetch.pop("norm1", None)
              if pre_done is not None:
                  pre_done()
                  for c_ in range(4):
                      build_diag(c_)
              for n_, u in enumerate([] if pre_done is not None else units):
                  pa.append((u, norm_A1(xu(u), s_X[u], urows(u))))
                  if len(pa) > 1:
                      uu, a1 = pa.pop(0)
                      pb.append((uu,) + norm_A2(a1, hsv, s_hs))
                  if len(pb) > 1:
                      uu, hs_, sh_ = pb.pop(0)
                      c0_, nn_ = ucols(uu)
                      norm_B(hs_, sh_, urows(uu), C_GMIX, Av[:, :, c0_:c0_ + nn_], [s_A[f][uu] for f in range(8)])
                  if 2 <= n_ < 6:
                      build_diag(n_ - 2)
              while pa:
                  uu, a1 = pa.pop(0)
                  pb.append((uu,) + norm_A2(a1, hsv, s_hs))
              while pb:
                  uu, hs_, sh_ = pb.pop(0)
                  c0_, nn_ = ucols(uu)
                  norm_B(hs_, sh_, urows(uu), C_GMIX, Av[:, :, c0_:c0_ + nn_], [s_A[f][uu] for f in range(8)])

              dve(lambda e: e.tensor_copy(out=gtv[:, :, 0:30], in_=gthv[:, :, 0:30]), [s_gth],
                  [s_gt[c][0] for c in range(4)])

              AR.alloc("at", 2 * 4 * 528 * 4)
              atv = AR.view("at", F32, "p (i c t) -> p i c t", i=2, c=4)
              s_at = [[AR.slot("at") for _ in range(4)] for _ in range(2)]
              AR.alloc("pt", 2 * 528 * 4)
              ptv = AR.view("pt", F32, "p (i t) -> p i t", i=2)
              s_pt = [AR.slot("pt") for _ in range(2)]
              AR.alloc("sg", 2 * 512 * 4)
              sgv = AR.view("sg", F32, "p (i t) -> p i t", i=2)
              s_sg = [AR.slot("sg") for _ in range(2)]
              if has_s:
                  AR.alloc("as", 4 * NS * 4)
                  asv = AR.view("as", F32, "p (c b) -> p c b", c=4)
                  s_as = [AR.slot("as") for _ in range(4)]
                  AR.alloc("gs", 4 * NS * 4)
                  gsv = AR.view("gs", F32, "p (c b) -> p c b", c=4)
                  s_gs = [AR.slot("gs") for _ in range(4)]
              if last_pass:
                  AR.alloc("gl", 4 * 32 * 4)
                  glv = AR.view("gl", F32, "p (c t) -> p c t", c=4)
                  s_gl = [AR.slot("gl") for _ in range(4)]

              sgi = [0]
              deferred_pool = []
              for t_i, us in tiles:
                  c0, n = tcols(us)
                  is_s = us[0] == 8
                  rd_A = lambda: [s_A[f][u] for f in range(8) for u in us]
                  ai = t_i % 2
                  for j in range(4):
                      bi, bs = bank()
                      pairs = [(w_in_v[:, k, j * 128:(j + 1) * 128], Av[:, k, c0:c0 + n]) for k in range(8)]
                      mm_group(psf(bi)[:, 0:n], pairs, rd_A() + [s_win3[0]], bs)
                      if is_s:
                          act(lambda e, bi=bi, j=j: e.activation(out=asv[:, j, :], in_=psf(bi)[:, 0:NS], func=AF.Copy),
                              [bs], [s_as[j]])
                      else:
                          act(lambda e, bi=bi, j=j, ai=ai: e.activation(out=atv[:, ai, j, 15:15 + 512],
                                                                         in_=psf(bi)[:, 0:512], func=AF.Copy),
                              [bs], [s_at[ai][j]])
                  for j in range(4):
                      bv, bsv = bank()
                      pairs = [(w_in_v[:, k, 512 + j * 128:512 + (j + 1) * 128], Av[:, k, c0:c0 + n]) for k in range(8)]
                      mm_group(psf(bv)[:, 0:n], pairs, rd_A() + [s_win3[1]], bsv)
                      bg, bsg = bank()
                      pairs = [(w_in_v[:, k, 1024 + j * 128:1024 + (j + 1) * 128], Av[:, k, c0:c0 + n]) for k in range(8)]
                      mm_group(psf(bg)[:, 0:n], pairs, rd_A() + [s_win3[2]], bsg)
                      si = sgi[0] % 2
                      sgi[0] += 1
                      act(lambda e, bg=bg, si=si, n=n: e.activation(out=sgv[:, si, 0:n], in_=psf(bg)[:, 0:n],
                                                                   func=AF.Sigmoid), [bsg], [s_sg[si]])
                      if is_s:
                          dve(lambda e, bv=bv, si=si, j=j: e.tensor_tensor(out=gsv[:, j, :], in0=psf(bv)[:, 0:NS],
                                                                           in1=sgv[:, si, 0:NS], op=ALU.mult),
                              [bsv, s_sg[si]], [s_gs[j]])
                      else:
                          g0 = 30 + c0
                          if last_pass and t_i == 1:
                              dve(lambda e, bv=bv, si=si, j=j: e.tensor_tensor(
                                  out=glv[:, j, :], in0=psf(bv)[:, 480:512], in1=sgv[:, si, 480:512], op=ALU.mult),
                                  [bsv, s_sg[si]], [s_gl[j]])
                          dve(lambda e, bv=bv, si=si, j=j, g0=g0: e.tensor_tensor(
                              out=gtv[:, j, g0:g0 + 512], in0=psf(bv)[:, 0:512], in1=sgv[:, si, 0:512], op=ALU.mult),
                              [bsv, s_sg[si]], [s_gt[j][1 + t_i]])
                  if is_s:
                      for fn_ in deferred_pool:
                          fn_()
                      del deferred_pool[:]

                  def pooling(t_i=t_i, us=us, c0=c0, ai=ai):
                      first_global = (ps_i == 0 and t_i == 0)
                      if t_i == 0:
                          dve(lambda e, ai=ai: e.tensor_copy(out=atv[:, ai, :, 0:15], in_=athv[:, :, 0:15]),
                              [s_ath], [s_at[ai][c] for c in range(4)])
                      else:
                          dve(lambda e, ai=ai: e.tensor_copy(out=atv[:, ai, :, 0:15], in_=atv[:, 1 - ai, :, 512:527]),
                              [s_at[1 - ai][c] for c in range(4)], [s_at[ai][c] for c in range(4)])
                      if t_i == 1:
                          dve(lambda e, ai=ai: e.tensor_copy(out=athv[:, :, 0:15], in_=atv[:, ai, :, 512:527]),
                              [s_at[ai][c] for c in range(4)], [s_ath])
                          dve(lambda e: e.tensor_copy(out=gthv[:, :, 0:30], in_=gtv[:, :, 1024:1054]),
                              [s_gt[c][2] for c in range(4)], [s_gth])
                      for g in range(4):
                          w = POOL_W[g]
                          a = atv[:, ai, g, :]
                          src, ssrc = a, s_at[ai][g]
                          sh = 1
                          pi = 0
                          while sh < w:
                              dst = ptv[:, pi, :]
                              lo = 2 * sh - 1
                              dve(lambda e, dst=dst, src=src, lo=lo, sh=sh: e.tensor_tensor(
                                  out=dst[:, lo:527], in0=src[:, lo:527], in1=src[:, lo - sh:527 - sh], op=ALU.add),
                                  [ssrc], [s_pt[pi]])
                              src, ssrc = dst, s_pt[pi]
                              pi = 1 - pi
                              sh *= 2
                          dve(lambda e, src=src, a=a, w=w, g=g, c0=c0: e.scalar_tensor_tensor(
                              out=Bv[:, 4 + g, c0:c0 + 512], in0=src[:, 15:527], scalar=1.0 / w, in1=a[:, 15:527],
                              op0=ALU.mult, op1=ALU.subtract), [ssrc, s_at[ai][g]], [s_B[4 + g][u] for u in us])
                          if first_global:
                              ic = con[:, C_IC + 16 * g:C_IC + 16 * g + 16]
                              tmp = stv[:, 32:48]
                              dve(lambda e, src=src, ic=ic, tmp=tmp: e.tensor_tensor(out=tmp, in0=src[:, 15:31], in1=ic,
                                                                                    op=ALU.mult),
                                  [ssrc, s_con], [s_stat[3]])
                              dve(lambda e, tmp=tmp, a=a, g=g: e.tensor_tensor(out=Bv[:, 4 + g, 0:16], in0=tmp,
                                                                              in1=a[:, 15:31], op=ALU.subtract),
                                  [s_stat[3], s_at[ai][g]], [s_B[4 + g][0]])
                      if last_pass and t_i == 1:
                          bi, bs = bank()
                          for g in range(4):
                              P.op("pe", lambda e, bi=bi, g=g, ai=ai: e.transpose(
                                  out=psf(bi)[0:15, g * 128:(g + 1) * 128], in_=atv[:, ai, g, 512:527], identity=idf),
                                  reads=[s_at[ai][g], s_con], writes=[bs] if g == 0 else [])
                          AR.alloc("ppo", 512 * 4)
                          ppov = AR.view("ppo", F32)
                          s_ppo = AR.slot("ppo")
                          bs.w = {"pe": P.eng["pe"].cnt}
                          act(lambda e, bi=bi: e.activation(out=ppov[0:15, :], in_=psf(bi)[0:15, :], func=AF.Copy),
                              [bs], [s_ppo])
                          out_toks.append(P.dma("sp", lambda e: e.dma_start(out=pp_d, in_=ppov[0:15, :]), reads=[s_ppo]))
                          AR.free("ppo")

                  if not is_s:
                      if has_s and t_i == 1:
                          deferred_pool.append(pooling)
                      else:
                          pooling()
              for fn_ in deferred_pool:
                  fn_()

              AR.free("sg")
              AR.free("pt")
              AR.free("hs")
              AR.free("w_in")

              ckpt(20 * ps_i + 2, "p%d_samp_pool" % ps_i)
              if has_s:
                  AR.alloc("es", 4 * NS * 16 * 4)
                  esv = AR.view("es", F32, "p (c b t) -> p c b t", c=4, b=NS)
                  s_es = [AR.slot("es") for _ in range(4)]
                  for c in range(4):
                      for r in range(2):
                          bi, bs = bank()
                          P.op("pe", lambda e, bi=bi, c=c, r=r: e.transpose(
                              out=psf(bi)[:, 0:120], in_=hstv[0:120, r, c * 128:(c + 1) * 128], identity=idf[0:120, 0:120]),
                              reads=[s_hst[r], s_con], writes=[bs])
                          act(lambda e, bi=bi, c=c, r=r: e.activation(
                              out=esv[:, c, 8 * r:8 * r + 8, 0:15],
                              in_=psf(bi)[:, 0:120].rearrange("p (b t) -> p b t", b=8), func=AF.Copy), [bs], [s_es[c]])
                      dve(lambda e, c=c: e.tensor_copy(out=esv[:, c, :, 15:16], in_=asv[:, c, :].unsqueeze(2)),
                          [s_as[c]], [s_es[c]])
                      w = POOL_W[c]
                      ws = stv[:, 48:64]
                      dve(lambda e, c=c, w=w, ws=ws: e.tensor_reduce(out=ws, in_=esv[:, c, :, 16 - w:16], axis=AX.X,
                                                                    op=ALU.add), [s_es[c]], [s_stat[3]])
                      dve(lambda e, c=c, w=w, ws=ws: e.scalar_tensor_tensor(
                          out=Bv[:, 4 + c, 1024:1040], in0=ws, scalar=1.0 / w, in1=asv[:, c, :], op0=ALU.mult,
                          op1=ALU.subtract), [s_stat[3], s_as[c]], [s_B[4 + c][8]])
                  out_toks.append(P.dma("sp", lambda e: e.dma_start(out=ps_d[:, 0:14, :], in_=sp_d[:, 1:15, :])))
                  bi, bs = bank()
                  for c in range(4):
                      P.op("pe", lambda e, bi=bi, c=c: e.transpose(out=psf(bi)[0:NS, c * 128:(c + 1) * 128],
                                                                  in_=asv[:, c, :], identity=idf),
                           reads=[s_as[c], s_con], writes=[bs] if c == 0 else [])
                  bs.w = {"pe": P.eng["pe"].cnt}
                  AR.alloc("aso", 512 * 4)
                  asov = AR.view("aso", F32)
                  s_aso = AR.slot("aso")
                  act(lambda e, bi=bi: e.activation(out=asov[0:NS, :], in_=psf(bi)[0:NS, :], func=AF.Copy), [bs], [s_aso])
                  out_toks.append(P.dma("sp", lambda e: e.dma_start(
                      out=ps_d[:, 14:15, :].rearrange("b o c -> b (o c)"), in_=asov[0:NS, :]), reads=[s_aso]))
                  AR.free("aso")
                  AR.free("es")
                  AR.free("hst")
              AR.free("at")

              ckpt(20 * ps_i + 3, "p%d_Ib_poolmap" % ps_i)
              w_out_v, s_wout = load_w("w_out", w_out_d, 8, D)

              def poolmap_all():
                  stg = P.stage
                  P.stage = "p%d_Ib_poolmap" % ps_i
                  for t_i, us in tiles:
                      c0, n = tcols(us)
                      for g in range(4):
                          bi, bs = bank()
                          mm_group(psf(bi)[:, 0:n], [(wmapv[:, g, :], Bv[:, 4 + g, c0:c0 + n])],
                                   [s_B[4 + g][u] for u in us] + [s_wmap], bs)
                          dve(lambda e, bi=bi, g=g, c0=c0, n=n: e.tensor_scalar(
                              out=Bv[:, g, c0:c0 + n], in0=psf(bi)[:, 0:n], scalar1=con[:, C_PB + g:C_PB + g + 1],
                              scalar2=con[:, C_PS + g:C_PS + g + 1], op0=ALU.add, op1=ALU.mult),
                              [bs, s_con], [s_B[g][u] for u in us])
                  P.stage = stg

              ckpt(20 * ps_i + 4, "p%d_conv" % ps_i)
              AR.alloc("cy", 4 * 512 * 4)
              cyv = AR.view("cy", F32, "p (c t) -> p c t", c=4)
              s_cy = [[AR.slot("cy") for _ in range(4)] for _ in range(2)]
              AR.alloc("cb", 8 * 512 * 2)
              cbv = AR.view("cb", BF16, "p (c t) -> p c t", c=8)
              s_cb = [[AR.slot("cb") for _ in range(8)] for _ in range(2)]
              AR.alloc("cm", 4 * 512 * 4)
              cmv = AR.view("cm", F32, "p (c t) -> p c t", c=4)
              s_cm = [[AR.slot("cm") for _ in range(4)] for _ in range(2)]
              CB = dict(cy=cyv, cb=cbv, cm=cmv, s_cy=s_cy, s_cb=s_cb, s_cm=s_cm)
              if has_s:
                  AR.alloc("cyS", 4 * NS * 4)
                  AR.alloc("cbS", 8 * NS * 2)
                  AR.alloc("cmS", 4 * NS * 4)
                  CBS = dict(cy=AR.view("cyS", F32, "p (c t) -> p c t", c=4), cb=AR.view("cbS", BF16, "p (c t) -> p c t", c=8),
                             cm=AR.view("cmS", F32, "p (c t) -> p c t", c=4),
                             s_cy=[[AR.slot("cyS") for _ in range(4)]], s_cb=[[AR.slot("cbS") for _ in range(8)]],
                             s_cm=[[AR.slot("cmS") for _ in range(4)]])
              AR.alloc("hs", 4 * D * 2)
              hsv = AR.view("hs", BF16, "p (i d) -> p i d", i=4)
              s_hs = [AR.slot("hs") for _ in range(4)]
              if has_s:
                  AR.alloc("ec", 4 * NS * 32 * 4)
                  ecv = AR.view("ec", F32, "p (c b k) -> p c b k", c=4, b=NS)
                  s_ec = [AR.slot("ec") for _ in range(4)]
                  AR.alloc("ect", NS * 32 * 4)
                  ectv = AR.view("ect", F32, "p (b k) -> p b k", b=NS)
                  s_ect = AR.slot("ect")

              def conv_aux(n, par, B=None):
                  B = B or CB
                  o = par * 256
                  cyv, cbv, cmv, s_cy, s_cb, s_cm = B["cy"], B["cb"], B["cm"], B["s_cy"], B["s_cb"], B["s_cm"]
                  for c in range(4):
                      dve(lambda e, c=c: e.tensor_copy(out=cbv[:, c, o:o + n], in_=cyv[:, c, o:o + n]),
                          [s_cy[par][c]], [s_cb[par][c]])
                      act(lambda e, c=c: e.activation(out=cbv[:, 4 + c, o:o + n], in_=cyv[:, c, o:o + n], func=AF.Square),
                          [s_cy[par][c]], [s_cb[par][4 + c]])

              def ln_silu(us, n, c0, par, B=None):
                  B = B or CB
                  o = par * 256
                  cyv, cbv, cmv, s_cy, s_cb, s_cm = B["cy"], B["cb"], B["cm"], B["s_cy"], B["s_cb"], B["s_cm"]
                  b1, bs1 = bank()
                  mm_group(psf(b1)[:, 0:n], [(ones, cbv[:, c, o:o + n]) for c in range(4)],
                           [s_cb[par][c] for c in range(4)] + [s_ones], bs1)
                  b2, bs2 = bank()
                  mm_group(psf(b2)[:, 0:n], [(ones, cbv[:, 4 + c, o:o + n]) for c in range(4)],
                           [s_cb[par][4 + c] for c in range(4)] + [s_ones], bs2)
                  mean, msq, var, rstd = (cmv[:, q, o:o + n] for q in range(4))
                  scm = s_cm[par]
                  dve(lambda e: e.tensor_scalar(out=mean, in0=psf(b1)[:, 0:n], scalar1=1.0 / 512, scalar2=None,
                                                op0=ALU.mult), [bs1], [scm[0]])
                  dve(lambda e: e.tensor_tensor(out=msq, in0=mean, in1=mean, op=ALU.mult), [scm[0]], [scm[1]])
                  dve(lambda e: e.scalar_tensor_tensor(out=var, in0=psf(b2)[:, 0:n], scalar=1.0 / 512, in1=msq,
                                                       op0=ALU.mult, op1=ALU.subtract), [bs2, scm[1]], [scm[2]])
                  act(lambda e: e.activation(out=var, in_=var, func=AF.Ln, bias=epsc, scale=1.0),
                      [scm[2], s_con], [scm[2]])
                  act(lambda e: e.activation(out=rstd, in_=var, func=AF.Exp, scale=-0.5), [scm[2]], [scm[3]])
                  for c in range(4):
                      y = cyv[:, c, o:o + n]
                      dve(lambda e, y=y: e.tensor_tensor(out=y, in0=y, in1=mean, op=ALU.subtract),
                          [s_cy[par][c], scm[0]], [s_cy[par][c]])
                      dve(lambda e, y=y: e.tensor_tensor(out=y, in0=y, in1=rstd, op=ALU.mult),
                          [s_cy[par][c], scm[3]], [s_cy[par][c]])
                      act(lambda e, y=y, c=c: e.activation(out=Bv[:, 4 + c, c0:c0 + n], in_=y, func=AF.Silu,
                                                           scale=con[:, C_LG + c:C_LG + c + 1],
                                                           bias=con[:, C_LB + c:C_LB + c + 1]),
                          [s_cy[par][c], s_con], [s_B[4 + c][u] for u in us])

              def conv_mm(p):
                  c0 = 256 * p
                  banks = []
                  for c in range(4):
                      bi, bs = bank()
                      pairs = [(diagv[:, c, k, :], gtv[:, c, c0 + k:c0 + k + 256]) for k in range(CONV_K)]
                      rd = [s_diagc[c]] + {0: [s_gt[c][0], s_gt[c][1]], 1: [s_gt[c][1]], 2: [s_gt[c][1], s_gt[c][2]],
                                       3: [s_gt[c][2]]}[p]
                      mm_group(psf(bi)[:, 0:256], pairs, rd, bs)
                      banks.append((bi, bs))
                  return banks

              def conv_evac(banks, par):
                  o = par * 256
                  for c, (bi, bs) in enumerate(banks):
                      cbias = con[:, C_CB + c:C_CB + c + 1]
                      act(lambda e, bi=bi, c=c, cbias=cbias: e.activation(out=cyv[:, c, o:o + 256], in_=psf(bi)[:, 0:256],
                                                                          func=AF.Identity, bias=cbias),
                          [bs, s_con], [s_cy[par][c]])
                  conv_aux(256, par)

              def conv_sample():
                  for c in range(4):
                      for r in range(4):
                          bi, bs = bank()
                          P.op("pe", lambda e, bi=bi, c=c, r=r: e.transpose(
                              out=psf(bi)[:, 0:120], in_=hscv[0:120, r, c * 128:(c + 1) * 128],
                              identity=idf[0:120, 0:120]), reads=[s_hsc[r], s_con], writes=[bs])
                          act(lambda e, bi=bi, c=c, r=r: e.activation(
                              out=ecv[:, c, 4 * r:4 * r + 4, 0:30],
                              in_=psf(bi)[:, 0:120].rearrange("p (b t) -> p b t", b=4), func=AF.Copy), [bs], [s_ec[c]])
                      dve(lambda e, c=c: e.tensor_copy(out=ecv[:, c, :, 30:31], in_=gsv[:, c, :].unsqueeze(2)),
                          [s_gs[c]], [s_ec[c]])

              def conv_sample2(par):
                  o = par * 256
                  for c in range(4):
                      wt = con[:, C_CW + c * CONV_K:C_CW + (c + 1) * CONV_K]
                      dve(lambda e, c=c, wt=wt: e.tensor_tensor(
                          out=ectv[:, :, 0:31], in0=ecv[:, c, :, 0:31],
                          in1=wt.unsqueeze(1).to_broadcast([128, NS, CONV_K]), op=ALU.mult),
                          [s_ec[c], s_con], [s_ect])
                      dve(lambda e, c=c: e.tensor_reduce(out=stv[:, 48:64], in_=ectv[:, :, 0:31], axis=AX.X, op=ALU.add),
                          [s_ect], [s_stat[3]])
                      cbias = con[:, C_CB + c:C_CB + c + 1]
                      act(lambda e, c=c, cbias=cbias: e.activation(out=CBS["cy"][:, c, 0:NS], in_=stv[:, 48:64],
                                                                   func=AF.Identity, bias=cbias),
                          [s_stat[3], s_con], [CBS["s_cy"][0][c]])
                  conv_aux(NS, 0, CBS)

              def conv_sample2b(par):
                  ln_silu(TS[1], NS, 1024, 0, CBS)
                  out_toks.append(P.dma("sp", lambda e: e.dma_start(out=cs_d[:, 0:29, :], in_=sc_d[:, 1:30, :])))
                  bi, bs = bank()
                  for c in range(4):
                      P.op("pe", lambda e, bi=bi, c=c: e.transpose(out=psf(bi)[0:NS, c * 128:(c + 1) * 128],
                                                                  in_=gsv[:, c, :], identity=idf),
                           reads=[s_gs[c], s_con], writes=[bs] if c == 0 else [])
                  bs.w = {"pe": P.eng["pe"].cnt}
                  AR.alloc("gso", 512 * 4)
                  gsov = AR.view("gso", F32)
                  s_gso = AR.slot("gso")
                  act(lambda e, bi=bi: e.activation(out=gsov[0:NS, :], in_=psf(bi)[0:NS, :], func=AF.Copy), [bs], [s_gso])
                  out_toks.append(P.dma("sp", lambda e: e.dma_start(
                      out=cs_d[:, 29:30, :].rearrange("b o c -> b (o c)"), in_=gsov[0:NS, :]), reads=[s_gso]))
                  AR.free("gso")

              pn = ProjNorm(C_GATT, Av, s_A, hsv, s_hs)

              def wout_units(us_):
                  stg = P.stage
                  P.stage = "p%d_wout_norm2" % ps_i
                  for u in us_:
                      lhs, rd = lhs_cols(Bv, s_B, 8, u)
                      pn.unit(u, lhs, rd, w_out_v, s_wout)
                  P.stage = stg

              for p in range(4):
                  bk_ = conv_mm(p)
                  conv_evac(bk_, p % 2)
                  if p == 0:
                      poolmap_all()
                  if p == 2 and has_s:
                      conv_sample()
                  if p >= 1:
                      ln_silu([2 * (p - 1), 2 * (p - 1) + 1], 256, 256 * (p - 1), (p - 1) % 2)
                  if p >= 2:
                      wout_units([2 * (p - 2), 2 * (p - 2) + 1])
              ln_silu([6, 7], 256, 768, 1)
              AR.free("diag")
              AR.free("gt")
              w_q_v, s_wq = load_w("w_q", w_q_d, 8, D)
              wout_units([4, 5])
              wout_units([6, 7])
              AR.free("cm")
              AR.free("cb")
              AR.free("cy")
              if has_s:
                  conv_sample2(0)

              if last_pass:
                  bi, bs = bank()
                  for c in range(4):
                      P.op("pe", lambda e, bi=bi, c=c: e.transpose(out=psf(bi)[0:30, c * 128:(c + 1) * 128],
                                                                  in_=glv[:, c, 2:32], identity=idf),
                           reads=[s_gl[c], s_con], writes=[bs] if c == 0 else [])
                  bs.w = {"pe": P.eng["pe"].cnt}
                  AR.alloc("cpo", 512 * 4)
                  cpov = AR.view("cpo", F32)
                  s_cpo = AR.slot("cpo")
                  act(lambda e, bi=bi: e.activation(out=cpov[0:30, :], in_=psf(bi)[0:30, :], func=AF.Copy), [bs], [s_cpo])
                  out_toks.append(P.dma("sp", lambda e: e.dma_start(out=cp_d, in_=cpov[0:30, :]), reads=[s_cpo]))
                  AR.free("cpo")

              ckpt(20 * ps_i + 5, "p%d_wout_norm2" % ps_i)
              if ps_i == 0:
                  w_k_v, s_wk = load_w("w_k", w_k_d, 8, D)
                  w_v_v, s_wv = load_w("w_v", w_v_d, 8, D)

              kvb = {}

              def kv_prep():
                  stg = P.stage
                  P.stage = "p0_kv"
                  AR.alloc("mem", 2 * D * 4)
                  memv = AR.view("mem", F32, "p (c d) -> p c d", c=2)
                  s_mem = [AR.slot("mem") for _ in range(2)]
                  AR.alloc("mnt", 8 * 256 * 2)
                  mntv = AR.view("mnt", BF16, "p (f m) -> p f m", f=8)
                  s_mnt = [[AR.slot("mnt") for _ in range(2)] for _ in range(8)]
                  P.dma("sp", lambda e: e.dma_start(out=memv, in_=mem_d.rearrange("(c p) d -> p c d", p=128)),
                        writes=s_mem)
                  kvb["hs"] = [norm_A(memv[:, mc, :], s_mem[mc], 128, hsv, s_hs) for mc in range(2)]
                  kvb["mnt"] = (mntv, s_mnt)
                  P.stage = stg

              def kv_prep_b():
                  stg = P.stage
                  P.stage = "p0_kv"
                  mntv, s_mnt = kvb["mnt"]
                  for mc in range(2):
                      hs, s_h = kvb["hs"][mc]
                      norm_B(hs, s_h, 128, C_GMEM, mntv[:, :, mc * 128:(mc + 1) * 128], [s_mnt[f][mc] for f in range(8)])
                  P.stage = stg

              def kv_stage():
                  stg = P.stage
                  P.stage = "p0_kv"
                  mntv, s_mnt = kvb["mnt"]
                  rd_m = [s_mnt[f][mc] for f in range(8) for mc in range(2)]
                  for j in range(8):
                      bi, bs = bank()
                      pairs = [(w_k_v[:, k, j * 128:(j + 1) * 128], mntv[:, k, :]) for k in range(8)]
                      mm_group(psf(bi)[:, 0:256], pairs, rd_m + [s_wk], bs)
                      act(lambda e, bi=bi, j=j: e.activation(out=KTv[:, j, :], in_=psf(bi)[:, 0:256], func=AF.Copy),
                          [bs], [s_KT[j]])
                  AR.alloc("kvo", 4 * 512 * 4)
                  kvov = AR.view("kvo", F32, "p (i n) -> p i n", i=4)
                  s_kvo = [AR.slot("kvo") for _ in range(4)]
                  oi = [0]
                  for (wv_, sw_, od_, isv) in ((w_k_v, s_wk, ko_d, False), (w_v_v, s_wv, vo_d, True)):
                      for mc in range(2):
                          for h in range(2):
                              bi, bs = bank()
                              pairs = [(mntv[:, k, mc * 128:(mc + 1) * 128], wv_[:, k, h * 512:(h + 1) * 512]) for k in range(8)]
                              mm_group(psf(bi), pairs, [s_mnt[f][mc] for f in range(8)] + [sw_], bs)
                              i = oi[0] % 4
                              oi[0] += 1
                              act(lambda e, bi=bi, i=i: e.activation(out=kvov[:, i, :], in_=psf(bi), func=AF.Copy),
                                  [bs], [s_kvo[i]])
                              if isv:
                                  dve(lambda e, i=i, mc=mc, h=h: e.tensor_copy(out=Vv[:, mc, h * 512:(h + 1) * 512],
                                                                                in_=kvov[:, i, :]),
                                      [s_kvo[i]], [s_V[mc]])
                              out_toks.append(P.dma("sp", lambda e, od_=od_, mc=mc, h=h, i=i: e.dma_start(
                                  out=od_[mc * 128:(mc + 1) * 128, h * 512:(h + 1) * 512], in_=kvov[:, i, :]),
                                  reads=[s_kvo[i]]))
                  AR.free("kvo")
                  AR.free("mnt")
                  AR.free("mem")
                  AR.free("w_k")
                  AR.free("w_v")
                  P.stage = stg

              def sample_tail():
                  stg = P.stage
                  P.stage = "p%d_conv" % ps_i
                  conv_sample2b(0)
                  AR.free("ect")
                  AR.free("ec")
                  AR.free("hsc")
                  kv_stage()
                  wout_units(TS[1])
                  pn.flush()
                  AR.free("gs")
                  AR.free("as")
                  AR.free("cyS")
                  AR.free("cbS")
                  AR.free("cmS")
                  P.stage = stg

              def free_conv():
                  if last_pass:
                      AR.free("gl")
                  AR.free("w_out")

              ckpt(20 * ps_i + 6, "p%d_q" % ps_i)
              if has_s:
                  AR.alloc("qs", D * 2)
                  qsv = AR.view("qs", BF16)
                  s_qs = AR.slot("qs")
              ev = [0]
              for t_i, us in tiles:
                  c0, n = tcols(us)
                  if t_i == 1:
                      pn.flush()
                      if has_s:
                          kv_prep()
                  if us[0] == 8:
                      kv_prep_b()
                      sample_tail()
                  for j in range(8):
                      bi, bs = bank()
                      pairs = [(w_q_v[:, k, j * 128:(j + 1) * 128], Av[:, k, c0:c0 + n]) for k in range(8)]
                      mm_group(psf(bi)[:, 0:n], pairs, [s_A[f][u] for f in range(8) for u in us] + [s_wq], bs)
                      if ev[0] % 2 == 0:
                          act(lambda e, bi=bi, j=j, c0=c0, n=n: e.activation(out=Bv[:, j, c0:c0 + n], in_=psf(bi)[:, 0:n],
                                                                             func=AF.Copy), [bs], [s_B[j][u] for u in us])
                      else:
                          dve(lambda e, bi=bi, j=j, c0=c0, n=n: e.tensor_copy(out=Bv[:, j, c0:c0 + n], in_=psf(bi)[:, 0:n]),
                              [bs], [s_B[j][u] for u in us])
                      ev[0] += 1
                  if us[0] == 8:
                      for h in range(2):
                          bi, bs = bank()
                          pairs = [(Av[:, k, c0:c0 + NS], w_q_v[:, k, h * 512:(h + 1) * 512]) for k in range(8)]
                          mm_group(psf(bi)[0:NS, :], pairs, [s_A[f][8] for f in range(8)] + [s_wq], bs)
                          act(lambda e, bi=bi, h=h: e.activation(out=qsv[0:NS, h * 512:(h + 1) * 512], in_=psf(bi)[0:NS, :],
                                                                 func=AF.Copy), [bs], [s_qs])
              AR.free("w_q")
              free_conv()
              w_o_v, s_wo = load_w("w_o", w_o_d, 8, D)


              samp = {"next": 0}
              if has_s:
                  NR = 4
                  AR.alloc("sel", NS * 128 * 2)
                  selv = AR.view("sel", BF16, "p (b j) -> p b j", b=NS)
                  s_sel = AR.slot("sel")
                  P.dma("pool", lambda e: e.dma_start(out=selv[0:NS], in_=sel_d.rearrange("p (b j) -> p b j", b=NS)),
                        writes=[s_sel])
                  AR.alloc("kvr", NR * 2 * 2 * D * 2)
                  kvrv = AR.view("kvr", BF16, "p (i w c d) -> p i w c d", i=NR, w=2, c=2)
                  s_kvr = [[AR.slot("kvr") for _ in range(2)] for _ in range(NR)]
                  AR.alloc("pz", NS * 2 * 4 * NS * 2)
                  pzv = AR.view("pz", BF16, "p (b c h j) -> p b c h j", b=NS, c=2, h=4)
                  s_pz = [AR.slot("pz") for _ in range(NS)]
                  AR.alloc("sj", 8 * 256 * 2)
                  sjv = AR.view("sj", BF16, "p (i d) -> p i d", i=8)
                  s_sj = [AR.slot("sj") for _ in range(8)]
                  AR.alloc("sc8", 3 * 3 * 8 * 4)
                  sc8 = AR.view("sc8", F32, "p (i a k) -> p i a k", i=3, a=3)
                  s_sc8 = [[AR.slot("sc8") for _ in range(9)] for _ in range(3)]
                  AR.alloc("ez", 4 * 8 * 2)
                  ezv = AR.view("ez", BF16, "p (i k) -> p i k", i=4)
                  s_ez = [AR.slot("ez") for _ in range(4)]
                  P.op("pool", lambda e: e.memset(pzv, 0.0), writes=s_pz)
                  obk = [bank(), bank()]
                  bank_reserved.update([obk[0][0], obk[1][0]])

                  def samp_load(b):
                      i = b % NR
                      P.dma("pool", lambda e: e.dma_start(
                          out=kvrv[:, i, 0, :, :], in_=ck_d[b].rearrange("(c p) d -> p c d", p=128)), writes=[s_kvr[i][0]])
                      P.dma("pool", lambda e: e.dma_start(
                          out=kvrv[:, i, 1, :, :], in_=cv_d[b].rearrange("(c p) d -> p c d", p=128)), writes=[s_kvr[i][1]])

                  for b_ in range(NR):
                      samp_load(b_)
                  pipe = []

                  def s1(b):
                      i = b % NR
                      qb = []
                      for h2 in range(2):
                          bi, bs = bank()
                          mm_group(psf(bi), [(selv[0:NS, b, :], qsv[0:NS, h2 * 512:(h2 + 1) * 512])], [s_sel, s_qs], bs)
                          qb.append((bi, bs))
                      si = b % 3
                      for mc in range(2):
                          for h in range(4):
                              bi, bs = qb[h // 2]
                              dve(lambda e, mc=mc, h=h, bi=bi: e.scalar_tensor_tensor(
                                  out=sjv[:, mc * 4 + h, :], in0=kvrv[:, i, 0, mc, h * 256:(h + 1) * 256], scalar=0.0625,
                                  in1=psf(bi)[:, (h % 2) * 256:(h % 2) * 256 + 256],
                                  op0=ALU.mult, op1=ALU.mult, accum_out=sc8[:, si, 0, mc * 4 + h:mc * 4 + h + 1]),
                                  [s_kvr[i][0], bs], [s_sj[mc * 4 + h], s_sc8[si][mc * 4 + h]], fuse=False)
                      act(lambda e: e.activation(out=ezv[:, si, :], in_=sc8[:, si, 0, :], func=AF.Exp),
                          s_sc8[si][0:8], [s_ez[si]])

                  def s2(b):
                      si = b % 3
                      bi, bs = bank()
                      mm_group(psf(bi)[:, 0:4], [(ones, ezv[:, si, mc * 4:mc * 4 + 4]) for mc in range(2)],
                               [s_ez[si], s_ones], bs)
                      dve(lambda e: e.reciprocal(out=sc8[:, si, 1, 0:4], in_=psf(bi)[:, 0:4]), [bs], [s_sc8[si][8]])
                      dve(lambda e: e.tensor_tensor(
                          out=pzv[:, b, :, :, b], in0=ezv[:, si, :].rearrange("p (c h) -> p c h", c=2),
                          in1=sc8[:, si, 1, 0:4].unsqueeze(1).to_broadcast([128, 2, 4]), op=ALU.mult),
                          [s_ez[si], s_sc8[si][8]], [s_pz[b]])

                  def s3(b):
                      i = b % NR
                      for h in range(4):
                          ob, obs = obk[h // 2]

                          def pvfn(e, h=h, ob=ob):
                              ins = first = None
                              for mc in range(2):
                                  ins = e.matmul(out=psf(ob)[0:NS, (h % 2) * 256:(h % 2) * 256 + 256],
                                                 lhsT=pzv[:, b, mc, h, :], rhs=kvrv[:, i, 1, mc, h * 256:(h + 1) * 256],
                                                 start=(b == 0 and mc == 0 and h % 2 == 0), stop=(b == NS - 1 and mc == 1),
                                                 skip_group_check=True)
                                  if first is None:
                                      first = ins
                              return (first, ins)
                          pvfn._fuse = True
                          P.op("pe", pvfn, reads=[s_pz[b], s_kvr[i][1]], writes=[obs] if (b == 0 and h % 2 == 0) else [])
                          if b > 0 or h % 2 == 1:
                              _merge(obs.w, {"pe": P.eng["pe"].cnt})
                      if b + NR < NS:
                          samp_load(b + NR)

                  def samp_token():
                      if samp["next"] >= NS and not pipe:
                          if not samp.get("fin"):
                              samp["fin"] = True
                              samp_finish()
                          return
                      stg = P.stage
                      P.stage = "p0_samp_attn"
                      for t in list(pipe):
                          if t[1] == 2:
                              s3(t[0])
                              pipe.remove(t)
                      for t in pipe:
                          if t[1] == 1:
                              s2(t[0])
                              t[1] = 2
                      if samp["next"] < NS:
                          b = samp["next"]
                          samp["next"] = b + 1
                          s1(b)
                          pipe.append([b, 1])
                      P.stage = stg

                  def samp_finish():
                      stg0 = P.stage
                      P.stage = "p0_samp_fin"
                      AR.alloc("osb", D * 2)
                      osbv = AR.view("osb", BF16)
                      s_osb = AR.slot("osb")
                      for h2 in range(2):
                          ob, obs = obk[h2]
                          act(lambda e, ob=ob, h2=h2: e.activation(out=osbv[0:NS, h2 * 512:(h2 + 1) * 512],
                                                                   in_=psf(ob)[0:NS, :], func=AF.Copy), [obs], [s_osb])
                      bank_reserved.clear()
                      bi, bs = bank()
                      pb = psb(bi).rearrange("p (f t) -> p f t", f=8)

                      def tfn2(e, pb=pb):
                          ins = None
                          for f in range(8):
                              ins = e.transpose(out=pb[:, f, 0:NS], in_=osbv[0:NS, f * 128:(f + 1) * 128],
                                                identity=idb[0:NS, 0:NS])
                          return ins
                      P.op("pe", tfn2, reads=[s_osb, s_idb], writes=[bs])
                      dve(lambda e, pb=pb: e.tensor_copy(out=otsv, in_=pb[:, :, 0:NS]), [bs], [s_ots])
                      for nm_ in ("osb", "ez", "sc8", "sj", "pz", "kvr", "sel", "qs"):
                          AR.free(nm_)
                      P.stage = stg0
              else:
                  def samp_token():
                      return

              blocks = ([(0, 4), (4, 4), (8, 4), (12, 4), (16, 3), (19, 3)] if has_s
                        else [(0, 6), (6, 6), (12, 5), (17, 5)])
              NBK = len(blocks)
              NJM = max(nj_ for _, nj_ in blocks)

              def load_block(bk):
                  j0, nj = blocks[bk]
                  nm = "gu%d" % (bk % 2)
                  AR.alloc(nm, 2 * 8 * nj * 128 * 2)
                  v = AR.view(nm, BF16, "p (w k n) -> p w k n", w=2, k=8)
                  sg_, su_ = AR.slot(nm), AR.slot(nm)
                  P.dma("pool", lambda e: e.dma_start(out=v[:, 0], in_=w_g_d[:, j0 * 128:(j0 + nj) * 128].rearrange(
                      "(k p) n -> p k n", p=128)), writes=[sg_])
                  P.dma("pool", lambda e: e.dma_start(out=v[:, 1], in_=w_u_d[:, j0 * 128:(j0 + nj) * 128].rearrange(
                      "(k p) n -> p k n", p=128)), writes=[su_])
                  nmd = "wd%d" % (bk % 2)
                  AR.alloc(nmd, nj * D * 2)
                  vd = AR.view(nmd, BF16, "p (j n) -> p j n", j=nj)
                  sd_ = AR.slot(nmd)
                  P.dma("pool", lambda e: e.dma_start(out=vd, in_=w_d_d[j0 * 128:(j0 + nj) * 128, :].rearrange(
                      "(j p) n -> p j n", p=128)), writes=[sd_])
                  return (v, sg_, su_, vd, sd_, nm, nmd)


              pre_blk0 = load_block(0)
              ckpt(20 * ps_i + 7, "p%d_attn" % ps_i)
              AR.alloc("et", 2 * 2 * 512 * 2)
              etv = AR.view("et", BF16, "p (i c t) -> p i c t", i=2, c=2)
              s_et = [[AR.slot("et") for _ in range(2)] for _ in range(2)]
              AR.alloc("rs", 2 * 512 * 4)
              rsv = AR.view("rs", F32, "p (i t) -> p i t", i=2)
              s_rs = [AR.slot("rs") for _ in range(2)]
              seq = [(us, h) for _, us in (T0, T1) for h in range(4)]

              def att_scores(idx):
                  us, h = seq[idx]
                  c0, n = tcols(us)
                  i = idx % 2
                  for mc in range(2):
                      bi, bs = bank()
                      pairs = [(KTv[:, 2 * h + dc, mc * 128:(mc + 1) * 128], Bv[:, 2 * h + dc, c0:c0 + n]) for dc in range(2)]
                      mm_group(psf(bi)[:, 0:n], pairs,
                               [s_KT[2 * h], s_KT[2 * h + 1]] + [s_B[2 * h + dc][u] for dc in range(2) for u in us], bs)
                      act(lambda e, bi=bi, i=i, mc=mc, n=n: e.activation(out=etv[:, i, mc, 0:n], in_=psf(bi)[:, 0:n],
                                                                         func=AF.Exp, scale=0.0625), [bs], [s_et[i][mc]])

              def att_pv(idx):
                  us, h = seq[idx]
                  c0, n = tcols(us)
                  i = idx % 2
                  bi, bs = bank()
                  mm_group(psf(bi)[:, 0:n], [(ones, etv[:, i, mc, 0:n]) for mc in range(2)],
                           [s_et[i][0], s_et[i][1], s_ones], bs)
                  act(lambda e, bi=bi: e.activation(out=rsv[:, i, 0:n], in_=psf(bi)[:, 0:n], func=AF.Ln), [bs], [s_rs[i]])
                  act(lambda e: e.activation(out=rsv[:, i, 0:n], in_=rsv[:, i, 0:n], func=AF.Exp, scale=-1.0),
                      [s_rs[i]], [s_rs[i]])
                  for dc in range(2):
                      bi, bs = bank()
                      pairs = [(Vv[:, mc, h * 256 + dc * 128:h * 256 + (dc + 1) * 128], etv[:, i, mc, 0:n]) for mc in range(2)]
                      mm_group(psf(bi)[:, 0:n], pairs, [s_V[0], s_V[1], s_et[i][0], s_et[i][1]], bs)
                      dve(lambda e, bi=bi, dc=dc: e.tensor_tensor(
                          out=Av[:, 2 * h + dc, c0:c0 + n], in0=psf(bi)[:, 0:n], in1=rsv[:, i, 0:n], op=ALU.mult),
                          [bs, s_rs[i]], [s_A[2 * h + dc][u] for u in us])

              att_scores(0)
              for idx in range(len(seq)):
                  if idx + 1 < len(seq):
                      att_scores(idx + 1)
                  att_pv(idx)
              AR.free("rs")
              AR.free("et")

              ckpt(20 * ps_i + 9, "p%d_wo_norm3" % ps_i)
              pn3 = ProjNorm(C_GFFN, Bv, s_B, hsv, s_hs)
              for u in range(8):
                  lhs, rd = lhs_cols(Av, s_A, 8, u)
                  pn3.unit(u, lhs, rd, w_o_v, s_wo)
              if not has_s:
                  pn3.unit(8, [otsv[:, f, :] for f in range(8)], [s_ots], w_o_v, s_wo)
              AR.free("w_o")
              pn3_pending = [not has_s]
              if has_s:
                  pn3.flush()
                  AR.free("hs")

              ckpt(20 * ps_i + 10, "p%d_ffn" % ps_i)
              ftiles = [T0, T1] + ([] if has_s else [TS])
              funits = [u for _, us in ftiles for u in us]
              AR.alloc("inter", NJM * 1040 * 2)
              intv = AR.view("inter", BF16, "p (j t) -> p j t", j=NJM)
              s_int = [[AR.slot("inter") for _ in range(9)] for _ in range(NJM)]
              AR.alloc("fsg", 2 * 512 * 4)
              fsgv = AR.view("fsg", F32, "p (i t) -> p i t", i=2)
              s_fsg = [AR.slot("fsg") for _ in range(2)]
              NYO = 2 if ps_i == 1 else 1
              AR.alloc("yo", NYO * D * 4)
              yov = AR.view("yo", F32, "p (i d) -> p i d", i=NYO)
              s_yo = [AR.slot("yo") for _ in range(NYO)]
              fcnt = [0]
              fi = [0]
              it_cnt = [0]

              class PreNorm:
                  def __init__(self):
                      AR.alloc("hs", 4 * D * 2)
                      self.hsv = AR.view("hs", BF16, "p (i d) -> p i d", i=4)
                      self.s_hs = [AR.slot("hs") for _ in range(4)]
                      prefetch["hs"] = (self.hsv, self.s_hs)
                      prefetch["norm1"] = self.flush
                      self.pa, self.pb = [], []

                  def unit(self, u):
                      self.pa.append((u, norm_A1(xu(u), s_X[u], urows(u))))
                      if len(self.pa) > 1:
                          self._a2()
                      if len(self.pb) > 2:
                          self._b()

                  def _a2(self):
                      uu, a1 = self.pa.pop(0)
                      self.pb.append((uu,) + norm_A2(a1, self.hsv, self.s_hs))

                  def _b(self):
                      uu, hs_, sh_ = self.pb.pop(0)
                      c0_, nn_ = ucols(uu)
                      norm_B(hs_, sh_, urows(uu), C_GMIX, Av[:, :, c0_:c0_ + nn_], [s_A[f][uu] for f in range(8)])

                  def flush(self):
                      self.unit(6)
                      self.unit(7)
                      while self.pa:
                          self._a2()
                      while self.pb:
                          self._b()

              pre = None
              cur = pre_blk0
              for bk in range(NBK):
                  j0, nj = blocks[bk]
                  nxt = load_block(bk + 1) if bk + 1 < NBK else None
                  v, sg_, su_, vd, sd_, nm, nmd = cur
                  for t_i, us in ftiles:
                      c0, n = tcols(us)
                      rdB = [s_B[f][u] for f in range(8) for u in us]
                      for j in range(nj):
                          bg, bsg = bank()
                          mm_group(psf(bg)[:, 0:n], [(v[:, 0, k, j * 128:(j + 1) * 128], Bv[:, k, c0:c0 + n]) for k in range(8)],
                                   rdB + [sg_], bsg)
                          bu, bsu = bank()
                          mm_group(psf(bu)[:, 0:n], [(v[:, 1, k, j * 128:(j + 1) * 128], Bv[:, k, c0:c0 + n]) for k in range(8)],
                                   rdB + [su_], bsu)
                          i = fi[0] % 2
                          fi[0] += 1
                          act(lambda e, bg=bg, i=i, n=n: e.activation(out=fsgv[:, i, 0:n], in_=psf(bg)[:, 0:n], func=AF.Silu),
                              [bsg], [s_fsg[i]])
                          dve(lambda e, bu=bu, i=i, j=j, c0=c0, n=n: e.tensor_tensor(
                              out=intv[:, j, c0:c0 + n], in0=psf(bu)[:, 0:n], in1=fsgv[:, i, 0:n], op=ALU.mult),
                              [bsu, s_fsg[i]], [s_int[j][u] for u in us])
                          it_cnt[0] += 1
                          if it_cnt[0] % 2 == 0:
                              samp_token()
                      if pn3_pending[0]:
                          stg_ = P.stage
                          P.stage = "p%d_wo_norm3" % ps_i
                          pn3.flush()
                          AR.free("hs")
                          P.stage = stg_
                          pn3_pending[0] = False
                  if ps_i == 0 and "w_in" not in prefetch and bk >= 2 and samp.get("fin", not has_s):
                      prefetch["w_in"] = load_w_in()
                  if ps_i == 0 and bk == NBK - 1:
                      pre = PreNorm()
                  pend_f = []

                  def finish_unit(uu, f1):
                      P.stage = "p%d_final" % ps_i
                      final_F2(uu, f1, tok0, yov[:, fcnt[0] % NYO, :], s_yo[fcnt[0] % NYO])
                      fcnt[0] += 1
                      if ps_i == 0 and uu < 8:
                          load_x_unit(1, uu)
                          P.stage = "p1_Ia"
                          if uu >= 2:
                              pre.unit(uu - 2)
                      P.stage = "p%d_ffn" % ps_i

                  for u in funits:
                      lhs, rd = lhs_cols(intv, s_int, nj, u)
                      proj_back(u, lhs, rd, vd, sd_)
                      if bk == NBK - 1:
                          P.stage = "p%d_final" % ps_i
                          pend_f.append((u, final_F1(u)))
                          P.stage = "p%d_ffn" % ps_i
                          if len(pend_f) > 1:
                              finish_unit(*pend_f.pop(0))
                  while pend_f:
                      finish_unit(*pend_f.pop(0))
                  AR.free(nm)
                  AR.free(nmd)
                  cur = nxt
              if has_s:
                  while not samp.get("fin"):
                      samp_token()
              AR.free("yo")
              AR.free("fsg")
              AR.free("inter")

          one_pass(0)
          one_pass(1)

        try:
            body()
        except _Stop:
            pass

        fin = {}
        for t in out_toks:
            _merge(fin, t)
        _merge(fin, P.dma_tot)
        E = P.eng["sp"]
        E.ops.append(([(k, v) for k, v in fin.items()], None, None, 'fin'))

        def run(e, name):
            for waits, fn, inc, stg in P.eng[name].ops:
                fw = None
                if FUSE_WAIT and fn is not None and waits and getattr(fn, "_fuse", False):
                    fw = waits[-1]
                    waits = waits[:-1]
                for k, v in waits:
                    e.wait_ge(sems[k], v)
                if fn is None:
                    continue
                if TAG:
                    with nc.named_scope(stg):
                        ins = fn(e)
                else:
                    ins = fn(e)
                first, last = ins if isinstance(ins, tuple) else (ins, ins)
                if fw is not None:
                    first._wait_ge(sems[fw[0]], fw[1])
                last.then_inc(sems[inc[0]], inc[1])

        with nc.Block() as block:
            @block.tensor
            def _(e):
                run(e, "pe")

            @block.scalar
            def _(e):
                run(e, "act")

            @block.vector
            def _(e):
                run(e, "dve")

            @block.gpsimd
            def _(e):
                run(e, "pool")

            @block.sync
            def _(e):
                run(e, "sp")
    return nc


_NC_CACHE = {}


def _consts():
    c = np.zeros((128, NCON), np.float32)
    for g, w in enumerate(POOL_W):
        for t in range(16):
            c[:, C_IC + 16 * g + t] = 1.0 / min(w, t + 1)
    c[:, C_EPS] = EPS
    c[:, C_ID:C_ID + 128] = np.eye(128, dtype=np.float32)
    return c


def _prep(x_prompt, x_sample, mem_prompt, state_pool, state_conv, cache_mem_k, cache_mem_v,
          g_mix, w_in, pool_map_w, pool_map_b, pool_scale, conv_dw_w, conv_dw_b, conv_ln_g, conv_ln_b,
          w_out, g_attn, g_mem, w_q, w_k, w_v, w_o, g_ffn, w_gate, w_up, w_down, g_final, cores=None):
    f = lambda a: np.ascontiguousarray(np.asarray(a, dtype=np.float32))
    con = _consts()

    def fm(v, n):
        return f(v).reshape(n, 128).T
    con[:, C_GMIX:C_GMIX + 8] = fm(g_mix[0], 8)
    con[:, C_GATT:C_GATT + 8] = fm(g_attn[0], 8)
    con[:, C_GFFN:C_GFFN + 8] = fm(g_ffn[0], 8)
    con[:, C_GMEM:C_GMEM + 8] = fm(g_mem[0], 8)
    con[:, C_PB:C_PB + 4] = fm(pool_map_b[0], 4)
    con[:, C_PS:C_PS + 4] = fm(pool_scale[0], 4)
    con[:, C_CB:C_CB + 4] = fm(conv_dw_b[0], 4)
    con[:, C_LG:C_LG + 4] = fm(conv_ln_g[0], 4)
    con[:, C_LB:C_LB + 4] = fm(conv_ln_b[0], 4)
    cw = f(conv_dw_w[0])
    for c in range(4):
        con[:, C_CW + c * CONV_K:C_CW + (c + 1) * CONV_K] = cw[:, c * 128:(c + 1) * 128].T
    gfin = np.ascontiguousarray(np.broadcast_to(f(g_final)[None, :], (128, D)))
    sel = np.zeros((NS, NS, 128), np.float32)
    for b in range(NS):
        sel[b, b, :] = 1.0
    sel = sel.reshape(NS, NS * 128)

    shared = {
        "w_in": f(w_in[0]), "wmap": f(pool_map_w[0]), "w_out": f(w_out[0]), "w_q": f(w_q[0]), "w_k": f(w_k[0]),
        "w_v": f(w_v[0]), "w_o": f(w_o[0]), "w_gate": f(w_gate[0]), "w_up": f(w_up[0]), "w_down": f(w_down[0]),
        "consts": con, "gfin": gfin, "sel": sel,
    }
    in_maps = []
    for c in (range(NCORES) if cores is None else cores):
        sl = slice(NS * c, NS * (c + 1))
        m = dict(shared)
        m["x"] = f(x_prompt[c])
        m["xs"] = f(x_sample[sl, 0])
        m["mem"] = f(mem_prompt[c])
        m["sp"] = f(state_pool[0, sl])
        m["sc"] = f(state_conv[0, sl])
        m["ck"] = f(cache_mem_k[0, sl]).reshape(NS, NMEM, D)
        m["cv"] = f(cache_mem_v[0, sl]).reshape(NS, NMEM, D)
        in_maps.append(m)
    return in_maps


def kernel(**inputs):
    if "nc" not in _NC_CACHE:
        _NC_CACHE["nc"] = build_program()
    nc = _NC_CACHE["nc"]
    in_maps = _prep(**inputs)
    res = run_bass_kernel_spmd(nc, in_maps, core_ids=list(range(NCORES)))
    r = res.results
    y_prompt = np.stack([r[c]["y"] for c in range(NCORES)], 0)
    y_sample = np.concatenate([r[c]["ys"] for c in range(NCORES)], 0)[:, None, :]
    pool_p = np.stack([r[c]["pool_p"] for c in range(NCORES)], 0)[None]
    pool_s = np.concatenate([r[c]["pool_s"] for c in range(NCORES)], 0)[None]
    conv_p = np.stack([r[c]["conv_p"] for c in range(NCORES)], 0)[None]
    conv_s = np.concatenate([r[c]["conv_s"] for c in range(NCORES)], 0)[None]
    k_p = np.stack([r[c]["k_out"] for c in range(NCORES)], 0).reshape(1, NCORES, NMEM, 4, 256)
    v_p = np.stack([r[c]["v_out"] for c in range(NCORES)], 0).reshape(1, NCORES, NMEM, 4, 256)
    return (y_prompt.astype(np.float32), y_sample.astype(np.float32), pool_p.astype(np.float32),
            pool_s.astype(np.float32), conv_p.astype(np.float32), conv_s.astype(np.float32),
            k_p.astype(np.float32), v_p.astype(np.float32))
```

```python
from contextlib import ExitStack
import numpy as np
import concourse.bass as bass
import concourse.mybir as mybir
from concourse.bass_utils import run_bass_kernel_spmd

F32 = mybir.dt.float32
BF16 = mybir.dt.bfloat16
AF = mybir.ActivationFunctionType
ALU = mybir.AluOpType
AX = mybir.AxisListType

D = 1024
SEQ = 2048
NS = 16
NMEM = 256
DFF = 2816
EPS = 1e-6
POOL_W = (2, 4, 8, 16)
CONV_K = 31
NCON = 384
NCORES = 8
TAG = False
FUSE_WAIT = True
REWAIT_DMA = False
STRICT = True

C_GMIX, C_GATT, C_GFFN, C_GMEM = 0, 8, 16, 24
C_PB, C_PS, C_CB, C_LG, C_LB = 32, 36, 40, 44, 48
C_CW = 52
C_IC = 176
C_EPS = 240
C_ID = 256


class Slot:
    __slots__ = ("w", "r", "excl")

    def __init__(self, inherit=None, excl=False):
        self.w = {}
        self.r = dict(inherit) if inherit else {}
        self.excl = excl


def _merge(d, toks):
    for k, v in toks.items():
        if d.get(k, 0) < v:
            d[k] = v


class Eng:
    def __init__(self, name):
        self.name = name
        self.ops = []
        self.cnt = 0
        self.seen = {}


class Prog:
    NDMA = {"sp": 24, "pool": 32}

    def __init__(self):
        self.eng = {n: Eng(n) for n in ("pe", "act", "dve", "pool", "sp")}
        self.dma_tot = {}
        self.dma_rr = {"sp": 0, "pool": 0}
        self.stage = "init"

    def _deps(self, E, reads, writes, extra):
        deps = {}
        for s in reads:
            _merge(deps, s.w)
            if s.excl:
                for k, v in s.r.items():
                    if k != E.name and deps.get(k, 0) < v:
                        deps[k] = v
        strict = STRICT and E.name != "pe"
        for s in writes:
            for k, v in s.w.items():
                if (strict or k != E.name) and deps.get(k, 0) < v:
                    deps[k] = v
            for k, v in s.r.items():
                if (strict or k != E.name) and deps.get(k, 0) < v:
                    deps[k] = v
        for t in extra:
            _merge(deps, t)
        waits = []
        for k, v in deps.items():
            if E.seen.get(k, 0) < v or (REWAIT_DMA and k.startswith("d_")):
                E.seen[k] = max(v, E.seen.get(k, 0))
                waits.append((k, v))
        return waits

    def op(self, eng, fn, reads=(), writes=(), extra=()):
        E = self.eng[eng]
        waits = self._deps(E, reads, writes, extra)
        E.cnt += 1
        tok = {E.name: E.cnt}
        E.ops.append((waits, fn, (E.name, 1), self.stage))
        for s in reads:
            _merge(s.r, tok)
        for s in writes:
            s.w = dict(tok)
            s.r = {}
        return tok

    def dma(self, eng, fn, reads=(), writes=(), extra=()):
        E = self.eng[eng]
        j = self.dma_rr[eng]
        self.dma_rr[eng] = (j + 1) % self.NDMA[eng]
        key = "d_%s_%d" % (eng, j)
        prev = self.dma_tot.get(key, 0)
        ex = list(extra)
        if prev:
            ex.append({key: prev})
        waits = self._deps(E, reads, writes, ex)
        tot = prev + 16
        self.dma_tot[key] = tot
        tok = {key: tot}
        E.ops.append((waits, fn, (key, 16), self.stage))
        for s in reads:
            _merge(s.r, tok)
        for s in writes:
            s.w = dict(tok)
            s.r = {}
        return tok

    def sem_keys(self):
        keys = list(self.eng.keys())
        for eng in ("sp", "pool"):
            keys += ["d_%s_%d" % (eng, j) for j in range(self.NDMA[eng])]
        return keys


class Arena:
    def __init__(self, ap, nbytes):
        self.ap = ap
        self.nbytes = nbytes
        self.live = {}
        self.dead = []
        self.peak = 0

    def alloc(self, name, nbytes):
        assert name not in self.live, name
        req = nbytes
        nbytes = (nbytes + 63) // 64 * 64
        spans = sorted((r["off"], r["size"]) for r in self.live.values())
        pos = 0
        off = None
        for o, sz in spans:
            if o - pos >= nbytes:
                off = pos
                break
            pos = max(pos, o + sz)
        if off is None:
            if self.nbytes - pos >= nbytes:
                off = pos
            else:
                raise RuntimeError("arena full allocating %s (%d B); live=%s" % (
                    name, nbytes, {k: (v["off"], v["size"]) for k, v in self.live.items()}))
        inherit = {}
        for (o, sz, toks) in self.dead:
            if o < off + nbytes and off < o + sz:
                _merge(inherit, toks)
        self.dead = [(o, sz, t) for (o, sz, t) in self.dead if not (o >= off and o + sz <= off + nbytes)]
        self.live[name] = dict(off=off, size=nbytes, req=req, slots=[], inherit=inherit)
        self.peak = max(self.peak, off + nbytes)
        return name

    def view(self, name, dtype, pattern=None, **kw):
        r = self.live[name]
        off, nb = r["off"], r["req"]
        a = self.ap[:, off // 2:(off + nb) // 2]
        if dtype != BF16:
            a = a.bitcast(dtype)
        if pattern:
            a = a.rearrange(pattern, **kw)
        return a

    def slot(self, name):
        r = self.live[name]
        sl = Slot(r["inherit"])
        r["slots"].append(sl)
        return sl

    def free(self, name):
        r = self.live.pop(name)
        toks = dict(r["inherit"])
        for sl in r["slots"]:
            _merge(toks, sl.w)
            _merge(toks, sl.r)
        self.dead.append((r["off"], r["size"], toks))


class _Stop(Exception):
    pass


def build_program(limit=None):
    nc = bass.Bass("TRN2", target_bir_lowering=False)
    P = Prog()

    def ckpt(n, name=None):
        if name is not None:
            P.stage = name
        if limit is not None and n >= limit:
            raise _Stop()

    def din(name, shape):
        return nc.dram_tensor(name, list(shape), F32, kind="ExternalInput").ap()

    def dout(name, shape):
        return nc.dram_tensor(name, list(shape), F32, kind="ExternalOutput").ap()

    x_d = din("x", [SEQ, D])
    xs_d = din("xs", [NS, D])
    mem_d = din("mem", [NMEM, D])
    sp_d = din("sp", [NS, 15, 512])
    sc_d = din("sc", [NS, 30, 512])
    ck_d = din("ck", [NS, NMEM, D])
    cv_d = din("cv", [NS, NMEM, D])
    w_in_d = din("w_in", [D, 1536])
    wmap_d = din("wmap", [4, 128, 128])
    w_out_d = din("w_out", [D, D])
    w_q_d = din("w_q", [D, D])
    w_k_d = din("w_k", [D, D])
    w_v_d = din("w_v", [D, D])
    w_o_d = din("w_o", [D, D])
    w_g_d = din("w_gate", [D, DFF])
    w_u_d = din("w_up", [D, DFF])
    w_d_d = din("w_down", [DFF, D])
    con_d = din("consts", [128, NCON])
    gf_d = din("gfin", [128, D])
    sel_d = din("sel", [NS, NS * 128])

    y_d = dout("y", [SEQ, D])
    ys_d = dout("ys", [NS, D])
    pp_d = dout("pool_p", [15, 512])
    ps_d = dout("pool_s", [NS, 15, 512])
    cp_d = dout("conv_p", [30, 512])
    cs_d = dout("conv_s", [NS, 30, 512])
    ko_d = dout("k_out", [NMEM, D])
    vo_d = dout("v_out", [NMEM, D])

    ARENA_BYTES = 212736
    es = ExitStack()
    with es:
        arena_t = es.enter_context(nc.sbuf_tensor("arena", [128, ARENA_BYTES // 2], BF16))
        ps_t = es.enter_context(nc.psum_tensor("ps", [128, 8, 512], F32))
        sems = {k: es.enter_context(nc.semaphore(k)) for k in P.sem_keys()}
        AR = Arena(arena_t[:], ARENA_BYTES)

        bank_slots = [Slot(excl=True) for _ in range(8)]
        bank_rr = [0]
        bank_reserved = set()

        def bank():
            while True:
                i = bank_rr[0]
                bank_rr[0] = (i + 1) % 8
                if i not in bank_reserved:
                    return i, bank_slots[i]

        def psf(i):
            return ps_t[:, i, :]

        def psb(i):
            return ps_t[:, i, :].bitcast(BF16)

        def mm_group(out_ap, pairs, reads, bslot):
            def fn(e):
                n = len(pairs)
                ins = first = None
                for i, (l, r) in enumerate(pairs):
                    ins = e.matmul(out=out_ap, lhsT=l, rhs=r, start=(i == 0), stop=(i == n - 1))
                    if first is None:
                        first = ins
                return (first, ins)
            fn._fuse = True
            return P.op("pe", fn, reads=reads, writes=[bslot])

        class _F:
            def __init__(self, f, fuse):
                self.f = f
                self._fuse = fuse

            def __call__(self, e):
                return self.f(e)

        def act(fn, reads, writes, extra=(), fuse=True):
            return P.op("act", _F(fn, fuse), reads=reads, writes=writes, extra=extra)

        def dve(fn, reads, writes, extra=(), fuse=True):
            return P.op("dve", _F(fn, fuse), reads=reads, writes=writes, extra=extra)

        AR.alloc("con", NCON * 4)
        AR.alloc("gf", D * 4)
        AR.alloc("idb", 128 * 2)
        AR.alloc("ones", 128 * 2)
        AR.alloc("wmap", 4 * 128 * 2)
        AR.alloc("X", 8 * D * 4)
        AR.alloc("XS", D * 4)
        AR.alloc("A", 8 * 1040 * 2)
        AR.alloc("B", 8 * 1040 * 2)
        AR.alloc("KT", 8 * 256 * 2)
        AR.alloc("V", 2 * D * 2)
        AR.alloc("gth", 4 * 32 * 2)
        AR.alloc("ath", 4 * 16 * 4)
        AR.alloc("st", 64 * 4)

        con = AR.view("con", F32)
        gfv = AR.view("gf", F32)
        idb = AR.view("idb", BF16)
        idf = con[:, C_ID:C_ID + 128]
        ones = AR.view("ones", BF16)
        wmapv = AR.view("wmap", BF16, "p (g d) -> p g d", g=4)
        Xv = AR.view("X", F32, "p (c d) -> p c d", c=8)
        XSv = AR.view("XS", F32)
        Av = AR.view("A", BF16, "p (f t) -> p f t", f=8)
        Bv = AR.view("B", BF16, "p (f t) -> p f t", f=8)
        KTv = AR.view("KT", BF16, "p (f m) -> p f m", f=8)
        Vv = AR.view("V", BF16, "p (c d) -> p c d", c=2)
        gthv = AR.view("gth", BF16, "p (c t) -> p c t", c=4)
        athv = AR.view("ath", F32, "p (c t) -> p c t", c=4)
        stv = AR.view("st", F32)

        s_con = AR.slot("con")
        s_gf = AR.slot("gf")
        s_idb = AR.slot("idb")
        s_ones = AR.slot("ones")
        s_wmap = AR.slot("wmap")
        s_X = [AR.slot("X") for _ in range(8)] + [AR.slot("XS")]
        s_A = [[AR.slot("A") for _ in range(9)] for _ in range(8)]
        s_B = [[AR.slot("B") for _ in range(9)] for _ in range(8)]
        s_KT = [AR.slot("KT") for _ in range(8)]
        s_V = [AR.slot("V") for _ in range(2)]
        s_gth = AR.slot("gth")
        s_ath = AR.slot("ath")

        def urows(u):
            return 128 if u < 8 else NS

        def ucols(u):
            return (128 * u, urows(u))

        def xu(u):
            return Xv[:, u, :] if u < 8 else XSv[0:NS, :]

        prefetch = {}
        P.dma("sp", lambda e: e.dma_start(out=con, in_=con_d), writes=[s_con])
        P.dma("sp", lambda e: e.dma_start(out=gfv, in_=gf_d), writes=[s_gf])
        P.dma("pool", lambda e: e.dma_start(out=wmapv, in_=wmap_d.rearrange("g c d -> c g d")), writes=[s_wmap])
        dve(lambda e: e.tensor_copy(out=idb, in_=idf), [s_con], [s_idb])
        P.op("pool", lambda e: e.memset(ones, 1.0), writes=[s_ones])
        P.op("pool", lambda e: e.memset(gthv, 0.0), writes=[s_gth])
        P.op("pool", lambda e: e.memset(athv, 0.0), writes=[s_ath])

        epsc = con[:, C_EPS:C_EPS + 1]

        def load_w_in():
            AR.alloc("w_in", 8 * 1536 * 2)
            v = AR.view("w_in", BF16, "p (k n) -> p k n", k=8)
            sl = [AR.slot("w_in") for _ in range(3)]
            src = w_in_d.rearrange("(k p) n -> p k n", p=128)
            for q in range(3):
                P.dma("pool", lambda e, q=q: e.dma_start(out=v[:, :, q * 512:(q + 1) * 512],
                                                         in_=src[:, :, q * 512:(q + 1) * 512]), writes=[sl[q]])
            return v, sl

        def load_w(name, dram_ap, kchunks, ncols):
            AR.alloc(name, kchunks * ncols * 2)
            v = AR.view(name, BF16, "p (k n) -> p k n", k=kchunks)
            s = AR.slot(name)
            P.dma("pool", lambda e: e.dma_start(out=v, in_=dram_ap.rearrange("(k p) n -> p k n", p=128)),
                  writes=[s])
            return v, s

        hs_ring = {"i": 0, "h": 0}
        s_stat = [AR.slot("st") for _ in range(8)]
        AR.alloc("sqj", D * 2)
        s_sqj = AR.slot("sqj")
        sqjv = AR.view("sqj", BF16)
        AR.alloc("ots", 8 * NS * 2)
        otsv = AR.view("ots", BF16, "p (f b) -> p f b", f=8)
        s_ots = AR.slot("ots")

        def stats_rstd(xin, s_in, rows):
            j_ = hs_ring["i"] % 8
            hs_ring["i"] += 1
            i = j_
            st = stv[0:rows, 4 * j_:4 * j_ + 4]
            s_st = s_stat[j_]
            act(lambda e: e.activation(out=sqjv[0:rows, :], in_=xin, func=AF.Square, accum_out=st[:, 0:1]),
                [s_in], [s_sqj, s_st], fuse=False)
            act(lambda e: e.activation(out=st[:, 1:2], in_=st[:, 0:1], func=AF.Ln, scale=1.0 / D,
                                       bias=epsc[0:rows]), [s_st, s_con], [s_st])
            act(lambda e: e.activation(out=st[:, 2:3], in_=st[:, 1:2], func=AF.Exp, scale=-0.5), [s_st], [s_st])
            return i, st, s_st

        def norm_A1(xin, s_in, rows):
            return (xin, s_in, rows) + stats_rstd(xin, s_in, rows)

        def norm_A2(a1, hsv, s_hs):
            xin, s_in, rows, _, st, s_st = a1
            i = hs_ring["h"] % 4
            hs_ring["h"] += 1
            hs = hsv[0:rows, i, :]
            act(lambda e: e.activation(out=hs, in_=xin, func=AF.Copy, scale=st[:, 2:3]),
                [s_in, s_st], [s_hs[i]])
            return hs, s_hs[i]

        def norm_A(xin, s_in, rows, hsv, s_hs):
            return norm_A2(norm_A1(xin, s_in, rows), hsv, s_hs)

        def norm_B(hs, s_h, rows, gcol, dst3, dst_slots):
            bi, bs = bank()
            pb = psb(bi).rearrange("p (f t) -> p f t", f=8)

            def tfn(e):
                ins = first = None
                for f in range(8):
                    ins = e.transpose(out=pb[:, f, 0:rows], in_=hs[:, f * 128:(f + 1) * 128],
                                      identity=idb[0:rows, 0:rows])
                    if first is None:
                        first = ins
                return (first, ins)
            tfn._fuse = True
            P.op("pe", tfn, reads=[s_h, s_idb], writes=[bs])
            g = con[:, gcol:gcol + 8]
            dve(lambda e: e.tensor_tensor(out=dst3, in0=pb[:, :, 0:rows],
                                          in1=g.unsqueeze(2).to_broadcast([128, 8, rows]), op=ALU.mult),
                [bs, s_con], dst_slots)

        def norm_unit(u, gcol, dstv, s_dst, hsv, s_hs):
            rows = urows(u)
            c0, n = ucols(u)
            hs, s_h = norm_A(xu(u), s_X[u], rows, hsv, s_hs)
            norm_B(hs, s_h, rows, gcol, dstv[:, :, c0:c0 + n], [s_dst[f][u] for f in range(8)])

        def proj_back(u, lhs, rd, wv, s_w):
            rows = urows(u)
            for h in range(2):
                bi, bs = bank()
                pairs = [(l, wv[:, k, h * 512:(h + 1) * 512]) for k, l in enumerate(lhs)]
                mm_group(psf(bi)[0:rows, :], pairs, rd + [s_w], bs)
                xs_ = xu(u)[:, h * 512:(h + 1) * 512]
                pv = psf(bi)[0:rows, :]
                dve(lambda e, xs_=xs_, pv=pv: e.tensor_tensor(out=xs_, in0=xs_, in1=pv, op=ALU.add),
                    [bs, s_X[u]], [s_X[u]])

        def lhs_cols(srcv, s_src, nf, u):
            c0, n = ucols(u)
            return [srcv[:, f, c0:c0 + n] for f in range(nf)], [s_src[f][u] for f in range(nf)]

        class ProjNorm:
            def __init__(self, gcol, dstv, s_dst, hsv, s_hs, lag=2):
                self.pa = []
                self.pb = []
                self.a = (gcol, dstv, s_dst, hsv, s_hs)
                self.lag = lag

            def unit(self, u, lhs, rd, wv, s_w):
                gcol, dstv, s_dst, hsv, s_hs = self.a
                proj_back(u, lhs, rd, wv, s_w)
                self.pa.append((u, norm_A1(xu(u), s_X[u], urows(u))))
                if len(self.pa) > 1:
                    self._a2()
                if len(self.pb) > self.lag:
                    self._b()

            def _a2(self):
                gcol, dstv, s_dst, hsv, s_hs = self.a
                u, a1 = self.pa.pop(0)
                hs, s_h = norm_A2(a1, hsv, s_hs)
                self.pb.append((u, hs, s_h))

            def _b(self):
                gcol, dstv, s_dst, hsv, s_hs = self.a
                u, hs, s_h = self.pb.pop(0)
                c0, n = ucols(u)
                norm_B(hs, s_h, urows(u), gcol, dstv[:, :, c0:c0 + n], [s_dst[f][u] for f in range(8)])

            def flush(self):
                while self.pa:
                    self._a2()
                while self.pb:
                    self._b()

        def tcols(units):
            c0 = 128 * units[0]
            n = sum(urows(u) for u in units)
            return c0, n

        out_toks = []

        def load_x_unit(ps_i, u):
            r0 = 1024 * ps_i + 128 * u
            P.dma("sp", lambda e: e.dma_start(out=Xv[:, u, :], in_=x_d[r0:r0 + 128, :]), writes=[s_X[u]])

        def final_F1(u):
            return stats_rstd(xu(u), s_X[u], urows(u))

        def final_F2(u, f1, tok0, yov, s_yo):
            rows = urows(u)
            xin = xu(u)
            i, st, s_st = f1
            yo = yov[0:rows, :]
            dve(lambda e: e.scalar_tensor_tensor(out=yo, in0=xin, scalar=st[:, 2:3], in1=gfv[0:rows, :],
                                                 op0=ALU.mult, op1=ALU.mult), [s_X[u], s_st, s_gf], [s_yo])
            if u < 8:
                r0 = tok0 + 128 * u
                out_toks.append(P.dma("sp", lambda e: e.dma_start(out=y_d[r0:r0 + 128, :], in_=yo), reads=[s_yo]))
            else:
                out_toks.append(P.dma("sp", lambda e: e.dma_start(out=ys_d, in_=yo), reads=[s_yo]))

        T0, T1, TS = (0, [0, 1, 2, 3]), (1, [4, 5, 6, 7]), (2, [8])

        def body():
          P.stage = "load"
          prefetch["w_in"] = load_w_in()
          for u in range(8):
              load_x_unit(0, u)
          P.dma("sp", lambda e: e.dma_start(out=XSv[0:NS, :], in_=xs_d), writes=[s_X[8]])

          def one_pass(ps_i):
              tok0 = 1024 * ps_i
              has_s = ps_i == 0
              last_pass = ps_i == 1
              tiles = [T0, T1] + ([TS] if has_s else [])
              units = [u for _, us in tiles for u in us]

              ckpt(20 * ps_i + 1, "p%d_Ia" % ps_i)
              if has_s:
                  AR.alloc("hst", 4 * 512 * 4)
                  hstv = AR.view("hst", F32, "p (r c) -> p r c", r=4)
                  s_hst = [AR.slot("hst") for _ in range(4)]
                  spf = sp_d.rearrange("b t c -> (b t) c")
                  for r in range(2):
                      P.dma("sp", lambda e, r=r: e.dma_start(out=hstv[0:120, r, :], in_=spf[120 * r:120 * r + 120, :]),
                            writes=[s_hst[r]])
                  AR.alloc("hsc", 4 * 512 * 4)
                  hscv = AR.view("hsc", F32, "p (r c) -> p r c", r=4)
                  s_hsc = [AR.slot("hsc") for _ in range(4)]
                  scf = sc_d.rearrange("b t c -> (b t) c")
                  for r in range(4):
                      P.dma("sp", lambda e, r=r: e.dma_start(out=hscv[0:120, r, :], in_=scf[120 * r:120 * r + 120, :]),
                            writes=[s_hsc[r]])
              if "w_in" in prefetch:
                  w_in_v, s_win3 = prefetch.pop("w_in")
              else:
                  w_in_v, s_win3 = load_w_in()
              AR.alloc("diag", 4 * CONV_K * 128 * 2)
              diagv = AR.view("diag", BF16, "p (c k j) -> p c k j", c=4, k=CONV_K)
              s_diagc = [AR.slot("diag") for _ in range(4)]

              def build_diag(c):
                  dve(lambda e: e.tensor_tensor(
                      out=diagv[:, c, :, :],
                      in0=idf.unsqueeze(1).to_broadcast([128, CONV_K, 128]),
                      in1=con[:, C_CW + c * CONV_K:C_CW + (c + 1) * CONV_K].unsqueeze(2).to_broadcast([128, CONV_K, 128]),
                      op=ALU.mult), [s_con], [s_diagc[c]])
              if "hs" in prefetch:
                  hsv, s_hs = prefetch.pop("hs")
              else:
                  AR.alloc("hs", 4 * D * 2)
                  hsv = AR.view("hs", BF16, "p (i d) -> p i d", i=4)
                  s_hs = [AR.slot("hs") for _ in range(4)]
              AR.alloc("gt", 4 * 1056 * 2)
              gtv = AR.view("gt", BF16, "p (c t) -> p c t", c=4)
              s_gt = [[AR.slot("gt") for _ in range(3)] for _ in range(4)]

              pa, pb = [], []
              pre_done = prefetch.pop("norm1", None)
              if pre_done is not None:
                  pre_done()
                  for c_ in range(4):
                      build_diag(c_)
              for n_, u in enumerate([] if pre_done is not None else units):
                  pa.append((u, norm_A1(xu(u), s_X[u], urows(u))))
                  if len(pa) > 1:
                      uu, a1 = pa.pop(0)
                      pb.append((uu,) + norm_A2(a1, hsv, s_hs))
                  if len(pb) > 1:
                      uu, hs_, sh_ = pb.pop(0)
                      c0_, nn_ = ucols(uu)
                      norm_B(hs_, sh_, urows(uu), C_GMIX, Av[:, :, c0_:c0_ + nn_], [s_A[f][uu] for f in range(8)])
                  if 2 <= n_ < 6:
                      build_diag(n_ - 2)
              while pa:
                  uu, a1 = pa.pop(0)
                  pb.append((uu,) + norm_A2(a1, hsv, s_hs))
              while pb:
                  uu, hs_, sh_ = pb.pop(0)
                  c0_, nn_ = ucols(uu)
                  norm_B(hs_, sh_, urows(uu), C_GMIX, Av[:, :, c0_:c0_ + nn_], [s_A[f][uu] for f in range(8)])

              dve(lambda e: e.tensor_copy(out=gtv[:, :, 0:30], in_=gthv[:, :, 0:30]), [s_gth],
                  [s_gt[c][0] for c in range(4)])

              AR.alloc("at", 2 * 4 * 528 * 4)
              atv = AR.view("at", F32, "p (i c t) -> p i c t", i=2, c=4)
              s_at = [[AR.slot("at") for _ in range(4)] for _ in range(2)]
              AR.alloc("pt", 2 * 528 * 4)
              ptv = AR.view("pt", F32, "p (i t) -> p i t", i=2)
              s_pt = [AR.slot("pt") for _ in range(2)]
              AR.alloc("sg", 2 * 512 * 4)
              sgv = AR.view("sg", F32, "p (i t) -> p i t", i=2)
              s_sg = [AR.slot("sg") for _ in range(2)]
              if has_s:
                  AR.alloc("as", 4 * NS * 4)
                  asv = AR.view("as", F32, "p (c b) -> p c b", c=4)
                  s_as = [AR.slot("as") for _ in range(4)]
                  AR.alloc("gs", 4 * NS * 4)
                  gsv = AR.view("gs", F32, "p (c b) -> p c b", c=4)
                  s_gs = [AR.slot("gs") for _ in range(4)]
              if last_pass:
                  AR.alloc("gl", 4 * 32 * 4)
                  glv = AR.view("gl", F32, "p (c t) -> p c t", c=4)
                  s_gl = [AR.slot("gl") for _ in range(4)]

              sgi = [0]
              deferred_pool = []
              for t_i, us in tiles:
                  c0, n = tcols(us)
                  is_s = us[0] == 8
                  rd_A = lambda: [s_A[f][u] for f in range(8) for u in us]
                  ai = t_i % 2
                  for j in range(4):
                      bi, bs = bank()
                      pairs = [(w_in_v[:, k, j * 128:(j + 1) * 128], Av[:, k, c0:c0 + n]) for k in range(8)]
                      mm_group(psf(bi)[:, 0:n], pairs, rd_A() + [s_win3[0]], bs)
                      if is_s:
                          act(lambda e, bi=bi, j=j: e.activation(out=asv[:, j, :], in_=psf(bi)[:, 0:NS], func=AF.Copy),
                              [bs], [s_as[j]])
                      else:
                          act(lambda e, bi=bi, j=j, ai=ai: e.activation(out=atv[:, ai, j, 15:15 + 512],
                                                                         in_=psf(bi)[:, 0:512], func=AF.Copy),
                              [bs], [s_at[ai][j]])
                  for j in range(4):
                      bv, bsv = bank()
                      pairs = [(w_in_v[:, k, 512 + j * 128:512 + (j + 1) * 128], Av[:, k, c0:c0 + n]) for k in range(8)]
                      mm_group(psf(bv)[:, 0:n], pairs, rd_A() + [s_win3[1]], bsv)
                      bg, bsg = bank()
                      pairs = [(w_in_v[:, k, 1024 + j * 128:1024 + (j + 1) * 128], Av[:, k, c0:c0 + n]) for k in range(8)]
                      mm_group(psf(bg)[:, 0:n], pairs, rd_A() + [s_win3[2]], bsg)
                      si = sgi[0] % 2
                      sgi[0] += 1
                      act(lambda e, bg=bg, si=si, n=n: e.activation(out=sgv[:, si, 0:n], in_=psf(bg)[:, 0:n],
                                                                   func=AF.Sigmoid), [bsg], [s_sg[si]])
                      if is_s:
                          dve(lambda e, bv=bv, si=si, j=j: e.tensor_tensor(out=gsv[:, j, :], in0=psf(bv)[:, 0:NS],
                                                                           in1=sgv[:, si, 0:NS], op=ALU.mult),
                              [bsv, s_sg[si]], [s_gs[j]])
                      else:
                          g0 = 30 + c0
                          if last_pass and t_i == 1:
                              dve(lambda e, bv=bv, si=si, j=j: e.tensor_tensor(
                                  out=glv[:, j, :], in0=psf(bv)[:, 480:512], in1=sgv[:, si, 480:512], op=ALU.mult),
                                  [bsv, s_sg[si]], [s_gl[j]])
                          dve(lambda e, bv=bv, si=si, j=j, g0=g0: e.tensor_tensor(
                              out=gtv[:, j, g0:g0 + 512], in0=psf(bv)[:, 0:512], in1=sgv[:, si, 0:512], op=ALU.mult),
                              [bsv, s_sg[si]], [s_gt[j][1 + t_i]])
                  if is_s:
                      for fn_ in deferred_pool:
                          fn_()
                      del deferred_pool[:]

                  def pooling(t_i=t_i, us=us, c0=c0, ai=ai):
                      first_global = (ps_i == 0 and t_i == 0)
                      if t_i == 0:
                          dve(lambda e, ai=ai: e.tensor_copy(out=atv[:, ai, :, 0:15], in_=athv[:, :, 0:15]),
                              [s_ath], [s_at[ai][c] for c in range(4)])
                      else:
                          dve(lambda e, ai=ai: e.tensor_copy(out=atv[:, ai, :, 0:15], in_=atv[:, 1 - ai, :, 512:527]),
                              [s_at[1 - ai][c] for c in range(4)], [s_at[ai][c] for c in range(4)])
                      if t_i == 1:
                          dve(lambda e, ai=ai: e.tensor_copy(out=athv[:, :, 0:15], in_=atv[:, ai, :, 512:527]),
                              [s_at[ai][c] for c in range(4)], [s_ath])
                          dve(lambda e: e.tensor_copy(out=gthv[:, :, 0:30], in_=gtv[:, :, 1024:1054]),
                              [s_gt[c][2] for c in range(4)], [s_gth])
                      for g in range(4):
                          w = POOL_W[g]
                          a = atv[:, ai, g, :]
                          src, ssrc = a, s_at[ai][g]
                          sh = 1
                          pi = 0
                          while sh < w:
                              dst = ptv[:, pi, :]
                              lo = 2 * sh - 1
                              dve(lambda e, dst=dst, src=src, lo=lo, sh=sh: e.tensor_tensor(
                                  out=dst[:, lo:527], in0=src[:, lo:527], in1=src[:, lo - sh:527 - sh], op=ALU.add),
                                  [ssrc], [s_pt[pi]])
                              src, ssrc = dst, s_pt[pi]
                              pi = 1 - pi
                              sh *= 2
                          dve(lambda e, src=src, a=a, w=w, g=g, c0=c0: e.scalar_tensor_tensor(
                              out=Bv[:, 4 + g, c0:c0 + 512], in0=src[:, 15:527], scalar=1.0 / w, in1=a[:, 15:527],
                              op0=ALU.mult, op1=ALU.subtract), [ssrc, s_at[ai][g]], [s_B[4 + g][u] for u in us])
                          if first_global:
                              ic = con[:, C_IC + 16 * g:C_IC + 16 * g + 16]
                              tmp = stv[:, 32:48]
                              dve(lambda e, src=src, ic=ic, tmp=tmp: e.tensor_tensor(out=tmp, in0=src[:, 15:31], in1=ic,
                                                                                    op=ALU.mult),
                                  [ssrc, s_con], [s_stat[3]])
                              dve(lambda e, tmp=tmp, a=a, g=g: e.tensor_tensor(out=Bv[:, 4 + g, 0:16], in0=tmp,
                                                                              in1=a[:, 15:31], op=ALU.subtract),
                                  [s_stat[3], s_at[ai][g]], [s_B[4 + g][0]])
                      if last_pass and t_i == 1:
                          bi, bs = bank()
                          for g in range(4):
                              P.op("pe", lambda e, bi=bi, g=g, ai=ai: e.transpose(
                                  out=psf(bi)[0:15, g * 128:(g + 1) * 128], in_=atv[:, ai, g, 512:527], identity=idf),
                                  reads=[s_at[ai][g], s_con], writes=[bs] if g == 0 else [])
                          AR.alloc("ppo", 512 * 4)
                          ppov = AR.view("ppo", F32)
                          s_ppo = AR.slot("ppo")
                          bs.w = {"pe": P.eng["pe"].cnt}
                          act(lambda e, bi=bi: e.activation(out=ppov[0:15, :], in_=psf(bi)[0:15, :], func=AF.Copy),
                              [bs], [s_ppo])
                          out_toks.append(P.dma("sp", lambda e: e.dma_start(out=pp_d, in_=ppov[0:15, :]), reads=[s_ppo]))
                          AR.free("ppo")

                  if not is_s:
                      if has_s and t_i == 1:
                          deferred_pool.append(pooling)
                      else:
                          pooling()
              for fn_ in deferred_pool:
                  fn_()

              AR.free("sg")
              AR.free("pt")
              AR.free("hs")
              AR.free("w_in")

              ckpt(20 * ps_i + 2, "p%d_samp_pool" % ps_i)
              if has_s:
                  AR.alloc("es", 4 * NS * 16 * 4)
                  esv = AR.view("es", F32, "p (c b t) -> p c b t", c=4, b=NS)
                  s_es = [AR.slot("es") for _ in range(4)]
                  for c in range(4):
                      for r in range(2):
                          bi, bs = bank()
                          P.op("pe", lambda e, bi=bi, c=c, r=r: e.transpose(
                              out=psf(bi)[:, 0:120], in_=hstv[0:120, r, c * 128:(c + 1) * 128], identity=idf[0:120, 0:120]),
                              reads=[s_hst[r], s_con], writes=[bs])
                          act(lambda e, bi=bi, c=c, r=r: e.activation(
                              out=esv[:, c, 8 * r:8 * r + 8, 0:15],
                              in_=psf(bi)[:, 0:120].rearrange("p (b t) -> p b t", b=8), func=AF.Copy), [bs], [s_es[c]])
                      dve(lambda e, c=c: e.tensor_copy(out=esv[:, c, :, 15:16], in_=asv[:, c, :].unsqueeze(2)),
                          [s_as[c]], [s_es[c]])
                      w = POOL_W[c]
                      ws = stv[:, 48:64]
                      dve(lambda e, c=c, w=w, ws=ws: e.tensor_reduce(out=ws, in_=esv[:, c, :, 16 - w:16], axis=AX.X,
                                                                    op=ALU.add), [s_es[c]], [s_stat[3]])
                      dve(lambda e, c=c, w=w, ws=ws: e.scalar_tensor_tensor(
                          out=Bv[:, 4 + c, 1024:1040], in0=ws, scalar=1.0 / w, in1=asv[:, c, :], op0=ALU.mult,
                          op1=ALU.subtract), [s_stat[3], s_as[c]], [s_B[4 + c][8]])
                  out_toks.append(P.dma("sp", lambda e: e.dma_start(out=ps_d[:, 0:14, :], in_=sp_d[:, 1:15, :])))
                  bi, bs = bank()
                  for c in range(4):
                      P.op("pe", lambda e, bi=bi, c=c: e.transpose(out=psf(bi)[0:NS, c * 128:(c + 1) * 128],
                                                                  in_=asv[:, c, :], identity=idf),
                           reads=[s_as[c], s_con], writes=[bs] if c == 0 else [])
                  bs.w = {"pe": P.eng["pe"].cnt}
                  AR.alloc("aso", 512 * 4)
                  asov = AR.view("aso", F32)
                  s_aso = AR.slot("aso")
                  act(lambda e, bi=bi: e.activation(out=asov[0:NS, :], in_=psf(bi)[0:NS, :], func=AF.Copy), [bs], [s_aso])
                  out_toks.append(P.dma("sp", lambda e: e.dma_start(
                      out=ps_d[:, 14:15, :].rearrange("b o c -> b (o c)"), in_=asov[0:NS, :]), reads=[s_aso]))
                  AR.free("aso")
                  AR.free("es")
                  AR.free("hst")
              AR.free("at")

              ckpt(20 * ps_i + 3, "p%d_Ib_poolmap" % ps_i)
              w_out_v, s_wout = load_w("w_out", w_out_d, 8, D)

              def poolmap_all():
                  stg = P.stage
                  P.stage = "p%d_Ib_poolmap" % ps_i
                  for t_i, us in tiles:
                      c0, n = tcols(us)
                      for g in range(4):
                          bi, bs = bank()
                          mm_group(psf(bi)[:, 0:n], [(wmapv[:, g, :], Bv[:, 4 + g, c0:c0 + n])],
                                   [s_B[4 + g][u] for u in us] + [s_wmap], bs)
                          dve(lambda e, bi=bi, g=g, c0=c0, n=n: e.tensor_scalar(
                              out=Bv[:, g, c0:c0 + n], in0=psf(bi)[:, 0:n], scalar1=con[:, C_PB + g:C_PB + g + 1],
                              scalar2=con[:, C_PS + g:C_PS + g + 1], op0=ALU.add, op1=ALU.mult),
                              [bs, s_con], [s_B[g][u] for u in us])
                  P.stage = stg

              ckpt(20 * ps_i + 4, "p%d_conv" % ps_i)
              AR.alloc("cy", 4 * 512 * 4)
              cyv = AR.view("cy", F32, "p (c t) -> p c t", c=4)
              s_cy = [[AR.slot("cy") for _ in range(4)] for _ in range(2)]
              AR.alloc("cb", 8 * 512 * 2)
              cbv = AR.view("cb", BF16, "p (c t) -> p c t", c=8)
              s_cb = [[AR.slot("cb") for _ in range(8)] for _ in range(2)]
              AR.alloc("cm", 4 * 512 * 4)
              cmv = AR.view("cm", F32, "p (c t) -> p c t", c=4)
              s_cm = [[AR.slot("cm") for _ in range(4)] for _ in range(2)]
              CB = dict(cy=cyv, cb=cbv, cm=cmv, s_cy=s_cy, s_cb=s_cb, s_cm=s_cm)
              if has_s:
                  AR.alloc("cyS", 4 * NS * 4)
                  AR.alloc("cbS", 8 * NS * 2)
                  AR.alloc("cmS", 4 * NS * 4)
                  CBS = dict(cy=AR.view("cyS", F32, "p (c t) -> p c t", c=4), cb=AR.view("cbS", BF16, "p (c t) -> p c t", c=8),
                             cm=AR.view("cmS", F32, "p (c t) -> p c t", c=4),
                             s_cy=[[AR.slot("cyS") for _ in range(4)]], s_cb=[[AR.slot("cbS") for _ in range(8)]],
                             s_cm=[[AR.slot("cmS") for _ in range(4)]])
              AR.alloc("hs", 4 * D * 2)
              hsv = AR.view("hs", BF16, "p (i d) -> p i d", i=4)
              s_hs = [AR.slot("hs") for _ in range(4)]
              if has_s:
                  AR.alloc("ec", 4 * NS * 32 * 4)
                  ecv = AR.view("ec", F32, "p (c b k) -> p c b k", c=4, b=NS)
                  s_ec = [AR.slot("ec") for _ in range(4)]
                  AR.alloc("ect", NS * 32 * 4)
                  ectv = AR.view("ect", F32, "p (b k) -> p b k", b=NS)
                  s_ect = AR.slot("ect")

              def conv_aux(n, par, B=None):
                  B = B or CB
                  o = par * 256
                  cyv, cbv, cmv, s_cy, s_cb, s_cm = B["cy"], B["cb"], B["cm"], B["s_cy"], B["s_cb"], B["s_cm"]
                  for c in range(4):
                      dve(lambda e, c=c: e.tensor_copy(out=cbv[:, c, o:o + n], in_=cyv[:, c, o:o + n]),
                          [s_cy[par][c]], [s_cb[par][c]])
                      act(lambda e, c=c: e.activation(out=cbv[:, 4 + c, o:o + n], in_=cyv[:, c, o:o + n], func=AF.Square),
                          [s_cy[par][c]], [s_cb[par][4 + c]])

              def ln_silu(us, n, c0, par, B=None):
                  B = B or CB
                  o = par * 256
                  cyv, cbv, cmv, s_cy, s_cb, s_cm = B["cy"], B["cb"], B["cm"], B["s_cy"], B["s_cb"], B["s_cm"]
                  b1, bs1 = bank()
                  mm_group(psf(b1)[:, 0:n], [(ones, cbv[:, c, o:o + n]) for c in range(4)],
                           [s_cb[par][c] for c in range(4)] + [s_ones], bs1)
                  b2, bs2 = bank()
                  mm_group(psf(b2)[:, 0:n], [(ones, cbv[:, 4 + c, o:o + n]) for c in range(4)],
                           [s_cb[par][4 + c] for c in range(4)] + [s_ones], bs2)
                  mean, msq, var, rstd = (cmv[:, q, o:o + n] for q in range(4))
                  scm = s_cm[par]
                  dve(lambda e: e.tensor_scalar(out=mean, in0=psf(b1)[:, 0:n], scalar1=1.0 / 512, scalar2=None,
                                                op0=ALU.mult), [bs1], [scm[0]])
                  dve(lambda e: e.tensor_tensor(out=msq, in0=mean, in1=mean, op=ALU.mult), [scm[0]], [scm[1]])
                  dve(lambda e: e.scalar_tensor_tensor(out=var, in0=psf(b2)[:, 0:n], scalar=1.0 / 512, in1=msq,
                                                       op0=ALU.mult, op1=ALU.subtract), [bs2, scm[1]], [scm[2]])
                  act(lambda e: e.activation(out=var, in_=var, func=AF.Ln, bias=epsc, scale=1.0),
                      [scm[2], s_con], [scm[2]])
                  act(lambda e: e.activation(out=rstd, in_=var, func=AF.Exp, scale=-0.5), [scm[2]], [scm[3]])
                  for c in range(4):
                      y = cyv[:, c, o:o + n]
                      dve(lambda e, y=y: e.tensor_tensor(out=y, in0=y, in1=mean, op=ALU.subtract),
                          [s_cy[par][c], scm[0]], [s_cy[par][c]])
                      dve(lambda e, y=y: e.tensor_tensor(out=y, in0=y, in1=rstd, op=ALU.mult),
                          [s_cy[par][c], scm[3]], [s_cy[par][c]])
                      act(lambda e, y=y, c=c: e.activation(out=Bv[:, 4 + c, c0:c0 + n], in_=y, func=AF.Silu,
                                                           scale=con[:, C_LG + c:C_LG + c + 1],
                                                           bias=con[:, C_LB + c:C_LB + c + 1]),
                          [s_cy[par][c], s_con], [s_B[4 + c][u] for u in us])

              def conv_mm(p):
                  c0 = 256 * p
                  banks = []
                  for c in range(4):
                      bi, bs = bank()
                      pairs = [(diagv[:, c, k, :], gtv[:, c, c0 + k:c0 + k + 256]) for k in range(CONV_K)]
                      rd = [s_diagc[c]] + {0: [s_gt[c][0], s_gt[c][1]], 1: [s_gt[c][1]], 2: [s_gt[c][1], s_gt[c][2]],
                                       3: [s_gt[c][2]]}[p]
                      mm_group(psf(bi)[:, 0:256], pairs, rd, bs)
                      banks.append((bi, bs))
                  return banks

              def conv_evac(banks, par):
                  o = par * 256
                  for c, (bi, bs) in enumerate(banks):
                      cbias = con[:, C_CB + c:C_CB + c + 1]
                      act(lambda e, bi=bi, c=c, cbias=cbias: e.activation(out=cyv[:, c, o:o + 256], in_=psf(bi)[:, 0:256],
                                                                          func=AF.Identity, bias=cbias),
                          [bs, s_con], [s_cy[par][c]])
                  conv_aux(256, par)

              def conv_sample():
                  for c in range(4):
                      for r in range(4):
                          bi, bs = bank()
                          P.op("pe", lambda e, bi=bi, c=c, r=r: e.transpose(
                              out=psf(bi)[:, 0:120], in_=hscv[0:120, r, c * 128:(c + 1) * 128],
                              identity=idf[0:120, 0:120]), reads=[s_hsc[r], s_con], writes=[bs])
                          act(lambda e, bi=bi, c=c, r=r: e.activation(
                              out=ecv[:, c, 4 * r:4 * r + 4, 0:30],
                              in_=psf(bi)[:, 0:120].rearrange("p (b t) -> p b t", b=4), func=AF.Copy), [bs], [s_ec[c]])
                      dve(lambda e, c=c: e.tensor_copy(out=ecv[:, c, :, 30:31], in_=gsv[:, c, :].unsqueeze(2)),
                          [s_gs[c]], [s_ec[c]])

              def conv_sample2(par):
                  o = par * 256
                  for c in range(4):
                      wt = con[:, C_CW + c * CONV_K:C_CW + (c + 1) * CONV_K]
                      dve(lambda e, c=c, wt=wt: e.tensor_tensor(
                          out=ectv[:, :, 0:31], in0=ecv[:, c, :, 0:31],
                          in1=wt.unsqueeze(1).to_broadcast([128, NS, CONV_K]), op=ALU.mult),
                          [s_ec[c], s_con], [s_ect])
                      dve(lambda e, c=c: e.tensor_reduce(out=stv[:, 48:64], in_=ectv[:, :, 0:31], axis=AX.X, op=ALU.add),
                          [s_ect], [s_stat[3]])
                      cbias = con[:, C_CB + c:C_CB + c + 1]
                      act(lambda e, c=c, cbias=cbias: e.activation(out=CBS["cy"][:, c, 0:NS], in_=stv[:, 48:64],
                                                                   func=AF.Identity, bias=cbias),
                          [s_stat[3], s_con], [CBS["s_cy"][0][c]])
                  conv_aux(NS, 0, CBS)

              def conv_sample2b(par):
                  ln_silu(TS[1], NS, 1024, 0, CBS)
                  out_toks.append(P.dma("sp", lambda e: e.dma_start(out=cs_d[:, 0:29, :], in_=sc_d[:, 1:30, :])))
                  bi, bs = bank()
                  for c in range(4):
                      P.op("pe", lambda e, bi=bi, c=c: e.transpose(out=psf(bi)[0:NS, c * 128:(c + 1) * 128],
                                                                  in_=gsv[:, c, :], identity=idf),
                           reads=[s_gs[c], s_con], writes=[bs] if c == 0 else [])
                  bs.w = {"pe": P.eng["pe"].cnt}
                  AR.alloc("gso", 512 * 4)
                  gsov = AR.view("gso", F32)
                  s_gso = AR.slot("gso")
                  act(lambda e, bi=bi: e.activation(out=gsov[0:NS, :], in_=psf(bi)[0:NS, :], func=AF.Copy), [bs], [s_gso])
                  out_toks.append(P.dma("sp", lambda e: e.dma_start(
                      out=cs_d[:, 29:30, :].rearrange("b o c -> b (o c)"), in_=gsov[0:NS, :]), reads=[s_gso]))
                  AR.free("gso")

              pn = ProjNorm(C_GATT, Av, s_A, hsv, s_hs)

              def wout_units(us_):
                  stg = P.stage
                  P.stage = "p%d_wout_norm2" % ps_i
                  for u in us_:
                      lhs, rd = lhs_cols(Bv, s_B, 8, u)
                      pn.unit(u, lhs, rd, w_out_v, s_wout)
                  P.stage = stg

              for p in range(4):
                  bk_ = conv_mm(p)
                  conv_evac(bk_, p % 2)
                  if p == 0:
                      poolmap_all()
                  if p == 2 and has_s:
                      conv_sample()
                  if p >= 1:
                      ln_silu([2 * (p - 1), 2 * (p - 1) + 1], 256, 256 * (p - 1), (p - 1) % 2)
                  if p >= 2:
                      wout_units([2 * (p - 2), 2 * (p - 2) + 1])
              ln_silu([6, 7], 256, 768, 1)
              AR.free("diag")
              AR.free("gt")
              w_q_v, s_wq = load_w("w_q", w_q_d, 8, D)
              wout_units([4, 5])
              wout_units([6, 7])
              AR.free("cm")
              AR.free("cb")
              AR.free("cy")
              if has_s:
                  conv_sample2(0)

              if last_pass:
                  bi, bs = bank()
                  for c in range(4):
                      P.op("pe", lambda e, bi=bi, c=c: e.transpose(out=psf(bi)[0:30, c * 128:(c + 1) * 128],
                                                                  in_=glv[:, c, 2:32], identity=idf),
                           reads=[s_gl[c], s_con], writes=[bs] if c == 0 else [])
                  bs.w = {"pe": P.eng["pe"].cnt}
                  AR.alloc("cpo", 512 * 4)
                  cpov = AR.view("cpo", F32)
                  s_cpo = AR.slot("cpo")
                  act(lambda e, bi=bi: e.activation(out=cpov[0:30, :], in_=psf(bi)[0:30, :], func=AF.Copy), [bs], [s_cpo])
                  out_toks.append(P.dma("sp", lambda e: e.dma_start(out=cp_d, in_=cpov[0:30, :]), reads=[s_cpo]))
                  AR.free("cpo")

              ckpt(20 * ps_i + 5, "p%d_wout_norm2" % ps_i)
              if ps_i == 0:
                  w_k_v, s_wk = load_w("w_k", w_k_d, 8, D)
                  w_v_v, s_wv = load_w("w_v", w_v_d, 8, D)

              kvb = {}

              def kv_prep():
                  stg = P.stage
                  P.stage = "p0_kv"
                  AR.alloc("mem", 2 * D * 4)
                  memv = AR.view("mem", F32, "p (c d) -> p c d", c=2)
                  s_mem = [AR.slot("mem") for _ in range(2)]
                  AR.alloc("mnt", 8 * 256 * 2)
                  mntv = AR.view("mnt", BF16, "p (f m) -> p f m", f=8)
                  s_mnt = [[AR.slot("mnt") for _ in range(2)] for _ in range(8)]
                  P.dma("sp", lambda e: e.dma_start(out=memv, in_=mem_d.rearrange("(c p) d -> p c d", p=128)),
                        writes=s_mem)
                  kvb["hs"] = [norm_A(memv[:, mc, :], s_mem[mc], 128, hsv, s_hs) for mc in range(2)]
                  kvb["mnt"] = (mntv, s_mnt)
                  P.stage = stg

              def kv_prep_b():
                  stg = P.stage
                  P.stage = "p0_kv"
                  mntv, s_mnt = kvb["mnt"]
                  for mc in range(2):
                      hs, s_h = kvb["hs"][mc]
                      norm_B(hs, s_h, 128, C_GMEM, mntv[:, :, mc * 128:(mc + 1) * 128], [s_mnt[f][mc] for f in range(8)])
                  P.stage = stg

              def kv_stage():
                  stg = P.stage
                  P.stage = "p0_kv"
                  mntv, s_mnt = kvb["mnt"]
                  rd_m = [s_mnt[f][mc] for f in range(8) for mc in range(2)]
                  for j in range(8):
                      bi, bs = bank()
                      pairs = [(w_k_v[:, k, j * 128:(j + 1) * 128], mntv[:, k, :]) for k in range(8)]
                      mm_group(psf(bi)[:, 0:256], pairs, rd_m + [s_wk], bs)
                      act(lambda e, bi=bi, j=j: e.activation(out=KTv[:, j, :], in_=psf(bi)[:, 0:256], func=AF.Copy),
                          [bs], [s_KT[j]])
                  AR.alloc("kvo", 4 * 512 * 4)
                  kvov = AR.view("kvo", F32, "p (i n) -> p i n", i=4)
                  s_kvo = [AR.slot("kvo") for _ in range(4)]
                  oi = [0]
                  for (wv_, sw_, od_, isv) in ((w_k_v, s_wk, ko_d, False), (w_v_v, s_wv, vo_d, True)):
                      for mc in range(2):
                          for h in range(2):
                              bi, bs = bank()
                              pairs = [(mntv[:, k, mc * 128:(mc + 1) * 128], wv_[:, k, h * 512:(h + 1) * 512]) for k in range(8)]
                              mm_group(psf(bi), pairs, [s_mnt[f][mc] for f in range(8)] + [sw_], bs)
                              i = oi[0] % 4
                              oi[0] += 1
                              act(lambda e, bi=bi, i=i: e.activation(out=kvov[:, i, :], in_=psf(bi), func=AF.Copy),
                                  [bs], [s_kvo[i]])
                              if isv:
                                  dve(lambda e, i=i, mc=mc, h=h: e.tensor_copy(out=Vv[:, mc, h * 512:(h + 1) * 512],
                                                                                in_=kvov[:, i, :]),
                                      [s_kvo[i]], [s_V[mc]])
                              out_toks.append(P.dma("sp", lambda e, od_=od_, mc=mc, h=h, i=i: e.dma_start(
                                  out=od_[mc * 128:(mc + 1) * 128, h * 512:(h + 1) * 512], in_=kvov[:, i, :]),
                                  reads=[s_kvo[i]]))
                  AR.free("kvo")
                  AR.free("mnt")
                  AR.free("mem")
                  AR.free("w_k")
                  AR.free("w_v")
                  P.stage = stg

              def sample_tail():
                  stg = P.stage
                  P.stage = "p%d_conv" % ps_i
                  conv_sample2b(0)
                  AR.free("ect")
                  AR.free("ec")
                  AR.free("hsc")
                  kv_stage()
                  wout_units(TS[1])
                  pn.flush()
                  AR.free("gs")
                  AR.free("as")
                  AR.free("cyS")
                  AR.free("cbS")
                  AR.free("cmS")
                  P.stage = stg

              def free_conv():
                  if last_pass:
                      AR.free("gl")
                  AR.free("w_out")

              ckpt(20 * ps_i + 6, "p%d_q" % ps_i)
              if has_s:
                  AR.alloc("qs", D * 2)
                  qsv = AR.view("qs", BF16)
                  s_qs = AR.slot("qs")
              ev = [0]
              for t_i, us in tiles:
                  c0, n = tcols(us)
                  if t_i == 1:
                      pn.flush()
                      if has_s:
                          kv_prep()
                  if us[0] == 8:
                      kv_prep_b()
                      sample_tail()
                  for j in range(8):
                      bi, bs = bank()
                      pairs = [(w_q_v[:, k, j * 128:(j + 1) * 128], Av[:, k, c0:c0 + n]) for k in range(8)]
                      mm_group(psf(bi)[:, 0:n], pairs, [s_A[f][u] for f in range(8) for u in us] + [s_wq], bs)
                      if ev[0] % 2 == 0:
                          act(lambda e, bi=bi, j=j, c0=c0, n=n: e.activation(out=Bv[:, j, c0:c0 + n], in_=psf(bi)[:, 0:n],
                                                                             func=AF.Copy), [bs], [s_B[j][u] for u in us])
                      else:
                          dve(lambda e, bi=bi, j=j, c0=c0, n=n: e.tensor_copy(out=Bv[:, j, c0:c0 + n], in_=psf(bi)[:, 0:n]),
                              [bs], [s_B[j][u] for u in us])
                      ev[0] += 1
                  if us[0] == 8:
                      for h in range(2):
                          bi, bs = bank()
                          pairs = [(Av[:, k, c0:c0 + NS], w_q_v[:, k, h * 512:(h + 1) * 512]) for k in range(8)]
                          mm_group(psf(bi)[0:NS, :], pairs, [s_A[f][8] for f in range(8)] + [s_wq], bs)
                          act(lambda e, bi=bi, h=h: e.activation(out=qsv[0:NS, h * 512:(h + 1) * 512], in_=psf(bi)[0:NS, :],
                                                                 func=AF.Copy), [bs], [s_qs])
              AR.free("w_q")
              free_conv()
              w_o_v, s_wo = load_w("w_o", w_o_d, 8, D)


              samp = {"next": 0}
              if has_s:
                  NR = 4
                  AR.alloc("sel", NS * 128 * 2)
                  selv = AR.view("sel", BF16, "p (b j) -> p b j", b=NS)
                  s_sel = AR.slot("sel")
                  P.dma("pool", lambda e: e.dma_start(out=selv[0:NS], in_=sel_d.rearrange("p (b j) -> p b j", b=NS)),
                        writes=[s_sel])
                  AR.alloc("kvr", NR * 2 * 2 * D * 2)
                  kvrv = AR.view("kvr", BF16, "p (i w c d) -> p i w c d", i=NR, w=2, c=2)
                  s_kvr = [[AR.slot("kvr") for _ in range(2)] for _ in range(NR)]
                  AR.alloc("pz", NS * 2 * 4 * NS * 2)
                  pzv = AR.view("pz", BF16, "p (b c h j) -> p b c h j", b=NS, c=2, h=4)
                  s_pz = [AR.slot("pz") for _ in range(NS)]
                  AR.alloc("sj", 8 * 256 * 2)
                  sjv = AR.view("sj", BF16, "p (i d) -> p i d", i=8)
                  s_sj = [AR.slot("sj") for _ in range(8)]
                  AR.alloc("sc8", 3 * 3 * 8 * 4)
                  sc8 = AR.view("sc8", F32, "p (i a k) -> p i a k", i=3, a=3)
                  s_sc8 = [[AR.slot("sc8") for _ in range(9)] for _ in range(3)]
                  AR.alloc("ez", 4 * 8 * 2)
                  ezv = AR.view("ez", BF16, "p (i k) -> p i k", i=4)
                  s_ez = [AR.slot("ez") for _ in range(4)]
                  P.op("pool", lambda e: e.memset(pzv, 0.0), writes=s_pz)
                  obk = [bank(), bank()]
                  bank_reserved.update([obk[0][0], obk[1][0]])

                  def samp_load(b):
                      i = b % NR
                      P.dma("pool", lambda e: e.dma_start(
                          out=kvrv[:, i, 0, :, :], in_=ck_d[b].rearrange("(c p) d -> p c d", p=128)), writes=[s_kvr[i][0]])
                      P.dma("pool", lambda e: e.dma_start(
                          out=kvrv[:, i, 1, :, :], in_=cv_d[b].rearrange("(c p) d -> p c d", p=128)), writes=[s_kvr[i][1]])

                  for b_ in range(NR):
                      samp_load(b_)
                  pipe = []

                  def s1(b):
                      i = b % NR
                      qb = []
                      for h2 in range(2):
                          bi, bs = bank()
                          mm_group(psf(bi), [(selv[0:NS, b, :], qsv[0:NS, h2 * 512:(h2 + 1) * 512])], [s_sel, s_qs], bs)
                          qb.append((bi, bs))
                      si = b % 3
                      for mc in range(2):
                          for h in range(4):
                              bi, bs = qb[h // 2]
                              dve(lambda e, mc=mc, h=h, bi=bi: e.scalar_tensor_tensor(
                                  out=sjv[:, mc * 4 + h, :], in0=kvrv[:, i, 0, mc, h * 256:(h + 1) * 256], scalar=0.0625,
                                  in1=psf(bi)[:, (h % 2) * 256:(h % 2) * 256 + 256],
                                  op0=ALU.mult, op1=ALU.mult, accum_out=sc8[:, si, 0, mc * 4 + h:mc * 4 + h + 1]),
                                  [s_kvr[i][0], bs], [s_sj[mc * 4 + h], s_sc8[si][mc * 4 + h]], fuse=False)
                      act(lambda e: e.activation(out=ezv[:, si, :], in_=sc8[:, si, 0, :], func=AF.Exp),
                          s_sc8[si][0:8], [s_ez[si]])

                  def s2(b):
                      si = b % 3
                      bi, bs = bank()
                      mm_group(psf(bi)[:, 0:4], [(ones, ezv[:, si, mc * 4:mc * 4 + 4]) for mc in range(2)],
                               [s_ez[si], s_ones], bs)
                      dve(lambda e: e.reciprocal(out=sc8[:, si, 1, 0:4], in_=psf(bi)[:, 0:4]), [bs], [s_sc8[si][8]])
                      dve(lambda e: e.tensor_tensor(
                          out=pzv[:, b, :, :, b], in0=ezv[:, si, :].rearrange("p (c h) -> p c h", c=2),
                          in1=sc8[:, si, 1, 0:4].unsqueeze(1).to_broadcast([128, 2, 4]), op=ALU.mult),
                          [s_ez[si], s_sc8[si][8]], [s_pz[b]])

                  def s3(b):
                      i = b % NR
                      for h in range(4):
                          ob, obs = obk[h // 2]

                          def pvfn(e, h=h, ob=ob):
                              ins = first = None
                              for mc in range(2):
                                  ins = e.matmul(out=psf(ob)[0:NS, (h % 2) * 256:(h % 2) * 256 + 256],
                                                 lhsT=pzv[:, b, mc, h, :], rhs=kvrv[:, i, 1, mc, h * 256:(h + 1) * 256],
                                                 start=(b == 0 and mc == 0 and h % 2 == 0), stop=(b == NS - 1 and mc == 1),
                                                 skip_group_check=True)
                                  if first is None:
                                      first = ins
                              return (first, ins)
                          pvfn._fuse = True
                          P.op("pe", pvfn, reads=[s_pz[b], s_kvr[i][1]], writes=[obs] if (b == 0 and h % 2 == 0) else [])
                          if b > 0 or h % 2 == 1:
                              _merge(obs.w, {"pe": P.eng["pe"].cnt})
                      if b + NR < NS:
                          samp_load(b + NR)

                  def samp_token():
                      if samp["next"] >= NS and not pipe:
                          if not samp.get("fin"):
                              samp["fin"] = True
                              samp_finish()
                          return
                      stg = P.stage
                      P.stage = "p0_samp_attn"
                      for t in list(pipe):
                          if t[1] == 2:
                              s3(t[0])
                              pipe.remove(t)
                      for t in pipe:
                          if t[1] == 1:
                              s2(t[0])
                              t[1] = 2
                      if samp["next"] < NS:
                          b = samp["next"]
                          samp["next"] = b + 1
                          s1(b)
                          pipe.append([b, 1])
                      P.stage = stg

                  def samp_finish():
                      stg0 = P.stage
                      P.stage = "p0_samp_fin"
                      AR.alloc("osb", D * 2)
                      osbv = AR.view("osb", BF16)
                      s_osb = AR.slot("osb")
                      for h2 in range(2):
                          ob, obs = obk[h2]
                          act(lambda e, ob=ob, h2=h2: e.activation(out=osbv[0:NS, h2 * 512:(h2 + 1) * 512],
                                                                   in_=psf(ob)[0:NS, :], func=AF.Copy), [obs], [s_osb])
                      bank_reserved.clear()
                      bi, bs = bank()
                      pb = psb(bi).rearrange("p (f t) -> p f t", f=8)

                      def tfn2(e, pb=pb):
                          ins = None
                          for f in range(8):
                              ins = e.transpose(out=pb[:, f, 0:NS], in_=osbv[0:NS, f * 128:(f + 1) * 128],
                                                identity=idb[0:NS, 0:NS])
                          return ins
                      P.op("pe", tfn2, reads=[s_osb, s_idb], writes=[bs])
                      dve(lambda e, pb=pb: e.tensor_copy(out=otsv, in_=pb[:, :, 0:NS]), [bs], [s_ots])
                      for nm_ in ("osb", "ez", "sc8", "sj", "pz", "kvr", "sel", "qs"):
                          AR.free(nm_)
                      P.stage = stg0
              else:
                  def samp_token():
                      return

              blocks = ([(0, 4), (4, 4), (8, 4), (12, 4), (16, 3), (19, 3)] if has_s
                        else [(0, 6), (6, 6), (12, 5), (17, 5)])
              NBK = len(blocks)
              NJM = max(nj_ for _, nj_ in blocks)

              def load_block(bk):
                  j0, nj = blocks[bk]
                  nm = "gu%d" % (bk % 2)
                  AR.alloc(nm, 2 * 8 * nj * 128 * 2)
                  v = AR.view(nm, BF16, "p (w k n) -> p w k n", w=2, k=8)
                  sg_, su_ = AR.slot(nm), AR.slot(nm)
                  P.dma("pool", lambda e: e.dma_start(out=v[:, 0], in_=w_g_d[:, j0 * 128:(j0 + nj) * 128].rearrange(
                      "(k p) n -> p k n", p=128)), writes=[sg_])
                  P.dma("pool", lambda e: e.dma_start(out=v[:, 1], in_=w_u_d[:, j0 * 128:(j0 + nj) * 128].rearrange(
                      "(k p) n -> p k n", p=128)), writes=[su_])
                  nmd = "wd%d" % (bk % 2)
                  AR.alloc(nmd, nj * D * 2)
                  vd = AR.view(nmd, BF16, "p (j n) -> p j n", j=nj)
                  sd_ = AR.slot(nmd)
                  P.dma("pool", lambda e: e.dma_start(out=vd, in_=w_d_d[j0 * 128:(j0 + nj) * 128, :].rearrange(
                      "(j p) n -> p j n", p=128)), writes=[sd_])
                  return (v, sg_, su_, vd, sd_, nm, nmd)


              pre_blk0 = load_block(0)
              ckpt(20 * ps_i + 7, "p%d_attn" % ps_i)
              AR.alloc("et", 2 * 2 * 512 * 2)
              etv = AR.view("et", BF16, "p (i c t) -> p i c t", i=2, c=2)
              s_et = [[AR.slot("et") for _ in range(2)] for _ in range(2)]
              AR.alloc("rs", 2 * 512 * 4)
              rsv = AR.view("rs", F32, "p (i t) -> p i t", i=2)
              s_rs = [AR.slot("rs") for _ in range(2)]
              seq = [(us, h) for _, us in (T0, T1) for h in range(4)]

              def att_scores(idx):
                  us, h = seq[idx]
                  c0, n = tcols(us)
                  i = idx % 2
                  for mc in range(2):
                      bi, bs = bank()
                      pairs = [(KTv[:, 2 * h + dc, mc * 128:(mc + 1) * 128], Bv[:, 2 * h + dc, c0:c0 + n]) for dc in range(2)]
                      mm_group(psf(bi)[:, 0:n], pairs,
                               [s_KT[2 * h], s_KT[2 * h + 1]] + [s_B[2 * h + dc][u] for dc in range(2) for u in us], bs)
                      act(lambda e, bi=bi, i=i, mc=mc, n=n: e.activation(out=etv[:, i, mc, 0:n], in_=psf(bi)[:, 0:n],
                                                                         func=AF.Exp, scale=0.0625), [bs], [s_et[i][mc]])

              def att_pv(idx):
                  us, h = seq[idx]
                  c0, n = tcols(us)
                  i = idx % 2
                  bi, bs = bank()
                  mm_group(psf(bi)[:, 0:n], [(ones, etv[:, i, mc, 0:n]) for mc in range(2)],
                           [s_et[i][0], s_et[i][1], s_ones], bs)
                  act(lambda e, bi=bi: e.activation(out=rsv[:, i, 0:n], in_=psf(bi)[:, 0:n], func=AF.Ln), [bs], [s_rs[i]])
                  act(lambda e: e.activation(out=rsv[:, i, 0:n], in_=rsv[:, i, 0:n], func=AF.Exp, scale=-1.0),
                      [s_rs[i]], [s_rs[i]])
                  for dc in range(2):
                      bi, bs = bank()
                      pairs = [(Vv[:, mc, h * 256 + dc * 128:h * 256 + (dc + 1) * 128], etv[:, i, mc, 0:n]) for mc in range(2)]
                      mm_group(psf(bi)[:, 0:n], pairs, [s_V[0], s_V[1], s_et[i][0], s_et[i][1]], bs)
                      dve(lambda e, bi=bi, dc=dc: e.tensor_tensor(
                          out=Av[:, 2 * h + dc, c0:c0 + n], in0=psf(bi)[:, 0:n], in1=rsv[:, i, 0:n], op=ALU.mult),
                          [bs, s_rs[i]], [s_A[2 * h + dc][u] for u in us])

              att_scores(0)
              for idx in range(len(seq)):
                  if idx + 1 < len(seq):
                      att_scores(idx + 1)
                  att_pv(idx)
              AR.free("rs")
              AR.free("et")

              ckpt(20 * ps_i + 9, "p%d_wo_norm3" % ps_i)
              pn3 = ProjNorm(C_GFFN, Bv, s_B, hsv, s_hs)
              for u in range(8):
                  lhs, rd = lhs_cols(Av, s_A, 8, u)
                  pn3.unit(u, lhs, rd, w_o_v, s_wo)
              if not has_s:
                  pn3.unit(8, [otsv[:, f, :] for f in range(8)], [s_ots], w_o_v, s_wo)
              AR.free("w_o")
              pn3_pending = [not has_s]
              if has_s:
                  pn3.flush()
                  AR.free("hs")

              ckpt(20 * ps_i + 10, "p%d_ffn" % ps_i)
              ftiles = [T0, T1] + ([] if has_s else [TS])
              funits = [u for _, us in ftiles for u in us]
              AR.alloc("inter", NJM * 1040 * 2)
              intv = AR.view("inter", BF16, "p (j t) -> p j t", j=NJM)
              s_int = [[AR.slot("inter") for _ in range(9)] for _ in range(NJM)]
              AR.alloc("fsg", 2 * 512 * 4)
              fsgv = AR.view("fsg", F32, "p (i t) -> p i t", i=2)
              s_fsg = [AR.slot("fsg") for _ in range(2)]
              NYO = 2 if ps_i == 1 else 1
              AR.alloc("yo", NYO * D * 4)
              yov = AR.view("yo", F32, "p (i d) -> p i d", i=NYO)
              s_yo = [AR.slot("yo") for _ in range(NYO)]
              fcnt = [0]
              fi = [0]
              it_cnt = [0]

              class PreNorm:
                  def __init__(self):
                      AR.alloc("hs", 4 * D * 2)
                      self.hsv = AR.view("hs", BF16, "p (i d) -> p i d", i=4)
                      self.s_hs = [AR.slot("hs") for _ in range(4)]
                      prefetch["hs"] = (self.hsv, self.s_hs)
                      prefetch["norm1"] = self.flush
                      self.pa, self.pb = [], []

                  def unit(self, u):
                      self.pa.append((u, norm_A1(xu(u), s_X[u], urows(u))))
                      if len(self.pa) > 1:
                          self._a2()
                      if len(self.pb) > 2:
                          self._b()

                  def _a2(self):
                      uu, a1 = self.pa.pop(0)
                      self.pb.append((uu,) + norm_A2(a1, self.hsv, self.s_hs))

                  def _b(self):
                      uu, hs_, sh_ = self.pb.pop(0)
                      c0_, nn_ = ucols(uu)
                      norm_B(hs_, sh_, urows(uu), C_GMIX, Av[:, :, c0_:c0_ + nn_], [s_A[f][uu] for f in range(8)])

                  def flush(self):
                      self.unit(6)
                      self.unit(7)
                      while self.pa:
                          self._a2()
                      while self.pb:
                          self._b()

              pre = None
              cur = pre_blk0
              for bk in range(NBK):
                  j0, nj = blocks[bk]
                  nxt = load_block(bk + 1) if bk + 1 < NBK else None
                  v, sg_, su_, vd, sd_, nm, nmd = cur
                  for t_i, us in ftiles:
                      c0, n = tcols(us)
                      rdB = [s_B[f][u] for f in range(8) for u in us]
                      for j in range(nj):
                          bg, bsg = bank()
                          mm_group(psf(bg)[:, 0:n], [(v[:, 0, k, j * 128:(j + 1) * 128], Bv[:, k, c0:c0 + n]) for k in range(8)],
                                   rdB + [sg_], bsg)
                          bu, bsu = bank()
                          mm_group(psf(bu)[:, 0:n], [(v[:, 1, k, j * 128:(j + 1) * 128], Bv[:, k, c0:c0 + n]) for k in range(8)],
                                   rdB + [su_], bsu)
                          i = fi[0] % 2
                          fi[0] += 1
                          act(lambda e, bg=bg, i=i, n=n: e.activation(out=fsgv[:, i, 0:n], in_=psf(bg)[:, 0:n], func=AF.Silu),
                              [bsg], [s_fsg[i]])
                          dve(lambda e, bu=bu, i=i, j=j, c0=c0, n=n: e.tensor_tensor(
                              out=intv[:, j, c0:c0 + n], in0=psf(bu)[:, 0:n], in1=fsgv[:, i, 0:n], op=ALU.mult),
                              [bsu, s_fsg[i]], [s_int[j][u] for u in us])
                          it_cnt[0] += 1
                          if it_cnt[0] % 2 == 0:
                              samp_token()
                      if pn3_pending[0]:
                          stg_ = P.stage
                          P.stage = "p%d_wo_norm3" % ps_i
                          pn3.flush()
                          AR.free("hs")
                          P.stage = stg_
                          pn3_pending[0] = False
                  if ps_i == 0 and "w_in" not in prefetch and bk >= 2 and samp.get("fin", not has_s):
                      prefetch["w_in"] = load_w_in()
                  if ps_i == 0 and bk == NBK - 1:
                      pre = PreNorm()
                  pend_f = []

                  def finish_unit(uu, f1):
                      P.stage = "p%d_final" % ps_i
                      final_F2(uu, f1, tok0, yov[:, fcnt[0] % NYO, :], s_yo[fcnt[0] % NYO])
                      fcnt[0] += 1
                      if ps_i == 0 and uu < 8:
                          load_x_unit(1, uu)
                          P.stage = "p1_Ia"
                          if uu >= 2:
                              pre.unit(uu - 2)
                      P.stage = "p%d_ffn" % ps_i

                  for u in funits:
                      lhs, rd = lhs_cols(intv, s_int, nj, u)
                      proj_back(u, lhs, rd, vd, sd_)
                      if bk == NBK - 1:
                          P.stage = "p%d_final" % ps_i
                          pend_f.append((u, final_F1(u)))
                          P.stage = "p%d_ffn" % ps_i
                          if len(pend_f) > 1:
                              finish_unit(*pend_f.pop(0))
                  while pend_f:
                      finish_unit(*pend_f.pop(0))
                  AR.free(nm)
                  AR.free(nmd)
                  cur = nxt
              if has_s:
                  while not samp.get("fin"):
                      samp_token()
              AR.free("yo")
              AR.free("fsg")
              AR.free("inter")

          one_pass(0)
          one_pass(1)

        try:
            body()
        except _Stop:
            pass

        fin = {}
        for t in out_toks:
            _merge(fin, t)
        _merge(fin, P.dma_tot)
        E = P.eng["sp"]
        E.ops.append(([(k, v) for k, v in fin.items()], None, None, 'fin'))

        def run(e, name):
            for waits, fn, inc, stg in P.eng[name].ops:
                fw = None
                single = inc is not None and inc[1] == 1 and name in ("pe", "pool") and getattr(fn, "__name__", "") == "<lambda>"
                if FUSE_WAIT and fn is not None and waits and (getattr(fn, "_fuse", False) or single):
                    fw = waits[-1]
                    waits = waits[:-1]
                for k, v in waits:
                    e.wait_ge(sems[k], v)
                if fn is None:
                    continue
                if TAG:
                    with nc.named_scope(stg):
                        ins = fn(e)
                else:
                    ins = fn(e)
                first, last = ins if isinstance(ins, tuple) else (ins, ins)
                if fw is not None:
                    first._wait_ge(sems[fw[0]], fw[1])
                last.then_inc(sems[inc[0]], inc[1])

        with nc.Block() as block:
            @block.tensor
            def _(e):
                run(e, "pe")

            @block.scalar
            def _(e):
                run(e, "act")

            @block.vector
            def _(e):
                run(e, "dve")

            @block.gpsimd
            def _(e):
                run(e, "pool")

            @block.sync
            def _(e):
                run(e, "sp")
    return nc


_NC_CACHE = {}


def _consts():
    c = np.zeros((128, NCON), np.float32)
    for g, w in enumerate(POOL_W):
        for t in range(16):
            c[:, C_IC + 16 * g + t] = 1.0 / min(w, t + 1)
    c[:, C_EPS] = EPS
    c[:, C_ID:C_ID + 128] = np.eye(128, dtype=np.float32)
    return c


def _prep(x_prompt, x_sample, mem_prompt, state_pool, state_conv, cache_mem_k, cache_mem_v,
          g_mix, w_in, pool_map_w, pool_map_b, pool_scale, conv_dw_w, conv_dw_b, conv_ln_g, conv_ln_b,
          w_out, g_attn, g_mem, w_q, w_k, w_v, w_o, g_ffn, w_gate, w_up, w_down, g_final, cores=None):
    f = lambda a: np.ascontiguousarray(np.asarray(a, dtype=np.float32))
    con = _consts()

    def fm(v, n):
        return f(v).reshape(n, 128).T
    con[:, C_GMIX:C_GMIX + 8] = fm(g_mix[0], 8)
    con[:, C_GATT:C_GATT + 8] = fm(g_attn[0], 8)
    con[:, C_GFFN:C_GFFN + 8] = fm(g_ffn[0], 8)
    con[:, C_GMEM:C_GMEM + 8] = fm(g_mem[0], 8)
    con[:, C_PB:C_PB + 4] = fm(pool_map_b[0], 4)
    con[:, C_PS:C_PS + 4] = fm(pool_scale[0], 4)
    con[:, C_CB:C_CB + 4] = fm(conv_dw_b[0], 4)
    con[:, C_LG:C_LG + 4] = fm(conv_ln_g[0], 4)
    con[:, C_LB:C_LB + 4] = fm(conv_ln_b[0], 4)
    cw = f(conv_dw_w[0])
    for c in range(4):
        con[:, C_CW + c * CONV_K:C_CW + (c + 1) * CONV_K] = cw[:, c * 128:(c + 1) * 128].T
    gfin = np.ascontiguousarray(np.broadcast_to(f(g_final)[None, :], (128, D)))
    sel = np.zeros((NS, NS, 128), np.float32)
    for b in range(NS):
        sel[b, b, :] = 1.0
    sel = sel.reshape(NS, NS * 128)

    shared = {
        "w_in": f(w_in[0]), "wmap": f(pool_map_w[0]), "w_out": f(w_out[0]), "w_q": f(w_q[0]), "w_k": f(w_k[0]),
        "w_v": f(w_v[0]), "w_o": f(w_o[0]), "w_gate": f(w_gate[0]), "w_up": f(w_up[0]), "w_down": f(w_down[0]),
        "consts": con, "gfin": gfin, "sel": sel,
    }
    in_maps = []
    for c in (range(NCORES) if cores is None else cores):
        sl = slice(NS * c, NS * (c + 1))
        m = dict(shared)
        m["x"] = f(x_prompt[c])
        m["xs"] = f(x_sample[sl, 0])
        m["mem"] = f(mem_prompt[c])
        m["sp"] = f(state_pool[0, sl])
        m["sc"] = f(state_conv[0, sl])
        m["ck"] = f(cache_mem_k[0, sl]).reshape(NS, NMEM, D)
        m["cv"] = f(cache_mem_v[0, sl]).reshape(NS, NMEM, D)
        in_maps.append(m)
    return in_maps


def kernel(**inputs):
    if "nc" not in _NC_CACHE:
        _NC_CACHE["nc"] = build_program()
    nc = _NC_CACHE["nc"]
    in_maps = _prep(**inputs)
    res = run_bass_kernel_spmd(nc, in_maps, core_ids=list(range(NCORES)))
    r = res.results
    y_prompt = np.stack([r[c]["y"] for c in range(NCORES)], 0)
    y_sample = np.concatenate([r[c]["ys"] for c in range(NCORES)], 0)[:, None, :]
    pool_p = np.stack([r[c]["pool_p"] for c in range(NCORES)], 0)[None]
    pool_s = np.concatenate([r[c]["pool_s"] for c in range(NCORES)], 0)[None]
    conv_p = np.stack([r[c]["conv_p"] for c in range(NCORES)], 0)[None]
    conv_s = np.concatenate([r[c]["conv_s"] for c in range(NCORES)], 0)[None]
    k_p = np.stack([r[c]["k_out"] for c in range(NCORES)], 0).reshape(1, NCORES, NMEM, 4, 256)
    v_p = np.stack([r[c]["v_out"] for c in range(NCORES)], 0).reshape(1, NCORES, NMEM, 4, 256)
    return (y_prompt.astype(np.float32), y_sample.astype(np.float32), pool_p.astype(np.float32),
            pool_s.astype(np.float32), conv_p.astype(np.float32), conv_s.astype(np.float32),
            k_p.astype(np.float32), v_p.astype(np.float32))
```
